# Optimizing a Trainium2 kernel written in Bass

```python
import jax, jax.numpy as jnp
from jax import lax
import numpy as np

D_MODEL = 2048
BATCH = 8
SEQ = 2048
DEPTH = 4
DEC_BATCH = 4
DEC_SEQ = 4096
PAST_LEN = 128

HEAD_DIM = 128
BLOCK = 128
WINDOW = 128
ATTN_WIDTH = D_MODEL * 3 // 4
N_HEADS = ATTN_WIDTH // HEAD_DIM
N_KV_HEADS = N_HEADS // 3
GROUP = N_HEADS // N_KV_HEADS
FOURIER_WIDTH = D_MODEL - ATTN_WIDTH
FOURIER_DIM = 128
N_FOURIER = FOURIER_WIDTH // FOURIER_DIM
CONV_WIDTH = D_MODEL // 2
CONV_K = 3
SG_WIDTH = D_MODEL - CONV_WIDTH
SG_DIM = 128
N_SG = SG_WIDTH // SG_DIM
CHUNK = 128
N_BUCKETS = 32
MAX_DISTANCE = 128
EPS = 1e-6
N_EVEN = (DEPTH + 1) // 2
N_ODD = DEPTH // 2
EVEN_SPLITS = [ATTN_WIDTH, N_KV_HEADS * HEAD_DIM, N_KV_HEADS * HEAD_DIM, FOURIER_WIDTH, ATTN_WIDTH, FOURIER_WIDTH]
ODD_SPLITS = [CONV_WIDTH, CONV_WIDTH, CONV_WIDTH, CONV_WIDTH, SG_WIDTH, SG_WIDTH, SG_WIDTH]
EVEN_IN = sum(EVEN_SPLITS)
ODD_IN = sum(ODD_SPLITS)

kernel_name = "hybrid_bidir_window_fnet_conv_gmlp"


def _rmsnorm(x, g):
    xf = x.astype(jnp.float32)
    r = lax.rsqrt(jnp.mean(xf * xf, axis=-1, keepdims=True) + EPS)
    return (xf * r).astype(x.dtype) * g


def _split(z, sizes):
    idx = [int(i) for i in np.cumsum(sizes)[:-1]]
    return jnp.split(z, idx, axis=-1)


def _t5_bucket(rel):
    nb = N_BUCKETS // 2
    ret = (rel > 0).astype(np.int32) * nb
    n = np.abs(rel)
    max_exact = nb // 2
    large = max_exact + (np.log(np.maximum(n, 1) / max_exact) / np.log(MAX_DISTANCE / max_exact)
                         * (nb - max_exact)).astype(np.int32)
    large = np.minimum(large, nb - 1)
    return (ret + np.where(n < max_exact, n, large)).astype(np.int32)


def _window_attention(q, k, v, rel_bias, sink):
    B, S = q.shape[0], q.shape[1]
    nb = S // BLOCK
    qb = q.reshape(B, nb, BLOCK, N_KV_HEADS, GROUP, HEAD_DIM)
    pad = ((0, 0), (BLOCK, BLOCK), (0, 0), (0, 0))

    def bands(t):
        tb = jnp.pad(t, pad).reshape(B, nb + 2, BLOCK, N_KV_HEADS, HEAD_DIM)
        return jnp.concatenate([tb[:, :-2], tb[:, 1:-1], tb[:, 2:]], axis=2)

    kb, vb = bands(k), bands(v)
    r = np.arange(BLOCK)[:, None]
    c = np.arange(3 * BLOCK)[None, :]
    rel = c - BLOCK - r
    pos_k = np.arange(nb)[:, None, None] * BLOCK + c[None] - BLOCK
    mask = jnp.asarray((np.abs(rel) <= WINDOW)[None] & (pos_k >= 0) & (pos_k < S))
    bias = jnp.take(rel_bias, jnp.asarray(_t5_bucket(rel)), axis=0)
    bias = jnp.transpose(bias, (2, 0, 1)).reshape(N_KV_HEADS, GROUP, BLOCK, 3 * BLOCK).astype(jnp.float32)

    s = jnp.einsum('bnqhgd,bnkhd->bnhgqk', qb, kb, preferred_element_type=jnp.float32) * (HEAD_DIM ** -0.5)
    s = jnp.where(mask[None, :, None, None], s + bias[None, None], -1e30)
    sink_l = sink.astype(jnp.float32).reshape(N_KV_HEADS, GROUP)[None, None, :, :, None, None]
    m = jnp.maximum(jnp.max(s, axis=-1, keepdims=True), sink_l)
    p = jnp.exp(s - m)
    denom = jnp.sum(p, axis=-1, keepdims=True) + jnp.exp(sink_l - m)
    o = jnp.einsum('bnhgqk,bnkhd->bnqhgd', (p / denom).astype(vb.dtype), vb)
    return o.reshape(B, S, N_HEADS * HEAD_DIM)


def _fourier(f, w_f, b_f):
    ff = jnp.fft.fft2(f.astype(jnp.float32), axes=(1, 3), norm='ortho').real.astype(f.dtype)
    return jnp.einsum('bsgc,gcd->bsgd', ff, w_f) + b_f


def _short_conv(h, w):
    hp = jnp.pad(h, ((0, 0), (1, 1), (0, 0)))
    return hp[:, :-2] * w[0] + hp[:, 1:-1] * w[1] + hp[:, 2:] * w[2]


def _spatial_gate(u, v, v_gain, w_s, b_s):
    B, S = u.shape[0], u.shape[1]
    nc = S // CHUNK
    vn = _rmsnorm(v.reshape(B, S, N_SG, SG_DIM), v_gain.reshape(N_SG, SG_DIM))
    vc = vn.reshape(B, nc, CHUNK, N_SG, SG_DIM)
    sv = jnp.einsum('gpq,bnqgc->bnpgc', w_s, vc) + jnp.transpose(b_s)[None, None, :, :, None]
    return u * sv.reshape(B, S, SG_WIDTH)


def _even_layer(x, g, w_in, w_out, q_gain, k_gain, sink, w_f, b_f, rel_bias):
    B, S = x.shape[0], x.shape[1]
    z = _rmsnorm(x, g) @ w_in
    q, k, v, f, ga, gf = _split(z, EVEN_SPLITS)
    q = _rmsnorm(q.reshape(B, S, N_HEADS, HEAD_DIM), q_gain)
    k = _rmsnorm(k.reshape(B, S, N_KV_HEADS, HEAD_DIM), k_gain)
    v = v.reshape(B, S, N_KV_HEADS, HEAD_DIM)
    a = _window_attention(q, k, v, rel_bias, sink) * jax.nn.silu(ga)
    fo = _fourier(f.reshape(B, S, N_FOURIER, FOURIER_DIM), w_f, b_f).reshape(B, S, FOURIER_WIDTH) * jax.nn.silu(gf)
    return x + jnp.concatenate([a, fo], axis=-1) @ w_out


def _odd_layer(x, g, w_in, conv_w, v_gain, w_s, b_s, w_out):
    z = _rmsnorm(x, g) @ w_in
    h, bg, cg, gc, u, v, gd = _split(z, ODD_SPLITS)
    co = (bg * _short_conv(cg * h, conv_w)) * jax.nn.silu(gc)
    so = _spatial_gate(u, v, v_gain, w_s, b_s) * jax.nn.silu(gd)
    return x + jnp.concatenate([co, so], axis=-1) @ w_out


def _trunk(x, norm_gain, rel_bias, w_in_e, w_out_e, q_gain, k_gain, sink, w_f, b_f,
           w_in_o, conv_w, v_gain, w_s, b_s, w_out_o):
    for l in range(DEPTH):
        i = l // 2
        if l % 2 == 0:
            x = _even_layer(x, norm_gain[l], w_in_e[i], w_out_e[i], q_gain[i], k_gain[i], sink[i],
                            w_f[i], b_f[i], rel_bias)
        else:
            x = _odd_layer(x, norm_gain[l], w_in_o[i], conv_w[i], v_gain[i], w_s[i], b_s[i], w_out_o[i])
    return x


def setup_inputs(seed: int = 0) -> dict:
    key = jax.random.key(seed)
    ks = jax.random.split(key, 20)
    nrm = lambda k, shape, s: jax.random.normal(k, shape, jnp.float32) * s
    return {
        "x_prompt": nrm(ks[0], (BATCH, SEQ, D_MODEL), 1.0),
        "x_sample": nrm(ks[1], (DEC_BATCH, DEC_SEQ, D_MODEL), 1.0),
        "norm_gain": 1.0 + nrm(ks[2], (DEPTH, D_MODEL), 0.02),
        "rel_bias": nrm(ks[3], (N_BUCKETS, N_HEADS), 0.1),
        "w_in_e": nrm(ks[4], (N_EVEN, D_MODEL, EVEN_IN), D_MODEL ** -0.5),
        "w_out_e": nrm(ks[5], (N_EVEN, ATTN_WIDTH + FOURIER_WIDTH, D_MODEL), (ATTN_WIDTH + FOURIER_WIDTH) ** -0.5),
        "q_gain": 1.0 + nrm(ks[6], (N_EVEN, HEAD_DIM), 0.02),
        "k_gain": 1.0 + nrm(ks[7], (N_EVEN, HEAD_DIM), 0.02),
        "sink": nrm(ks[8], (N_EVEN, N_HEADS), 0.1),
        "w_f": nrm(ks[9], (N_EVEN, N_FOURIER, FOURIER_DIM, FOURIER_DIM), FOURIER_DIM ** -0.5),
        "b_f": nrm(ks[10], (N_EVEN, N_FOURIER, FOURIER_DIM), 0.02),
        "w_in_o": nrm(ks[11], (N_ODD, D_MODEL, ODD_IN), D_MODEL ** -0.5),
        "conv_w": nrm(ks[12], (N_ODD, CONV_K, CONV_WIDTH), CONV_K ** -0.5),
        "v_gain": 1.0 + nrm(ks[13], (N_ODD, SG_WIDTH), 0.02),
        "w_s": nrm(ks[14], (N_ODD, N_SG, CHUNK, CHUNK), CHUNK ** -0.5),
        "b_s": nrm(ks[15], (N_ODD, N_SG, CHUNK), 0.02),
        "w_out_o": nrm(ks[16], (N_ODD, CONV_WIDTH + SG_WIDTH, D_MODEL), (CONV_WIDTH + SG_WIDTH) ** -0.5),
    }


def reference(x_prompt, x_sample, norm_gain, rel_bias, w_in_e, w_out_e, q_gain, k_gain, sink, w_f, b_f,
              w_in_o, conv_w, v_gain, w_s, b_s, w_out_o):
    y_prompt = _trunk(x_prompt, norm_gain, rel_bias, w_in_e, w_out_e, q_gain, k_gain, sink, w_f, b_f,
                      w_in_o, conv_w, v_gain, w_s, b_s, w_out_o)
    y_sample = _trunk(x_sample, norm_gain, rel_bias, w_in_e, w_out_e, q_gain, k_gain, sink, w_f, b_f,
                      w_in_o, conv_w, v_gain, w_s, b_s, w_out_o)
    return (y_prompt, y_sample)
```

```python
import contextlib
import numpy as np
import ml_dtypes
import concourse.bass as bass
import concourse.mybir as mybir
from concourse.bass_utils import run_bass_kernel_spmd

F32 = mybir.dt.float32
BF16 = mybir.dt.bfloat16
AF = mybir.ActivationFunctionType
ALU = mybir.AluOpType
AX = mybir.AxisListType

D = 2048
KC = 16
P = 128
TT = 512
WG = 256
E_IN = 5120
O_IN = 7168
NEG = -30000.0
SHIFT = 4.0
EPS = 1e-6
SAME_SYNC = True
NDMA_SEM = 8
SBUF_LO = 16384 + 512
SBUF_BYTES = 224 * 1024 - 512


class Sched:
    ENGS = ("pe", "act", "dve", "pool", "sp")

    def __init__(self, nc, stack):
        self.nc = nc
        self.sems = {}
        self.prog = {e: [] for e in self.ENGS}
        self.count = {}
        self.seen = {e: {} for e in self.ENGS}
        self.res = {}
        for e in ("pe", "act", "dve", "pool"):
            self.sems["c_" + e] = stack.enter_context(nc.semaphore("c_" + e))
            self.count["c_" + e] = 0
        self.dq = {}
        for q in ("sp", "pool"):
            keys = []
            for i in range(NDMA_SEM):
                k = "d_%s%d" % (q, i)
                self.sems[k] = stack.enter_context(nc.semaphore(k))
                self.count[k] = 0
                keys.append(k)
            self.dq[q] = [keys, 0]

    def _deps(self, reads, writes):
        deps = {}

        def add(tok):
            if tok is None:
                return
            k, v = tok
            if deps.get(k, 0) < v:
                deps[k] = v
        for r in reads:
            st = self.res.get(r)
            if st:
                add(st[0])
        for w in writes:
            st = self.res.get(w)
            if st:
                add(st[0])
                for k, v in st[1].items():
                    add((k, v))
        return deps

    def _commit(self, eng, deps, fn, tok, reads, writes, amount):
        waits = []
        own = "c_" + eng
        for k, v in deps.items():
            if k == own and (eng == "pe" or not SAME_SYNC):
                continue
            if self.seen[eng].get(k, 0) >= v:
                continue
            self.seen[eng][k] = v
            waits.append((k, v))
        self.prog[eng].append((waits, fn, (tok[0], amount)))
        for r in reads:
            st = self.res.setdefault(r, [None, {}])
            if st[1].get(tok[0], 0) < tok[1]:
                st[1][tok[0]] = tok[1]
        for w in writes:
            self.res[w] = [tok, {}]

    def op(self, eng, fn, reads=(), writes=()):
        ps_r = [r for r in reads if isinstance(r, tuple) and r[0] == "ps"]
        if ps_r:
            reads = [r for r in reads if not (isinstance(r, tuple) and r[0] == "ps")]
            writes = list(writes) + ps_r
        deps = self._deps(reads, writes)
        k = "c_" + eng
        self.count[k] += 1
        tok = (k, self.count[k])
        self._commit(eng, deps, fn, tok, reads, writes, 1)
        return tok

    def dma(self, q, fn, reads=(), writes=()):
        keys, rr = self.dq[q]
        k = keys[rr]
        self.dq[q][1] = (rr + 1) % len(keys)
        deps = self._deps(reads, writes)
        if self.count[k] > 0:
            v = 16 * self.count[k]
            if deps.get(k, 0) < v:
                deps[k] = v
        self.count[k] += 1
        tok = (k, 16 * self.count[k])
        self._commit(q, deps, fn, tok, reads, writes, 16)
        return tok

    def barrier(self):
        finals = {}
        for k, c in self.count.items():
            v = c * (16 if k.startswith("d_") else 1)
            if v > 0:
                finals[k] = v
        for e in self.ENGS:
            waits = []
            for k, v in finals.items():
                if k == "c_" + e and e == "pe":
                    continue
                if self.seen[e].get(k, 0) >= v:
                    continue
                self.seen[e][k] = v
                waits.append((k, v))
            if waits:
                self.prog[e].append((waits, None, None))
        self.res = {}

    def replay(self, name, e):
        sems = self.sems
        for waits, fn, inc in self.prog[name]:
            for k, v in waits:
                e.wait_ge(sems[k], v)
            if fn is not None:
                ins = fn(e)
                ins.then_inc(sems[inc[0]], inc[1])


class Arena:
    def __init__(self, nc):
        self.nc = nc
        self.base = SBUF_LO
        self.off = SBUF_LO
        self.n = 0

    def alloc(self, shape, dtype):
        sz = 1
        for s in shape[1:]:
            sz *= s
        sz *= 2 if dtype == BF16 else 4
        sz = (sz + 63) // 64 * 64
        off = self.off
        self.off += sz
        assert self.off <= SBUF_BYTES, "SBUF arena overflow %d" % self.off
        self.n += 1
        self.last_off = off
        return self.nc.alloc_sbuf_tensor_at("sb%d" % self.n, list(shape), dtype, offset=off)

    def alias(self, shape, dtype, off):
        self.n += 1
        return self.nc.alloc_sbuf_tensor_at("sb%d" % self.n, list(shape), dtype, offset=off)

    def mark(self):
        self.base = self.off

    def reset(self):
        self.off = self.base


def build(T, NL=4, STOP=None):
    NB = T // P
    NT = T // TT
    NG4 = NB // 4
    HB = NB // 2
    nc = bass.Bass("TRN2", target_bir_lowering=False)

    def din(name, shape, dt=F32):
        return nc.dram_tensor(name, list(shape), dt, kind="ExternalInput").ap()

    def dscr(name, shape, dt=BF16):
        return nc.dram_tensor(name, list(shape), dt, kind="Internal").ap()

    x_in = din("x", [T, D])
    y_out = nc.dram_tensor("y", [T, D], F32, kind="ExternalOutput").ap()
    w_in_e = din("w_in_e", [2, D, E_IN])
    w_out_e = din("w_out_e", [2, D, D])
    w_in_o = din("w_in_o", [2, D, O_IN])
    w_out_o = din("w_out_o", [2, D, D])
    c_ident = din("c_ident", [P, P], BF16)
    c_cs = din("c_cs", [P, 2 * P], BF16)
    c_oh = din("c_oh", [34, 3 * P * P], BF16)
    c_rbx = din("c_rbx", [34, 12])
    c_mdft = din("c_mdft", [NT, NG4, P, 4 * 2 * TT], BF16)
    c_bm = din("c_bm", [P, 2])
    p_gcol = din("p_gcol", [P, 4 * KC])
    p_qkg = din("p_qkg", [P, 4])
    p_sink = din("p_sink", [P, 24])
    p_bf = din("p_bf", [P, 8])
    p_wf = din("p_wf", [P, 8 * P])
    p_convw = din("p_convw", [P, 48])
    p_vgain = din("p_vgain", [P, 16])
    p_wsT = din("p_wsT", [P, 16 * P])
    p_bs = din("p_bs", [P, 16 * P])

    NGE, NGO, NGW = E_IN // WG, O_IN // WG, D // 512
    wie = [dscr("wie%d" % i, [NGE, P, KC * WG]) for i in range(2)]
    wio = [dscr("wio%d" % i, [NGO, P, KC * WG]) for i in range(2)]
    woe = [dscr("woe%d" % i, [NGW, P, KC * 512]) for i in range(2)]
    woo = [dscr("woo%d" % i, [NGW, P, KC * 512]) for i in range(2)]
    xa = dscr("xa", [T, D], F32)
    xb = dscr("xb", [T, D], F32)
    qk_s = dscr("qk_s", [NT, P, 16 * TT])
    sg_s = dscr("sg_s", [NT, P, 16 * TT])
    v_s = dscr("v_s", [T, 512])
    P_s = dscr("P_s", [T, 1024])
    fo_s = dscr("fo_s", [NT, P, 4 * TT])
    so_s = dscr("so_s", [NT, P, 24 * TT])
    bias_s = dscr("bias_s", [P, 3 * 12 * P])

    stack = contextlib.ExitStack()
    S = Sched(nc, stack)
    A = Arena(nc)
    psb = [nc.alloc_psum_tensor("psb%d" % i, [P, 512], F32) for i in range(8)]
    ps_rr = [0]

    def ps_next():
        i = ps_rr[0]
        ps_rr[0] = (i + 1) % 8
        return i

    ident = A.alloc([P, P], BF16)
    ones_s = A.alloc([P, P], BF16)
    ones1 = A.alloc([P, P], BF16)
    ones_f = A.alloc([P, P], F32)
    epsc = A.alloc([P, 1], F32)
    onec = A.alloc([P, 1], F32)
    gcol = A.alloc([P, 4 * KC], F32)
    qkg = A.alloc([P, 4], F32)
    sinkb = A.alloc([P, 24], F32)
    bfc = A.alloc([P, 8], F32)
    convw = A.alloc([P, 48], F32)
    vgain = A.alloc([P, 16], F32)
    bmc = A.alloc([P, 2], F32)
    mhalo = A.alloc([P, NT * 2 * 8], BF16)
    ss = A.alloc([P, 8], F32)
    bg_f = A.alloc([P, KC, 256], F32)
    bg_f_off = A.last_off
    bg_b = [A.alloc([P, KC, 256], BF16) for _ in range(2)]
    A.mark()

    def ld(dst, src, key, reads=()):
        S.dma("sp", lambda e, d=dst, s=src: e.dma_start(out=d, in_=s), reads=list(reads), writes=[key])

    ld(ident[:], c_ident, "ident")
    ld(gcol[:], p_gcol, "gcol")
    ld(qkg[:], p_qkg, "qkg")
    ld(sinkb[:], p_sink, "sinkb")
    ld(bfc[:], p_bf, "bfc")
    ld(convw[:], p_convw, "convw")
    ld(vgain[:], p_vgain, "vgain")
    ld(bmc[:], c_bm, "bmc")
    S.op("dve", lambda e: e.memset(ones_s[:], 1.0 / 128), writes=["ones_s"])
    S.op("dve", lambda e: e.memset(ones1[:], 1.0), writes=["ones1"])
    S.op("dve", lambda e: e.memset(ones_f[:], 1.0), writes=["ones_f"])
    S.op("dve", lambda e: e.memset(epsc[:], EPS), writes=["epsc"])
    S.op("dve", lambda e: e.memset(onec[:], 1.0), writes=["onec"])
    S.op("dve", lambda e: e.tensor_scalar(out=qkg[:, 0:2], in0=qkg[:, 0:2], scalar1=float(P) ** -0.5,
                                          scalar2=None, op0=ALU.mult), reads=["qkg"], writes=["qkg"])
    S.op("dve", lambda e: e.tensor_scalar(out=sinkb[:], in0=sinkb[:], scalar1=-SHIFT, scalar2=None,
                                          op0=ALU.add), reads=["sinkb"], writes=["sinkb"])
    S.op("act", lambda e: e.activation(out=sinkb[:], in_=sinkb[:], func=AF.Exp), reads=["sinkb"],
         writes=["sinkb"])

    n_even = (NL + 1) // 2
    n_odd = NL // 2
    tasks = []

    def add_tasks(src, dst, ncols, gw):
        for g in range(ncols // 256):
            s_ap = src[:, g * 256:(g + 1) * 256].rearrange("(kc p) o -> p kc o", p=P)
            dg, dh = (g * 256) // gw, ((g * 256) % gw) // 256
            d_ap = dst[dg].rearrange("p (kc o) -> p kc o", o=gw)[:, :, dh * 256:(dh + 1) * 256]
            tasks.append((s_ap, d_ap))

    need = {}
    for l_ in range(NL):
        i_ = l_ // 2
        if l_ % 2 == 0:
            add_tasks(w_in_e[i_], wie[i_], E_IN, WG)
            need[("in", l_)] = len(tasks)
            add_tasks(w_out_e[i_], woe[i_], D, 512)
            need[("out", l_)] = len(tasks)
        else:
            add_tasks(w_in_o[i_], wio[i_], O_IN, WG)
            need[("in", l_)] = len(tasks)
            add_tasks(w_out_o[i_], woo[i_], D, 512)
            need[("out", l_)] = len(tasks)
    tdone = [0]

    def do_cast(task, fb, fk, bb, bk, eng):
        s_ap, d_ap = task
        S.dma("sp", lambda e, fb=fb, s=s_ap: e.dma_start(out=fb[:], in_=s), writes=[fk])
        if eng == "act":
            S.op("act", lambda e, fb=fb, bb=bb: e.activation(out=bb[:], in_=fb[:], func=AF.Copy), reads=[fk], writes=[bk])
        else:
            S.op("dve", lambda e, fb=fb, bb=bb: e.tensor_copy(out=bb[:], in_=fb[:]), reads=[fk], writes=[bk])
        S.dma("pool", lambda e, bb=bb, d=d_ap: e.dma_start(out=d, in_=bb[:]), reads=[bk], writes=[("wtask", tasks.index(task))])

    def bg(k):
        for _ in range(k):
            if tdone[0] >= len(tasks):
                return
            j = tdone[0]
            tdone[0] += 1
            do_cast(tasks[j], bg_f, "bg_f", bg_b[j % 2], ("bg_b", j % 2), "act")

    def bg_need(key):
        while tdone[0] < need[key]:
            bg(1)

    BG_PER_TILE = -(-24 // NT)
    bgl = [0]

    def bg_tile():
        bgl[0] = BG_PER_TILE

    def bg1():
        if bgl[0] > 0:
            bgl[0] -= 1
            bg(1)

    def bg_rest():
        while bgl[0] > 0:
            bg1()
    def build_bias_table():
        rbx_f = A.alloc([34, 12], F32)
        rbx_b = A.alloc([34, 12], BF16)
        ohb = [A.alloc([34, 4096], BF16) for _ in range(2)]
        btabs = [A.alloc([12, 4096], BF16) for _ in range(2)]
        ld(rbx_f[:], c_rbx, "rbx_f")
        S.op("dve", lambda e: e.tensor_copy(out=rbx_b[:], in_=rbx_f[:]), reads=["rbx_f"], writes=["rbx_b"])
        for c in range(12):
            ob = ohb[c % 2]
            ld(ob[:], c_oh[:, c * 4096:(c + 1) * 4096], ("ohb", c % 2))
            for j in range(8):
                b = ps_next()
                S.op("pe", lambda e, b=b, ob=ob, j=j: e.matmul(psb[b][0:12, :], lhsT=rbx_b[:], rhs=ob[:, j * 512:(j + 1) * 512],
                                                             start=True, stop=True),
                     reads=["rbx_b", ("ohb", c % 2)], writes=[("ps", b)])
                col = j * 512
                btab = btabs[c % 2]
                bkey = ("btab", c % 2)
                if j % 2:
                    S.op("act", lambda e, b=b, col=col, btab=btab: e.activation(out=btab[:, col:col + 512], in_=psb[b][0:12, :], func=AF.Copy),
                         reads=[("ps", b)], writes=[bkey])
                else:
                    S.op("dve", lambda e, b=b, col=col, btab=btab: e.tensor_copy(out=btab[:, col:col + 512], in_=psb[b][0:12, :]),
                         reads=[("ps", b)], writes=[bkey])
            kc_, k0 = c // 4, 32 * (c % 4)
            d_ap = bias_s.rearrange("k (kc h q) -> kc h k q", kc=3, h=12)[kc_, :, k0:k0 + 32, :]
            S.dma("pool", lambda e, d_ap=d_ap, btab=btabs[c % 2]: e.dma_start(out=d_ap, in_=btab[:].rearrange("h (k q) -> h k q", q=P)),
                  reads=[("btab", c % 2)], writes=[("bias_s", c)])

    S.barrier()
    A.reset()
    if STOP == "prologue":
        tmpx = A.alloc([P, D], F32)
        for gb in range(NB):
            S.dma("sp", lambda e, gb=gb: e.dma_start(out=tmpx[:], in_=x_in[gb * P:(gb + 1) * P, :]), writes=["tmpx"])
            S.dma("pool", lambda e, gb=gb: e.dma_start(out=y_out[gb * P:(gb + 1) * P, :], in_=tmpx[:]), reads=["tmpx"], writes=[("y", gb)])
        NL = 0

    def make_norm(l, x_src, xin, xn, xnT):
        def norm(t, blks=(0, 1, 2, 3)):
            for blk in blks:
                gb = t * 4 + blk
                xi = xin[gb % 2]
                kx = ("xin", gb % 2)
                ld(xi[:], x_src[gb * P:(gb + 1) * P, :], kx)
                c = gb % 4
                S.op("act", lambda e, xi=xi, blk=blk, c=c: e.activation(out=xn[blk][:], in_=xi[:], func=AF.Square,
                                                                        accum_out=ss[:, c:c + 1]),
                     reads=[kx], writes=[("xn", blk), ("ss", c)])
                S.op("act", lambda e, c=c: e.activation(out=ss[:, 4 + c:5 + c], in_=ss[:, c:c + 1], func=AF.Ln,
                                                        scale=1.0 / D, bias=epsc[:]),
                     reads=[("ss", c), "epsc"], writes=[("ss", 4 + c)])
                S.op("act", lambda e, c=c: e.activation(out=ss[:, 4 + c:5 + c], in_=ss[:, 4 + c:5 + c], func=AF.Exp, scale=-0.5),
                     reads=[("ss", 4 + c)], writes=[("ss", 4 + c)])
                S.op("dve", lambda e, xi=xi, blk=blk, c=c: e.tensor_scalar(out=xn[blk][:], in0=xi[:], scalar1=ss[:, 4 + c:5 + c],
                                                                           scalar2=None, op0=ALU.mult),
                     reads=[kx, ("ss", 4 + c)], writes=[("xn", blk)])

        def transp(t, kcs=None):
            tb = t % 2
            for kc in (range(KC) if kcs is None else kcs):
                b = ps_next()

                def f(e, b=b, kc=kc):
                    for blk in range(4):
                        ins = e.matmul(psb[b][:, blk * P:(blk + 1) * P], lhsT=xn[blk][:, kc * P:(kc + 1) * P], rhs=ident[:],
                                       start=True, stop=True)
                    return ins
                S.op("pe", f, reads=[("xn", i) for i in range(4)] + ["ident"], writes=[("ps", b)])
                gc_ = gcol[:, l * KC + kc:l * KC + kc + 1]
                if kc % 2:
                    S.op("act", lambda e, b=b, kc=kc, gc_=gc_: e.activation(out=xnT[tb][:, kc, :], in_=psb[b][:], func=AF.Copy, scale=gc_),
                         reads=[("ps", b), "gcol"], writes=[("xnT", tb)])
                else:
                    S.op("dve", lambda e, b=b, kc=kc, gc_=gc_: e.tensor_scalar(out=xnT[tb][:, kc, :], in0=psb[b][:], scalar1=gc_,
                                                                              scalar2=None, op0=ALU.mult),
                         reads=[("ps", b), "gcol"], writes=[("xnT", tb)])
        return norm, transp

    pend = []

    def defer(fn, depth=1):
        pend.append([depth, fn])

    trq = []

    def tick():
        if trq:
            trq.pop(0)()
        ready = [p for p in pend if p[0] <= 1]
        for p in pend:
            p[0] -= 1
        pend[:] = [p for p in pend if p[0] > 0]
        for p in ready:
            p[1]()

    def flush():
        while pend:
            tick()

    def mm_feat(wb, j, xT, tb, wkey):
        b = ps_next()

        def f(e):
            for kc in range(KC):
                ins = e.matmul(psb[b][:], lhsT=wb[:, kc, j * P:(j + 1) * P], rhs=xT[tb][:, kc, :],
                               start=(kc == 0), stop=(kc == KC - 1))
            return ins
        S.op("pe", f, reads=[wkey, ("xnT", tb)], writes=[("ps", b)])
        tick()
        return b

    def mm_tok(wb, blk, xT, tb, wkey, n):
        b = ps_next()

        def f(e):
            for kc in range(KC):
                ins = e.matmul(psb[b][:, 0:n], lhsT=xT[tb][:, kc, blk * P:(blk + 1) * P], rhs=wb[:, kc, 0:n],
                               start=(kc == 0), stop=(kc == KC - 1))
            return ins
        S.op("pe", f, reads=[wkey, ("xnT", tb)], writes=[("ps", b)])
        tick()
        return b

    def silu_gate(b, ebuf, ei, out_ap, out_key, mul_ap=None, mul_key=None):
        eb = ebuf[ei]
        ek = ("ebuf", ei)
        S.op("act", lambda e: e.activation(out=eb[:], in_=psb[b][:], func=AF.Exp, scale=-1.0),
             reads=[("ps", b)], writes=[ek])
        S.op("act", lambda e: e.activation(out=eb[:], in_=eb[:], func=AF.Ln, bias=onec[:]), reads=[ek, "onec"], writes=[ek])
        S.op("act", lambda e: e.activation(out=eb[:], in_=eb[:], func=AF.Exp, scale=-1.0), reads=[ek], writes=[ek])
        if mul_ap is None:
            S.op("dve", lambda e: e.tensor_tensor(out=out_ap, in0=psb[b][:], in1=eb[:], op=ALU.mult),
                 reads=[("ps", b), ek], writes=[out_key])
        else:
            S.op("dve", lambda e: e.tensor_tensor(out=eb[:], in0=psb[b][:], in1=eb[:], op=ALU.mult),
                 reads=[("ps", b), ek], writes=[ek])
            S.op("dve", lambda e: e.tensor_tensor(out=out_ap, in0=eb[:], in1=mul_ap, op=ALU.mult),
                 reads=[ek, mul_key], writes=[out_key])

    def out_proj(t, aT, akey, wo, x_src, x_dst, wobuf, xrbuf, ctr, resident=None, hook=None):
        for og in range(4):
            k = ctr[0]
            ctr[0] += 1
            if resident is None:
                wob = wobuf[k % 2]
                wk = ("wob", k % 2)
                ld(wob[:], wo[og].rearrange("p (kc o) -> p kc o", o=512), wk)
            else:
                wob = resident[og]
                wk = ("wres", og)
            xr = xrbuf[k % len(xrbuf)]
            xk = ("xr", k % len(xrbuf))
            src = x_src[t * TT:(t + 1) * TT, og * 512:(og + 1) * 512].rearrange("(b p) c -> p b c", p=P)
            dst = x_dst[t * TT:(t + 1) * TT, og * 512:(og + 1) * 512].rearrange("(b p) c -> p b c", p=P)
            ld(xr[:], src, xk)
            for blk in range(4):
                b = ps_next()

                def f(e, b=b, blk=blk, wob=wob):
                    for kc in range(KC):
                        ins = e.matmul(psb[b][:], lhsT=aT[:, kc, blk * P:(blk + 1) * P], rhs=wob[:, kc, :],
                                       start=(kc == 0), stop=(kc == KC - 1))
                    return ins
                S.op("pe", f, reads=[akey, wk], writes=[("ps", b)])
                S.op("dve", lambda e, b=b, blk=blk, xr=xr: e.tensor_tensor(out=xr[:, blk, :], in0=psb[b][:], in1=xr[:, blk, :], op=ALU.add),
                     reads=[("ps", b), xk], writes=[xk])
                if hook is not None:
                    hook(og * 4 + blk)
            S.dma("pool", lambda e, dst=dst, xr=xr: e.dma_start(out=dst, in_=xr[:]), reads=[xk], writes=[("xdst", t, og)])

    def even_layer(l, x_src, x_dst):
        i = l // 2
        A.reset()
        xin = [A.alloc([P, D], F32) for _ in range(2)]
        xn = [A.alloc([P, D], BF16) for _ in range(4)]
        xnT = [A.alloc([P, KC, TT], BF16) for _ in range(2)]
        wbuf = [A.alloc([P, KC, WG], BF16) for _ in range(3)]
        st_qk = A.alloc([P, 16, TT], BF16)
        st_sg = A.alloc([P, 16, TT], BF16)
        st_v = A.alloc([P, 4, 512], BF16)
        st_P = A.alloc([P, 4, 4, 256], BF16)
        qraw = [A.alloc([P, TT], F32) for _ in range(3)]
        sq = [A.alloc([P, TT], BF16) for _ in range(3)]
        rr = [A.alloc([P, TT], F32) for _ in range(2)]
        fT = [A.alloc([P, TT], BF16) for _ in range(2)]
        ebuf = [A.alloc([P, TT], F32) for _ in range(2)]
        W12 = A.alloc([P, 4, 256], BF16)
        wf_f = A.alloc([P, 4, P], F32)
        wf_b = A.alloc([P, 4, P], BF16)
        cs_b = A.alloc([P, 2 * P], BF16)
        ld(wf_f[:], p_wf[:, i * 4 * P:(i + 1) * 4 * P].rearrange("p (g d) -> p g d", d=P), "wf_f")
        ld(cs_b[:], c_cs, "cs_b")
        S.op("dve", lambda e: e.tensor_copy(out=wf_b[:], in_=wf_f[:]), reads=["wf_f"], writes=["wf_b"])
        for g in range(4):
            if STOP in ("e1n", "e1t"):
                break
            b = ps_next()

            def f(e, b=b, g=g):
                e.matmul(psb[b][:, 0:P], lhsT=cs_b[:, 0:P], rhs=wf_b[:, g, :], start=True, stop=True)
                return e.matmul(psb[b][:, P:2 * P], lhsT=cs_b[:, P:2 * P], rhs=wf_b[:, g, :], start=True, stop=True)
            S.op("pe", f, reads=["cs_b", "wf_b"], writes=[("ps", b)])
            S.op("dve", lambda e, b=b, g=g: e.tensor_copy(out=W12[:, g, :], in_=psb[b][:, 0:256]), reads=[("ps", b)], writes=["W12"])

        inline = (l == 0)
        NSTG = 2
        if inline:
            stg_f = [bg_f, A.alloc([P, KC, 256], F32)]
            LEAD = 2
            n_in = need[("in", 0)]

            def inline_load(j):
                if j < n_in:
                    fb, fk = stg_f[j % 2], (("stg_f", j % 2) if j % 2 else "bg_f")
                    S.dma("sp", lambda e, fb=fb, s_=tasks[j][0]: e.dma_start(out=fb[:], in_=s_), writes=[fk])

            def inline_cast(j):
                if j < n_in:
                    fb, fk = stg_f[j % 2], (("stg_f", j % 2) if j % 2 else "bg_f")
                    bb, bk = bg_b[j % NSTG], ("bg_b", j % NSTG)
                    if j % 2:
                        S.op("act", lambda e, fb=fb, bb=bb: e.activation(out=bb[:], in_=fb[:], func=AF.Copy), reads=[fk], writes=[bk])
                    else:
                        S.op("dve", lambda e, fb=fb, bb=bb: e.tensor_copy(out=bb[:], in_=fb[:]), reads=[fk], writes=[bk])
                    S.dma("pool", lambda e, bb=bb, d=tasks[j][1]: e.dma_start(out=d, in_=bb[:]), reads=[bk], writes=[("wtask", j)])
            tdone[0] = n_in
        norm, transp = make_norm(l, x_src, xin, xn, xnT)
        norm(0)
        if inline:
            for j in range(LEAD):
                inline_load(j)
                inline_cast(j)
        if STOP == "e1n":
            S.barrier()
            return
        transp(0)
        if STOP == "e1t":
            S.barrier()
            return
        wctr = 0
        cnt = [0]
        wlimit = int(STOP[3:]) if (STOP or "").startswith("e1w") else None
        for t in range(NT):
            tb = t % 2
            if wlimit is not None and t > 0:
                break
            for wg in range(NGE):
                if wlimit is not None and wg >= wlimit:
                    break
                if wg == 0:
                    bg_tile()
                if wg >= 4 and wg % 2 == 0 and (wg - 4) // 2 < 4 and t + 1 < NT:
                    norm(t + 1, ((wg - 4) // 2,))
                if wg == 12 and t + 1 < NT and wlimit is None:
                    for kc_ in range(KC):
                        trq.append(lambda kc_=kc_, t=t: transp(t + 1, (kc_,)))
                wb = wbuf[wctr % 3]
                wk = ("wbuf", wctr % 3)
                wctr += 1
                if inline and t == 0:
                    inline_load(wg + LEAD)
                    wb = bg_b[wg % NSTG]
                    wk = ("bg_b", wg % NSTG)
                else:
                    ld(wb[:], wie[i][wg].rearrange("p (kc o) -> p kc o", o=WG), wk, reads=[("wtask", wg)] if inline else ())
                    if wg % 2 == 1:
                        bg1()
                c0 = wg * WG // P
                for j in range(WG // P):
                    oc = c0 + j
                    if oc < 16:
                        b = mm_feat(wb, j, xnT, tb, wk)
                        n = cnt[0] % 3
                        cnt[0] += 1
                        S.op("dve", lambda e, b=b, n=n: e.tensor_copy(out=qraw[n][:], in_=psb[b][:]), reads=[("ps", b)], writes=[("qraw", n)])
                        S.op("act", lambda e, b=b, n=n: e.activation(out=sq[n][:], in_=psb[b][:], func=AF.Square), reads=[("ps", b)], writes=[("sq", n)])
                        gcol_ = qkg[:, i:i + 1] if oc < 12 else qkg[:, 2 + i:3 + i]

                        def rest(n=n, oc=oc, gcol_=gcol_, r_=cnt[0] % 2):
                            b2 = ps_next()
                            S.op("pe", lambda e, b2=b2, n=n: e.matmul(psb[b2][:], lhsT=ones_s[:], rhs=sq[n][:], start=True, stop=True),
                                 reads=["ones_s", ("sq", n)], writes=[("ps", b2)])
                            S.op("act", lambda e, b2=b2, r_=r_: e.activation(out=rr[r_][:], in_=psb[b2][:], func=AF.Ln, bias=epsc[:]),
                                 reads=[("ps", b2), "epsc"], writes=[("rr", r_)])
                            S.op("act", lambda e, r_=r_: e.activation(out=rr[r_][:], in_=rr[r_][:], func=AF.Exp, scale=-0.5),
                                 reads=[("rr", r_)], writes=[("rr", r_)])
                            S.op("dve", lambda e, n=n, oc=oc, gcol_=gcol_, r_=r_: e.scalar_tensor_tensor(out=st_qk[:, oc, :], in0=qraw[n][:], scalar=gcol_,
                                                                                                 in1=rr[r_][:], op0=ALU.mult, op1=ALU.mult),
                                 reads=[("qraw", n), ("rr", r_), "qkg"], writes=["st_qk"])
                        defer(rest, 2)
                    elif oc < 20:
                        pass
                    elif oc < 24:
                        g = oc - 20
                        b = mm_feat(wb, j, xnT, tb, wk)
                        n = cnt[0] % 2
                        cnt[0] += 1
                        S.op("act", lambda e, b=b, n=n: e.activation(out=fT[n][:], in_=psb[b][:], func=AF.Copy), reads=[("ps", b)], writes=[("fT", n)])
                        def restf(n=n, g=g):
                          for hp in range(2):
                            b2 = ps_next()

                            def f(e, b2=b2, n=n, g=g, hp=hp):
                                for q in range(2):
                                    blk = hp * 2 + q
                                    ins = e.matmul(psb[b2][:, q * 256:(q + 1) * 256], lhsT=fT[n][:, blk * P:(blk + 1) * P], rhs=W12[:, g, :],
                                                   start=True, stop=True)
                                return ins
                            S.op("pe", f, reads=[("fT", n), "W12"], writes=[("ps", b2)])
                            S.op("dve", lambda e, b2=b2, g=g, hp=hp: e.tensor_copy(out=st_P[:, hp * 2:hp * 2 + 2, g, :],
                                                                                 in_=psb[b2][:].rearrange("p (a c) -> p a c", c=256)),
                                 reads=[("ps", b2)], writes=["st_P"])
                        defer(restf, 1)
                    else:
                        b = mm_feat(wb, j, xnT, tb, wk)
                        n = cnt[0] % 2
                        cnt[0] += 1
                        silu_gate(b, ebuf, n, st_sg[:, oc - 24, :], "st_sg")
                if 16 <= c0 < 20:
                    vo = (c0 - 16) * P
                    for blk in range(4):
                        b = mm_tok(wb, blk, xnT, tb, wk, WG)
                        if blk % 2:
                            S.op("act", lambda e, b=b, blk=blk, vo=vo: e.activation(out=st_v[:, blk, vo:vo + WG], in_=psb[b][:, 0:WG], func=AF.Copy),
                                 reads=[("ps", b)], writes=["st_v"])
                        else:
                            S.op("dve", lambda e, b=b, blk=blk, vo=vo: e.tensor_copy(out=st_v[:, blk, vo:vo + WG], in_=psb[b][:, 0:WG]),
                                 reads=[("ps", b)], writes=["st_v"])
                if inline and t == 0:
                    inline_cast(wg + LEAD)
                last = c0 + WG // P - 1
                if last in (15, 19, 23, 39):
                    flush()
                if last == 15:
                    S.dma("pool", lambda e, t=t: e.dma_start(out=qk_s[t].rearrange("p (c n) -> p c n", n=TT), in_=st_qk[:]),
                          reads=["st_qk"], writes=[("qk_s", t)])
                if last == 19:
                    S.dma("pool", lambda e, t=t: e.dma_start(out=v_s[t * TT:(t + 1) * TT, :].rearrange("(b p) c -> p b c", p=P), in_=st_v[:]),
                          reads=["st_v"], writes=[("v_s", t)])
                if last == 23:
                    S.dma("pool", lambda e, t=t: e.dma_start(out=P_s[t * TT:(t + 1) * TT, :].rearrange("(b p) (g c) -> p b g c", p=P, c=256), in_=st_P[:]),
                          reads=["st_P"], writes=[("P_s", t)])
                if last == 39:
                    S.dma("pool", lambda e, t=t: e.dma_start(out=sg_s[t].rearrange("p (c n) -> p c n", n=TT), in_=st_sg[:]),
                          reads=["st_sg"], writes=[("sg_s", t)])
            bg_rest()
            if t + 1 < NT:
                if wlimit is not None:
                    transp(t + 1)
                while trq:
                    trq.pop(0)()
        S.barrier()
        if STOP == "e1" or wlimit is not None:
            return

        A.reset()
        Pres = A.alloc([P, NB, 1024], BF16)
        mbuf = [A.alloc([P, 4, 2, TT], BF16) for _ in range(4)]
        sgf = [A.alloc([P, 4, TT], BF16) for _ in range(2)]
        st_fo = [A.alloc([P, 4, TT], BF16) for _ in range(2)]
        if l == 0:
            build_bias_table()
        mctr = 0
        for t in range(NT):
            n = t % 2
            bg_tile()
            ld(sgf[n][:], sg_s[t].rearrange("p (c n) -> p c n", n=TT)[:, 12:16, :], ("sgf", n))
            banks = [ps_next() for _ in range(4)]
            for grp in range(NG4):
                if t == 0:
                    ld(Pres[:, grp * 4:(grp + 1) * 4, :], P_s[grp * 512:(grp + 1) * 512, :].rearrange("(b p) f -> p b f", p=P), ("Pres", grp))
                mb = mbuf[mctr % 4]
                mk = ("mbuf", mctr % 4)
                mctr += 1
                ld(mb[:], c_mdft[t, grp].rearrange("p (c h n) -> p c h n", h=2, n=TT), mk)
                if grp % 2 == 1:
                    bg1()

                def f(e, grp=grp, mb=mb, banks=banks):
                    for cc in range(4):
                        sc = grp * 4 + cc
                        for hf in range(2):
                            for g in range(4):
                                ins = e.matmul(psb[banks[g]][:], lhsT=Pres[:, sc, g * 256 + hf * P:g * 256 + (hf + 1) * P],
                                               rhs=mb[:, cc, hf, :], start=(sc == 0 and hf == 0), stop=(sc == NB - 1 and hf == 1))
                    return ins
                S.op("pe", f, reads=[mk, ("Pres", grp)], writes=[("ps", bb) for bb in banks])
            for g in range(4):
                S.op("dve", lambda e, g=g, n=n, bb=banks[g]: e.scalar_tensor_tensor(out=st_fo[n][:, g, :], in0=psb[bb][:], scalar=bfc[:, i * 4 + g:i * 4 + g + 1],
                                                                                   in1=sgf[n][:, g, :], op0=ALU.add, op1=ALU.mult),
                     reads=[("ps", banks[g]), ("sgf", n), "bfc"], writes=[("st_fo", n)])
            S.dma("pool", lambda e, t=t, n=n: e.dma_start(out=fo_s[t].rearrange("p (c n) -> p c n", n=TT), in_=st_fo[n][:]),
                  reads=[("st_fo", n)], writes=[("fo_s", t)])
            bg_rest()
        bg_need(("out", l))
        S.barrier()
        if STOP == "e15":
            return

        A.reset()
        qk = [A.alloc([P, 16, TT], BF16) for _ in range(2)]
        kh = [A.alloc([P, 4, 2, P], BF16) for _ in range(2)]
        v6 = [A.alloc([P, 6, 512], BF16) for _ in range(2)]
        sga = [A.alloc([P, 12, TT], BF16) for _ in range(2)]
        aTb = [A.alloc([P, 16, TT], BF16), A.alias([P, 16, TT], BF16, bg_f_off)]
        xrbuf = [A.alloc([P, 4, 512], F32) for _ in range(2)]
        wobuf = [A.alloc([P, KC, 512], BF16) for _ in range(2)]
        esrow = A.alloc([1, 4, 384], BF16)
        PT = [A.alloc([P, 3, 384], BF16) for _ in range(2)]
        t1 = [A.alloc([P, 384], F32) for _ in range(2)]
        t2 = [A.alloc([P, 384], F32) for _ in range(2)]
        biasT = A.alloc([P, 3, 12, P], BF16)
        biasB = A.alloc([P, 2, 12, P], BF16)
        esf = A.alloc([1, 4, 384], F32)
        ld(biasT[:], bias_s.rearrange("k (kc h q) -> k kc h q", kc=3, h=12), "biasT")
        S.op("dve", lambda e: e.tensor_scalar(out=biasB[:, 0, :, :], in0=biasT[:, 0, :, :], scalar1=bmc[:, 0:1], scalar2=None, op0=ALU.add),
             reads=["biasT", "bmc"], writes=["biasB"])
        S.op("dve", lambda e: e.tensor_scalar(out=biasB[:, 1, :, :], in0=biasT[:, 2, :, :], scalar1=bmc[:, 0:1], scalar2=None, op0=ALU.add),
             reads=["biasT", "bmc", "biasB"], writes=["biasB"])
        for h in range(12):
            S.op("dve", lambda e, h=h: e.tensor_scalar(out=esf[0:1, h // 3, (h % 3) * P:(h % 3 + 1) * P], in0=ones_f[0:1, :],
                                                       scalar1=sinkb[0:1, i * 12 + h:i * 12 + h + 1], scalar2=None, op0=ALU.mult),
                 reads=["ones_f", "sinkb", "esf"], writes=["esf"])

        S.op("dve", lambda e: e.tensor_copy(out=esrow[:], in_=esf[0:1, :, :]), reads=["esf"], writes=["esrow"])

        def e2_loads(t):
            n = t % 2
            ld(qk[n][:], qk_s[t].rearrange("p (c n) -> p c n", n=TT), ("qk", n))
            if t > 0:
                ld(kh[n][:, :, 0, :], qk_s[t - 1].rearrange("p (c n) -> p c n", n=TT)[:, 12:16, TT - P:TT], ("kh", n))
            if t + 1 < NT:
                ld(kh[n][:, :, 1, :], qk_s[t + 1].rearrange("p (c n) -> p c n", n=TT)[:, 12:16, 0:P], ("kh", n))
            b0 = max(4 * t - 1, 0)
            b1 = min(4 * t + 5, NB)
            j0 = b0 - (4 * t - 1)
            ld(v6[n][:, j0:j0 + (b1 - b0), :], v_s[b0 * P:b1 * P, :].rearrange("(b p) c -> p b c", p=P), ("v6", n))
            ld(sga[n][:], sg_s[t].rearrange("p (c n) -> p c n", n=TT)[:, 0:12, :], ("sga", n))

        e2_loads(0)
        if NT > 1:
            e2_loads(1)
        octr = [0]
        units = [(blk, h) for blk in range(4) for h in range(4)]
        NU = len(units)
        ustate = {}

        def emit_st(t, u):
            n = t % 2
            blk, h = units[u]
            gb = 4 * t + blk
            m = (t * NU + u) % 2
            kcs = [kc for kc in range(3) if 0 <= gb + kc - 1 < NB]
            for kc in kcs:
                nbk = gb + kc - 1
                lb = nbk - 4 * t
                if 0 <= lb < 4:
                    k_ap = qk[n][:, 12 + h, lb * P:(lb + 1) * P]
                elif lb < 0:
                    k_ap = kh[n][:, h, 0, :]
                else:
                    k_ap = kh[n][:, h, 1, :]
                if kc == 0 and gb == HB:
                    bias_ap = biasB[:, 0, 3 * h:3 * h + 3, :]
                elif kc == 2 and gb == HB - 1:
                    bias_ap = biasB[:, 1, 3 * h:3 * h + 3, :]
                else:
                    bias_ap = biasT[:, kc, 3 * h:3 * h + 3, :]
                b = ps_next()

                def f(e, b=b, k_ap=k_ap, bias_ap=bias_ap, blk=blk, h=h, n=n):
                    o = psb[b][:, 0:384].rearrange("p (g q) -> p g q", q=P)
                    e.matmul(o, lhsT=k_ap, rhs=qk[n][:, 3 * h:3 * h + 3, blk * P:(blk + 1) * P], start=True, stop=False)
                    return e.matmul(o, lhsT=ident[:], rhs=bias_ap, start=False, stop=True)
                S.op("pe", f, reads=[("qk", n), ("kh", n), "biasT", "biasB", "ident"], writes=[("ps", b)])
                S.op("act", lambda e, b=b, kc=kc, m=m: e.activation(out=PT[m][:, kc, :], in_=psb[b][:, 0:384], func=AF.Exp),
                     reads=[("ps", b)], writes=[("PT", m)])
            ustate[(t, u)] = (kcs, m)

        def emit_pv(t, u):
            n = t % 2
            aT = aTb[n]
            ak = ("aT", n)
            blk, h = units[u]
            gb = 4 * t + blk
            kcs, m = ustate.pop((t, u))
            bo = ps_next()
            bd = ps_next()

            def f(e, bo=bo, bd=bd, kcs=kcs, m=m, h=h, gb=gb, t=t, n=n):
                for kc in kcs:
                    j = gb + kc - 1 - (4 * t - 1)
                    e.matmul(psb[bo][:, 0:384], lhsT=v6[n][:, j, h * P:(h + 1) * P], rhs=PT[m][:, kc, :],
                             start=(kc == kcs[0]), stop=(kc == kcs[-1]))
                for kc in kcs:
                    e.matmul(psb[bd][:, 0:384], lhsT=ones1[:], rhs=PT[m][:, kc, :], start=(kc == kcs[0]), stop=False)
                return e.matmul(psb[bd][:, 0:384], lhsT=ones1[0:1, :], rhs=esrow[0:1, h, :], start=False, stop=True)
            S.op("pe", f, reads=[("v6", n), ("PT", m), "ones1", "esrow"], writes=[("ps", bo), ("ps", bd)])
            S.op("dve", lambda e, bd=bd, m=m: e.reciprocal(out=t1[m][:], in_=psb[bd][:, 0:384]), reads=[("ps", bd)], writes=[("t1", m)])
            S.op("dve", lambda e, bo=bo, m=m: e.tensor_tensor(out=t2[m][:], in0=psb[bo][:, 0:384], in1=t1[m][:], op=ALU.mult),
                 reads=[("ps", bo), ("t1", m)], writes=[("t2", m)])
            S.op("pool", lambda e, m=m, h=h, blk=blk, n=n, aT=aT: e.tensor_tensor(out=aT[:, 3 * h:3 * h + 3, blk * P:(blk + 1) * P],
                                                                                 in0=t2[m][:].rearrange("p (g q) -> p g q", q=P),
                                                                                 in1=sga[n][:, 3 * h:3 * h + 3, blk * P:(blk + 1) * P], op=ALU.mult),
                 reads=[("t2", m), ("sga", n), ak], writes=[ak])

        SKEW = 2

        def attn_begin(t):
            n = t % 2
            ld(aTb[n][:, 12:16, :], fo_s[t].rearrange("p (c n) -> p c n", n=TT), ("aT", n))
            emit_st(t, 0)
            for u in range(SKEW):
                attn_step(t, u)

        def attn_step(t, u):
            if u + 1 < NU:
                emit_st(t, u + 1)
            emit_pv(t, u)

        attn_begin(0)
        for u in range(SKEW, NU):
            attn_step(0, u)
        for t in range(NT):
            n = t % 2
            if t + 1 < NT:
                attn_begin(t + 1)

            def hook(u, t=t):
                if t + 1 < NT and u + SKEW < NU:
                    attn_step(t + 1, u + SKEW)
                if u == 5 and t + 2 < NT:
                    e2_loads(t + 2)
            out_proj(t, aTb[n], ("aT", n), woe[i], x_src, x_dst, wobuf, xrbuf, octr, hook=hook)
        S.barrier()
        if l + 1 < NL and tdone[0] < need[("in", l + 1)]:
            bg_need(("in", l + 1))
            S.barrier()

    def odd_layer(l, x_src, x_dst):
        i = l // 2
        A.reset()
        xin = [A.alloc([P, D], F32) for _ in range(2)]
        xn = [A.alloc([P, D], BF16) for _ in range(4)]
        xnT = [A.alloc([P, KC, TT], BF16) for _ in range(2)]
        wbuf = [A.alloc([P, KC, WG], BF16) for _ in range(3)]
        st_h = A.alloc([P, 8, TT], BF16)
        st_bg = A.alloc([P, 8, TT], BF16)
        st_o = A.alloc([P, 24, TT], BF16)
        vn = A.alloc([P, 4, 8, P], BF16)
        vsq = [A.alloc([P, WG], F32) for _ in range(2)]
        vss = [A.alloc([P, 4], F32) for _ in range(2)]
        tmp = [A.alloc([P, P], F32) for _ in range(2)]
        ebuf = [A.alloc([P, TT], F32) for _ in range(2)]
        ws_f = A.alloc([P, 8, P], F32)
        ws_b = A.alloc([P, 8, P], BF16)
        bs_bc = A.alloc([P, 8, P], F32)
        ld(ws_f[:], p_wsT[:, i * 8 * P:(i + 1) * 8 * P].rearrange("p (g q) -> p g q", q=P), "ws_f")
        ld(bs_bc[:], p_bs[:, i * 8 * P:(i + 1) * 8 * P].rearrange("p (g q) -> p g q", q=P), "bs_bc")
        S.op("dve", lambda e: e.tensor_copy(out=ws_b[:], in_=ws_f[:]), reads=["ws_f"], writes=["ws_b"])

        norm, transp = make_norm(l, x_src, xin, xn, xnT)
        norm(0)
        transp(0)
        seg_order = [0, 2, 1, 3, 5, 4, 6]
        groups = [s * 4 + k for s in seg_order for k in range(4)]
        wctr = 0
        cnt = [0]
        for t in range(NT):
            tb = t % 2
            for gi, wg in enumerate(groups):
                if gi == 0:
                    bg_tile()
                if gi >= 6 and gi % 2 == 0 and (gi - 6) // 2 < 4 and t + 1 < NT:
                    norm(t + 1, ((gi - 6) // 2,))
                if gi == 16 and t + 1 < NT:
                    for kc_ in range(KC):
                        trq.append(lambda kc_=kc_, t=t: transp(t + 1, (kc_,)))
                wb = wbuf[wctr % 3]
                wk = ("wbuf", wctr % 3)
                wctr += 1
                ld(wb[:], wio[i][wg].rearrange("p (kc o) -> p kc o", o=WG), wk)
                if gi % 2 == 1:
                    bg1()
                seg = wg // 4
                for j in range(2):
                    c = (wg % 4) * 2 + j
                    if seg == 5:
                        continue
                    b = mm_feat(wb, j, xnT, tb, wk)
                    n = cnt[0] % 2
                    cnt[0] += 1
                    if seg == 0:
                        S.op("act", lambda e, b=b, c=c: e.activation(out=st_h[:, c, :], in_=psb[b][:], func=AF.Copy), reads=[("ps", b)], writes=["st_h"])
                    elif seg == 2:
                        S.op("dve", lambda e, b=b, c=c: e.tensor_tensor(out=st_o[:, c, :], in0=psb[b][:], in1=st_h[:, c, :], op=ALU.mult),
                             reads=[("ps", b), "st_h"], writes=["st_o"])
                    elif seg == 1:
                        S.op("act", lambda e, b=b, c=c: e.activation(out=st_bg[:, c, :], in_=psb[b][:], func=AF.Copy), reads=[("ps", b)], writes=["st_bg"])
                    elif seg == 3:
                        silu_gate(b, ebuf, n, st_o[:, 8 + c, :], "st_o", st_bg[:, c, :], "st_bg")
                    elif seg == 4:
                        S.op("dve", lambda e, b=b, c=c: e.tensor_tensor(out=st_o[:, 16 + c, :], in0=psb[b][:], in1=st_o[:, 16 + c, :], op=ALU.mult),
                             reads=[("ps", b), "st_o"], writes=["st_o"])
                    elif seg == 6:
                        silu_gate(b, ebuf, n, st_o[:, 16 + c, :], "st_o", st_o[:, 16 + c, :], "st_o")
                if seg == 5:
                    g0 = (wg % 4) * 2
                    for blk in range(4):
                        b = mm_tok(wb, blk, xnT, tb, wk, WG)
                        n = cnt[0] % 2
                        cnt[0] += 1
                        S.op("act", lambda e, b=b, n=n: e.activation(out=vsq[n][:], in_=psb[b][:, 0:WG], func=AF.Square), reads=[("ps", b)], writes=[("vsq", n)])
                        S.op("dve", lambda e, n=n: e.tensor_reduce(out=vss[n][:, 0:2], in_=vsq[n][:].rearrange("p (g c) -> p g c", c=P), axis=AX.X, op=ALU.add),
                             reads=[("vsq", n)], writes=[("vss", n)])
                        S.op("act", lambda e, n=n: e.activation(out=vss[n][:, 2:4], in_=vss[n][:, 0:2], func=AF.Ln, scale=1.0 / P, bias=epsc[:]),
                             reads=[("vss", n), "epsc"], writes=[("vss", n)])
                        S.op("act", lambda e, n=n: e.activation(out=vss[n][:, 2:4], in_=vss[n][:, 2:4], func=AF.Exp, scale=-0.5),
                             reads=[("vss", n)], writes=[("vss", n)])
                        for q in range(2):
                            S.op("act", lambda e, b=b, n=n, q=q, blk=blk, g0=g0: e.activation(out=vn[:, blk, g0 + q, :], in_=psb[b][:, q * P:(q + 1) * P], func=AF.Copy,
                                                                                      scale=vss[n][:, 2 + q:3 + q]),
                                 reads=[("ps", b), ("vss", n)], writes=[("vn", blk)])
                        def restv(blk=blk, g0=g0):
                            b2 = ps_next()

                            def f(e, b2=b2, blk=blk, g0=g0):
                                for q in range(2):
                                    ins = e.matmul(psb[b2][:, q * P:(q + 1) * P], lhsT=vn[:, blk, g0 + q, :], rhs=ws_b[:, g0 + q, :], start=True, stop=True)
                                return ins
                            S.op("pe", f, reads=[("vn", blk), "ws_b"], writes=[("ps", b2)])
                            for q in range(2):
                                g = g0 + q
                                S.op("dve", lambda e, b2=b2, q=q, g=g, blk=blk: e.scalar_tensor_tensor(out=st_o[:, 16 + g, blk * P:(blk + 1) * P], in0=psb[b2][:, q * P:(q + 1) * P],
                                                                                            scalar=vgain[:, i * 8 + g:i * 8 + g + 1], in1=bs_bc[:, g, :],
                                                                                            op0=ALU.mult, op1=ALU.add),
                                     reads=[("ps", b2), "vgain", "bs_bc", "st_o"], writes=["st_o"])
                        defer(restv, 4)
            flush()
            S.op("dve", lambda e, t=t: e.tensor_copy(out=mhalo[:, (t * 2) * 8:(t * 2 + 1) * 8], in_=st_o[:, 0:8, 0]), reads=["st_o"], writes=["mhalo"])
            S.op("dve", lambda e, t=t: e.tensor_copy(out=mhalo[:, (t * 2 + 1) * 8:(t * 2 + 2) * 8], in_=st_o[:, 0:8, TT - 1]), reads=["st_o", "mhalo"], writes=["mhalo"])
            S.dma("pool", lambda e, t=t: e.dma_start(out=so_s[t].rearrange("p (c n) -> p c n", n=TT), in_=st_o[:]), reads=["st_o"], writes=[("so_s", t)])
            bg_rest()
            while trq:
                trq.pop(0)()
        bg_need(("out", l))
        S.barrier()

        A.reset()
        mpad = [A.alloc([P, 8, TT + 2], BF16) for _ in range(2)]
        bgs = [A.alloc([P, 8, TT], BF16) for _ in range(2)]
        aT = [A.alloc([P, 16, TT], BF16) for _ in range(2)]
        cacc = [A.alloc([P, TT], F32) for _ in range(2)]
        xrbuf = [A.alloc([P, 4, 512], F32) for _ in range(4)]
        wobuf = None

        def o2_loads(t):
            n = t % 2
            so3 = so_s[t].rearrange("p (c n) -> p c n", n=TT)
            ld(mpad[n][:, :, 1:TT + 1], so3[:, 0:8, :], ("mpad", n))
            ld(bgs[n][:], so3[:, 8:16, :], ("bgs", n))
            ld(aT[n][:, 8:16, :], so3[:, 16:24, :], ("aT", n))

        wres = [A.alloc([P, KC, 512], BF16) for _ in range(4)]
        o2_loads(0)
        for og in range(4):
            ld(wres[og][:], woo[i][og].rearrange("p (kc o) -> p kc o", o=512), ("wres", og))
        octr = [0]
        ccn = [0]

        def halos(t):
            n = t % 2
            mk = ("mpad", n)
            if t == 0:
                S.op("dve", lambda e, n=n: e.memset(mpad[n][:, :, 0], 0.0), reads=[mk], writes=[mk])
            else:
                S.op("dve", lambda e, n=n, t=t: e.tensor_copy(out=mpad[n][:, :, 0], in_=mhalo[:, ((t - 1) * 2 + 1) * 8:((t - 1) * 2 + 2) * 8]),
                     reads=[mk, "mhalo"], writes=[mk])
                if 4 * t == HB:
                    S.op("dve", lambda e, n=n: e.tensor_scalar(out=mpad[n][:, :, 0], in0=mpad[n][:, :, 0], scalar1=bmc[:, 1:2], scalar2=None, op0=ALU.mult),
                         reads=[mk, "bmc"], writes=[mk])
            if t == NT - 1:
                S.op("dve", lambda e, n=n: e.memset(mpad[n][:, :, TT + 1], 0.0), reads=[mk], writes=[mk])
            else:
                S.op("dve", lambda e, n=n, t=t: e.tensor_copy(out=mpad[n][:, :, TT + 1], in_=mhalo[:, ((t + 1) * 2) * 8:((t + 1) * 2 + 1) * 8]),
                     reads=[mk, "mhalo"], writes=[mk])
                if 4 * (t + 1) == HB:
                    S.op("dve", lambda e, n=n: e.tensor_scalar(out=mpad[n][:, :, TT + 1], in0=mpad[n][:, :, TT + 1], scalar1=bmc[:, 1:2], scalar2=None, op0=ALU.mult),
                         reads=[mk, "bmc"], writes=[mk])

        def conv_chunk(t, c):
            n = t % 2
            mk = ("mpad", n)
            ca = cacc[ccn[0] % 2]
            ck = ("cacc", ccn[0] % 2)
            ccn[0] += 1
            w0 = convw[:, i * 24 + 0 * 8 + c:i * 24 + 0 * 8 + c + 1]
            w1 = convw[:, i * 24 + 1 * 8 + c:i * 24 + 1 * 8 + c + 1]
            w2 = convw[:, i * 24 + 2 * 8 + c:i * 24 + 2 * 8 + c + 1]
            S.op("dve", lambda e, ca=ca, n=n, c=c, w0=w0: e.tensor_scalar(out=ca[:], in0=mpad[n][:, c, 0:TT], scalar1=w0, scalar2=None, op0=ALU.mult),
                 reads=[mk, "convw"], writes=[ck])
            S.op("dve", lambda e, ca=ca, n=n, c=c, w1=w1: e.scalar_tensor_tensor(out=ca[:], in0=mpad[n][:, c, 1:TT + 1], scalar=w1, in1=ca[:], op0=ALU.mult, op1=ALU.add),
                 reads=[mk, "convw", ck], writes=[ck])
            S.op("dve", lambda e, ca=ca, n=n, c=c, w2=w2: e.scalar_tensor_tensor(out=ca[:], in0=mpad[n][:, c, 2:TT + 2], scalar=w2, in1=ca[:], op0=ALU.mult, op1=ALU.add),
                 reads=[mk, "convw", ck], writes=[ck])
            S.op("dve", lambda e, ca=ca, n=n, c=c: e.tensor_tensor(out=aT[n][:, c, :], in0=ca[:], in1=bgs[n][:, c, :], op=ALU.mult),
                 reads=[ck, ("bgs", n), ("aT", n)], writes=[("aT", n)])

        halos(0)
        for c in range(8):
            conv_chunk(0, c)
        for t in range(NT):
            n = t % 2
            bg_tile()
            bg1()
            if t + 1 < NT:
                o2_loads(t + 1)
                halos(t + 1)

            def hook(u, t=t):
                if t + 1 < NT and u % 2 == 0:
                    conv_chunk(t + 1, u // 2)
                if u == 8:
                    bg1()
            out_proj(t, aT[n], ("aT", n), woo[i], x_src, x_dst, wobuf, xrbuf, octr, resident=wres, hook=hook)
            bg_rest()
        if l + 1 < NL:
            bg_need(("in", l + 1))
        S.barrier()

    chain = [x_in, xa, xb, xa, xb]
    for l in range(NL):
        src = chain[l]
        dst = y_out if l == NL - 1 else chain[l + 1]
        if l % 2 == 0:
            even_layer(l, src, dst)
        else:
            odd_layer(l, src, dst)
    S.barrier()

    with nc.Block() as block:
        @block.tensor
        def _(e):
            S.replay("pe", e)

        @block.scalar
        def _(e):
            S.replay("act", e)

        @block.vector
        def _(e):
            S.replay("dve", e)

        @block.gpsimd
        def _(e):
            S.replay("pool", e)

        @block.sync
        def _(e):
            S.replay("sp", e)
    stack.close()
    return nc


def _t5_bucket(rel):
    nb = 16
    ret = (rel > 0).astype(np.int32) * nb
    n = np.abs(rel)
    max_exact = nb // 2
    large = max_exact + (np.log(np.maximum(n, 1) / max_exact) / np.log(128 / max_exact) * (nb - max_exact)).astype(np.int32)
    large = np.minimum(large, nb - 1)
    return (ret + np.where(n < max_exact, n, large)).astype(np.int32)


def make_consts(T):
    bf = ml_dtypes.bfloat16
    c = {}
    c["c_ident"] = np.eye(P, dtype=np.float32).astype(bf)
    a = np.arange(P)
    ang = 2 * np.pi * ((a[:, None] * a[None, :]) % P) / P
    c["c_cs"] = np.concatenate([np.cos(ang), np.sin(ang)], axis=1).astype(np.float64) / np.sqrt(P)
    c["c_cs"] = c["c_cs"].astype(np.float32).astype(bf)
    kc = np.arange(3)[:, None, None]
    k = np.arange(P)[None, :, None]
    q = np.arange(P)[None, None, :]
    rel = (kc - 1) * P + k - q
    bucket = _t5_bucket(rel)
    oh = np.zeros((34, 3, P, P), np.float32)
    for b in range(32):
        oh[b] = (bucket == b)
    oh[32] = (np.abs(rel) > 128)
    oh[33] = 1.0
    c["c_oh"] = oh.reshape(34, 3 * P * P).astype(bf)
    return c


def make_mdft(T, two_seq):
    bf = ml_dtypes.bfloat16
    NT, NG4 = T // TT, T // 512
    S_ = T // 2 if two_seq else T
    s = np.arange(T, dtype=np.int64)
    M = np.zeros((T, 2, T), np.float32)
    for seq in range(T // S_):
        idx = s[seq * S_:(seq + 1) * S_] - seq * S_
        ang = 2 * np.pi * ((idx[:, None] * idx[None, :]) % S_).astype(np.float64) / S_
        M[seq * S_:(seq + 1) * S_, 0, seq * S_:(seq + 1) * S_] = np.cos(ang) / np.sqrt(S_)
        M[seq * S_:(seq + 1) * S_, 1, seq * S_:(seq + 1) * S_] = -np.sin(ang) / np.sqrt(S_)
    M = M.reshape(NG4, 4, P, 2, NT, TT).transpose(4, 0, 2, 1, 3, 5)
    return np.ascontiguousarray(M).reshape(NT, NG4, P, 4 * 2 * TT).astype(bf)


def layout_params(norm_gain, rel_bias, q_gain, k_gain, sink, w_f, b_f, conv_w, v_gain, w_s, b_s):
    f = np.float32
    p = {}
    p["p_gcol"] = np.ascontiguousarray(np.asarray(norm_gain, f).reshape(4, KC, P).transpose(2, 0, 1).reshape(P, 4 * KC))
    p["p_qkg"] = np.ascontiguousarray(np.concatenate([np.asarray(q_gain, f).T, np.asarray(k_gain, f).T], axis=1))
    p["p_sink"] = np.ascontiguousarray(np.broadcast_to(np.asarray(sink, f).reshape(1, 24), (P, 24)))
    p["p_bf"] = np.ascontiguousarray(np.asarray(b_f, f).reshape(8, P).T)
    p["p_wf"] = np.ascontiguousarray(np.asarray(w_f, f).reshape(8, P, P).transpose(1, 0, 2).reshape(P, 8 * P))
    p["p_convw"] = np.ascontiguousarray(np.asarray(conv_w, f).reshape(2, 3, 8, P).transpose(3, 0, 1, 2).reshape(P, 48))
    p["p_vgain"] = np.ascontiguousarray(np.asarray(v_gain, f).reshape(2, 8, P).transpose(2, 0, 1).reshape(P, 16))
    p["p_wsT"] = np.ascontiguousarray(np.asarray(w_s, f).reshape(16, P, P).transpose(2, 0, 1).reshape(P, 16 * P))
    p["p_bs"] = np.ascontiguousarray(np.broadcast_to(np.asarray(b_s, f).reshape(1, 16 * P), (P, 16 * P)))
    rbx = np.zeros((34, 12), f)
    rbx[:32] = np.asarray(rel_bias, f)
    rbx[32] = NEG
    rbx[33] = -SHIFT
    p["c_rbx"] = rbx
    return p


_NC_CACHE = {}


def run_cores(xs, two_seq_flags, weights, params, T, NL=4):
    key = (T, NL)
    if key not in _NC_CACHE:
        _NC_CACHE[key] = build(T, NL)
    nc = _NC_CACHE[key]
    consts = make_consts(T)
    md = {True: make_mdft(T, True), False: make_mdft(T, False)}
    in_maps = []
    for x, ts in zip(xs, two_seq_flags):
        m = {"x": np.ascontiguousarray(x, dtype=np.float32)}
        m.update(weights)
        m.update(params)
        m.update(consts)
        m["c_mdft"] = md[bool(ts)]
        bm = np.zeros((P, 2), np.float32)
        bm[:, 0] = NEG if ts else 0.0
        bm[:, 1] = 0.0 if ts else 1.0
        m["c_bm"] = bm
        in_maps.append(m)
    res = run_bass_kernel_spmd(nc, in_maps, core_ids=list(range(len(xs))))
    return [r["y"] for r in res.results]


def kernel(x_prompt, x_sample, norm_gain, rel_bias, w_in_e, w_out_e, q_gain, k_gain, sink, w_f, b_f,
           w_in_o, conv_w, v_gain, w_s, b_s, w_out_o):
    T = 4096
    f = np.float32
    xp = np.asarray(x_prompt, f)
    xs_ = np.asarray(x_sample, f)
    xs = [xp[2 * c:2 * c + 2].reshape(T, D) for c in range(4)] + [xs_[c].reshape(T, D) for c in range(4)]
    flags = [True] * 4 + [False] * 4
    weights = {"w_in_e": np.ascontiguousarray(w_in_e, dtype=f), "w_out_e": np.ascontiguousarray(w_out_e, dtype=f),
               "w_in_o": np.ascontiguousarray(w_in_o, dtype=f), "w_out_o": np.ascontiguousarray(w_out_o, dtype=f)}
    params = layout_params(norm_gain, rel_bias, q_gain, k_gain, sink, w_f, b_f, conv_w, v_gain, w_s, b_s)
    ys = run_cores(xs, flags, weights, params, T, 4)
    y_prompt = np.stack([ys[c].reshape(2, 2048, D) for c in range(4)]).reshape(8, 2048, D)
    y_sample = np.stack([ys[4 + c].reshape(4096, D) for c in range(4)])
    return (y_prompt.astype(f), y_sample.astype(f))
```

```python
import contextlib
import numpy as np
import ml_dtypes
import concourse.bass as bass
import concourse.mybir as mybir
from concourse.bass_utils import run_bass_kernel_spmd

F32 = mybir.dt.float32
BF16 = mybir.dt.bfloat16
AF = mybir.ActivationFunctionType
ALU = mybir.AluOpType
AX = mybir.AxisListType

D = 2048
KC = 16
P = 128
TT = 512
WG = 256
E_IN = 5120
O_IN = 7168
NEG = -30000.0
SHIFT = 4.0
EPS = 1e-6
SAME_SYNC = True
NDMA_SEM = 8
SBUF_LO = 16384 + 512
SBUF_BYTES = 224 * 1024 - 512


class Sched:
    ENGS = ("pe", "act", "dve", "pool", "sp")

    def __init__(self, nc, stack):
        self.nc = nc
        self.sems = {}
        self.prog = {e: [] for e in self.ENGS}
        self.count = {}
        self.seen = {e: {} for e in self.ENGS}
        self.res = {}
        for e in ("pe", "act", "dve", "pool"):
            self.sems["c_" + e] = stack.enter_context(nc.semaphore("c_" + e))
            self.count["c_" + e] = 0
        self.dq = {}
        for q in ("sp", "pool"):
            keys = []
            for i in range(NDMA_SEM):
                k = "d_%s%d" % (q, i)
                self.sems[k] = stack.enter_context(nc.semaphore(k))
                self.count[k] = 0
                keys.append(k)
            self.dq[q] = [keys, 0]

    def _deps(self, reads, writes):
        deps = {}

        def add(tok):
            if tok is None:
                return
            k, v = tok
            if deps.get(k, 0) < v:
                deps[k] = v
        for r in reads:
            st = self.res.get(r)
            if st:
                add(st[0])
        for w in writes:
            st = self.res.get(w)
            if st:
                add(st[0])
                for k, v in st[1].items():
                    add((k, v))
        return deps

    def _commit(self, eng, deps, fn, tok, reads, writes, amount):
        waits = []
        own = "c_" + eng
        for k, v in deps.items():
            if k == own and (eng == "pe" or not SAME_SYNC):
                continue
            if self.seen[eng].get(k, 0) >= v:
                continue
            self.seen[eng][k] = v
            waits.append((k, v))
        self.prog[eng].append((waits, fn, (tok[0], amount)))
        for r in reads:
            st = self.res.setdefault(r, [None, {}])
            if st[1].get(tok[0], 0) < tok[1]:
                st[1][tok[0]] = tok[1]
        for w in writes:
            self.res[w] = [tok, {}]

    def op(self, eng, fn, reads=(), writes=()):
        ps_r = [r for r in reads if isinstance(r, tuple) and r[0] == "ps"]
        if ps_r:
            reads = [r for r in reads if not (isinstance(r, tuple) and r[0] == "ps")]
            writes = list(writes) + ps_r
        deps = self._deps(reads, writes)
        k = "c_" + eng
        self.count[k] += 1
        tok = (k, self.count[k])
        self._commit(eng, deps, fn, tok, reads, writes, 1)
        return tok

    def dma(self, q, fn, reads=(), writes=()):
        keys, rr = self.dq[q]
        k = keys[rr]
        self.dq[q][1] = (rr + 1) % len(keys)
        deps = self._deps(reads, writes)
        if self.count[k] > 0:
            v = 16 * self.count[k]
            if deps.get(k, 0) < v:
                deps[k] = v
        self.count[k] += 1
        tok = (k, 16 * self.count[k])
        self._commit(q, deps, fn, tok, reads, writes, 16)
        return tok

    def barrier(self):
        finals = {}
        for k, c in self.count.items():
            v = c * (16 if k.startswith("d_") else 1)
            if v > 0:
                finals[k] = v
        for e in self.ENGS:
            waits = []
            for k, v in finals.items():
                if k == "c_" + e and e == "pe":
                    continue
                if self.seen[e].get(k, 0) >= v:
                    continue
                self.seen[e][k] = v
                waits.append((k, v))
            if waits:
                self.prog[e].append((waits, None, None))
        self.res = {}

    def replay(self, name, e):
        sems = self.sems
        for waits, fn, inc in self.prog[name]:
            for k, v in waits:
                e.wait_ge(sems[k], v)
            if fn is not None:
                ins = fn(e)
                ins.then_inc(sems[inc[0]], inc[1])


class Arena:
    def __init__(self, nc):
        self.nc = nc
        self.base = SBUF_LO
        self.off = SBUF_LO
        self.n = 0

    def alloc(self, shape, dtype):
        sz = 1
        for s in shape[1:]:
            sz *= s
        sz *= 2 if dtype == BF16 else 4
        sz = (sz + 63) // 64 * 64
        off = self.off
        self.off += sz
        assert self.off <= SBUF_BYTES, "SBUF arena overflow %d" % self.off
        self.n += 1
        self.last_off = off
        return self.nc.alloc_sbuf_tensor_at("sb%d" % self.n, list(shape), dtype, offset=off)

    def alias(self, shape, dtype, off):
        self.n += 1
        return self.nc.alloc_sbuf_tensor_at("sb%d" % self.n, list(shape), dtype, offset=off)

    def mark(self):
        self.base = self.off

    def reset(self):
        self.off = self.base


def build(T, NL=4, STOP=None):
    NB = T // P
    NT = T // TT
    NG4 = NB // 4
    HB = NB // 2
    nc = bass.Bass("TRN2", target_bir_lowering=False)

    def din(name, shape, dt=F32):
        return nc.dram_tensor(name, list(shape), dt, kind="ExternalInput").ap()

    def dscr(name, shape, dt=BF16):
        return nc.dram_tensor(name, list(shape), dt, kind="Internal").ap()

    x_in = din("x", [T, D])
    y_out = nc.dram_tensor("y", [T, D], F32, kind="ExternalOutput").ap()
    w_in_e = din("w_in_e", [2, D, E_IN])
    w_out_e = din("w_out_e", [2, D, D])
    w_in_o = din("w_in_o", [2, D, O_IN])
    w_out_o = din("w_out_o", [2, D, D])
    c_ident = din("c_ident", [P, P], BF16)
    c_cs = din("c_cs", [P, 2 * P], BF16)
    c_oh = din("c_oh", [34, 3 * P * P], BF16)
    c_rbx = din("c_rbx", [34, 12])
    c_mdft = din("c_mdft", [NT, NG4, P, 4 * 2 * TT], BF16)
    c_bm = din("c_bm", [P, 2])
    p_gcol = din("p_gcol", [P, 4 * KC])
    p_qkg = din("p_qkg", [P, 4])
    p_sink = din("p_sink", [P, 24])
    p_bf = din("p_bf", [P, 8])
    p_wf = din("p_wf", [P, 8 * P])
    p_convw = din("p_convw", [P, 48])
    p_vgain = din("p_vgain", [P, 16])
    p_wsT = din("p_wsT", [P, 16 * P])
    p_bs = din("p_bs", [P, 16 * P])

    NGE, NGO, NGW = E_IN // WG, O_IN // WG, D // 512
    wie = [dscr("wie%d" % i, [NGE, P, KC * WG]) for i in range(2)]
    wio = [dscr("wio%d" % i, [NGO, P, KC * WG]) for i in range(2)]
    woe = [dscr("woe%d" % i, [NGW, P, KC * 512]) for i in range(2)]
    woo = [dscr("woo%d" % i, [NGW, P, KC * 512]) for i in range(2)]
    xa = dscr("xa", [T, D], F32)
    xb = dscr("xb", [T, D], F32)
    qk_s = dscr("qk_s", [NT, P, 16 * TT])
    sg_s = dscr("sg_s", [NT, P, 16 * TT])
    v_s = dscr("v_s", [T, 512])
    P_s = dscr("P_s", [T, 1024])
    fo_s = dscr("fo_s", [NT, P, 4 * TT])
    so_s = dscr("so_s", [NT, P, 24 * TT])
    bias_s = dscr("bias_s", [P, 3 * 12 * P])

    stack = contextlib.ExitStack()
    S = Sched(nc, stack)
    A = Arena(nc)
    psb = [nc.alloc_psum_tensor("psb%d" % i, [P, 512], F32) for i in range(8)]
    ps_rr = [0]

    def ps_next():
        i = ps_rr[0]
        ps_rr[0] = (i + 1) % 8
        return i

    ident = A.alloc([P, P], BF16)
    ones_s = A.alloc([P, P], BF16)
    ones1 = A.alloc([P, P], BF16)
    ones_f = A.alloc([P, P], F32)
    epsc = A.alloc([P, 1], F32)
    onec = A.alloc([P, 1], F32)
    gcol = A.alloc([P, 4 * KC], F32)
    qkg = A.alloc([P, 4], F32)
    sinkb = A.alloc([P, 24], F32)
    bfc = A.alloc([P, 8], F32)
    convw = A.alloc([P, 48], F32)
    vgain = A.alloc([P, 16], F32)
    bmc = A.alloc([P, 2], F32)
    mhalo = A.alloc([P, NT * 2 * 8], BF16)
    ss = A.alloc([P, 8], F32)
    bg_f = A.alloc([P, KC, 256], F32)
    bg_f_off = A.last_off
    bg_b = [A.alloc([P, KC, 256], BF16) for _ in range(2)]
    A.mark()

    def ld(dst, src, key, reads=()):
        S.dma("sp", lambda e, d=dst, s=src: e.dma_start(out=d, in_=s), reads=list(reads), writes=[key])

    ld(ident[:], c_ident, "ident")
    ld(gcol[:], p_gcol, "gcol")
    ld(qkg[:], p_qkg, "qkg")
    ld(sinkb[:], p_sink, "sinkb")
    ld(bfc[:], p_bf, "bfc")
    ld(convw[:], p_convw, "convw")
    ld(vgain[:], p_vgain, "vgain")
    ld(bmc[:], c_bm, "bmc")
    S.op("dve", lambda e: e.memset(ones_s[:], 1.0 / 128), writes=["ones_s"])
    S.op("dve", lambda e: e.memset(ones1[:], 1.0), writes=["ones1"])
    S.op("dve", lambda e: e.memset(ones_f[:], 1.0), writes=["ones_f"])
    S.op("dve", lambda e: e.memset(epsc[:], EPS), writes=["epsc"])
    S.op("dve", lambda e: e.memset(onec[:], 1.0), writes=["onec"])
    S.op("dve", lambda e: e.tensor_scalar(out=qkg[:, 0:2], in0=qkg[:, 0:2], scalar1=float(P) ** -0.5,
                                          scalar2=None, op0=ALU.mult), reads=["qkg"], writes=["qkg"])
    S.op("dve", lambda e: e.tensor_scalar(out=sinkb[:], in0=sinkb[:], scalar1=-SHIFT, scalar2=None,
                                          op0=ALU.add), reads=["sinkb"], writes=["sinkb"])
    S.op("act", lambda e: e.activation(out=sinkb[:], in_=sinkb[:], func=AF.Exp), reads=["sinkb"],
         writes=["sinkb"])

    n_even = (NL + 1) // 2
    n_odd = NL // 2
    tasks = []

    def add_tasks(src, dst, ncols, gw):
        for g in range(ncols // 256):
            s_ap = src[:, g * 256:(g + 1) * 256].rearrange("(kc p) o -> p kc o", p=P)
            dg, dh = (g * 256) // gw, ((g * 256) % gw) // 256
            d_ap = dst[dg].rearrange("p (kc o) -> p kc o", o=gw)[:, :, dh * 256:(dh + 1) * 256]
            tasks.append((s_ap, d_ap))

    need = {}
    for l_ in range(NL):
        i_ = l_ // 2
        if l_ % 2 == 0:
            add_tasks(w_in_e[i_], wie[i_], E_IN, WG)
            need[("in", l_)] = len(tasks)
            add_tasks(w_out_e[i_], woe[i_], D, 512)
            need[("out", l_)] = len(tasks)
        else:
            add_tasks(w_in_o[i_], wio[i_], O_IN, WG)
            need[("in", l_)] = len(tasks)
            add_tasks(w_out_o[i_], woo[i_], D, 512)
            need[("out", l_)] = len(tasks)
    tdone = [0]

    def do_cast(task, fb, fk, bb, bk, eng):
        s_ap, d_ap = task
        S.dma("sp", lambda e, fb=fb, s=s_ap: e.dma_start(out=fb[:], in_=s), writes=[fk])
        if eng == "act":
            S.op("act", lambda e, fb=fb, bb=bb: e.activation(out=bb[:], in_=fb[:], func=AF.Copy), reads=[fk], writes=[bk])
        else:
            S.op("dve", lambda e, fb=fb, bb=bb: e.tensor_copy(out=bb[:], in_=fb[:]), reads=[fk], writes=[bk])
        S.dma("pool", lambda e, bb=bb, d=d_ap: e.dma_start(out=d, in_=bb[:]), reads=[bk], writes=[("wtask", tasks.index(task))])

    def bg(k):
        for _ in range(k):
            if tdone[0] >= len(tasks):
                return
            j = tdone[0]
            tdone[0] += 1
            do_cast(tasks[j], bg_f, "bg_f", bg_b[j % 2], ("bg_b", j % 2), "act")

    def bg_need(key):
        while tdone[0] < need[key]:
            bg(1)

    BG_PER_TILE = -(-24 // NT)
    bgl = [0]

    def bg_tile():
        bgl[0] = BG_PER_TILE

    def bg1():
        if bgl[0] > 0:
            bgl[0] -= 1
            bg(1)

    def bg_rest():
        while bgl[0] > 0:
            bg1()
    def build_bias_table():
        rbx_f = A.alloc([34, 12], F32)
        rbx_b = A.alloc([34, 12], BF16)
        ohb = [A.alloc([34, 4096], BF16) for _ in range(2)]
        btabs = [A.alloc([12, 4096], BF16) for _ in range(2)]
        ld(rbx_f[:], c_rbx, "rbx_f")
        S.op("dve", lambda e: e.tensor_copy(out=rbx_b[:], in_=rbx_f[:]), reads=["rbx_f"], writes=["rbx_b"])
        for c in range(12):
            ob = ohb[c % 2]
            ld(ob[:], c_oh[:, c * 4096:(c + 1) * 4096], ("ohb", c % 2))
            for j in range(8):
                b = ps_next()
                S.op("pe", lambda e, b=b, ob=ob, j=j: e.matmul(psb[b][0:12, :], lhsT=rbx_b[:], rhs=ob[:, j * 512:(j + 1) * 512],
                                                             start=True, stop=True),
                     reads=["rbx_b", ("ohb", c % 2)], writes=[("ps", b)])
                col = j * 512
                btab = btabs[c % 2]
                bkey = ("btab", c % 2)
                if j % 2:
                    S.op("act", lambda e, b=b, col=col, btab=btab: e.activation(out=btab[:, col:col + 512], in_=psb[b][0:12, :], func=AF.Copy),
                         reads=[("ps", b)], writes=[bkey])
                else:
                    S.op("dve", lambda e, b=b, col=col, btab=btab: e.tensor_copy(out=btab[:, col:col + 512], in_=psb[b][0:12, :]),
                         reads=[("ps", b)], writes=[bkey])
            kc_, k0 = c // 4, 32 * (c % 4)
            d_ap = bias_s.rearrange("k (kc h q) -> kc h k q", kc=3, h=12)[kc_, :, k0:k0 + 32, :]
            S.dma("pool", lambda e, d_ap=d_ap, btab=btabs[c % 2]: e.dma_start(out=d_ap, in_=btab[:].rearrange("h (k q) -> h k q", q=P)),
                  reads=[("btab", c % 2)], writes=[("bias_s", c)])

    S.barrier()
    A.reset()
    if STOP == "prologue":
        tmpx = A.alloc([P, D], F32)
        for gb in range(NB):
            S.dma("sp", lambda e, gb=gb: e.dma_start(out=tmpx[:], in_=x_in[gb * P:(gb + 1) * P, :]), writes=["tmpx"])
            S.dma("pool", lambda e, gb=gb: e.dma_start(out=y_out[gb * P:(gb + 1) * P, :], in_=tmpx[:]), reads=["tmpx"], writes=[("y", gb)])
        NL = 0

    def make_norm(l, x_src, xin, xn, xnT):
        def norm(t, blks=(0, 1, 2, 3)):
            for blk in blks:
                gb = t * 4 + blk
                xi = xin[gb % 2]
                kx = ("xin", gb % 2)
                ld(xi[:], x_src[gb * P:(gb + 1) * P, :], kx)
                c = gb % 4
                S.op("act", lambda e, xi=xi, blk=blk, c=c: e.activation(out=xn[blk][:], in_=xi[:], func=AF.Square,
                                                                        accum_out=ss[:, c:c + 1]),
                     reads=[kx], writes=[("xn", blk), ("ss", c)])
                S.op("act", lambda e, c=c: e.activation(out=ss[:, 4 + c:5 + c], in_=ss[:, c:c + 1], func=AF.Ln,
                                                        scale=1.0 / D, bias=epsc[:]),
                     reads=[("ss", c), "epsc"], writes=[("ss", 4 + c)])
                S.op("act", lambda e, c=c: e.activation(out=ss[:, 4 + c:5 + c], in_=ss[:, 4 + c:5 + c], func=AF.Exp, scale=-0.5),
                     reads=[("ss", 4 + c)], writes=[("ss", 4 + c)])
                S.op("dve", lambda e, xi=xi, blk=blk, c=c: e.tensor_scalar(out=xn[blk][:], in0=xi[:], scalar1=ss[:, 4 + c:5 + c],
                                                                           scalar2=None, op0=ALU.mult),
                     reads=[kx, ("ss", 4 + c)], writes=[("xn", blk)])

        def transp(t, kcs=None):
            tb = t % 2
            for kc in (range(KC) if kcs is None else kcs):
                b = ps_next()

                def f(e, b=b, kc=kc):
                    for blk in range(4):
                        ins = e.matmul(psb[b][:, blk * P:(blk + 1) * P], lhsT=xn[blk][:, kc * P:(kc + 1) * P], rhs=ident[:],
                                       start=True, stop=True)
                    return ins
                S.op("pe", f, reads=[("xn", i) for i in range(4)] + ["ident"], writes=[("ps", b)])
                gc_ = gcol[:, l * KC + kc:l * KC + kc + 1]
                if kc % 2:
                    S.op("act", lambda e, b=b, kc=kc, gc_=gc_: e.activation(out=xnT[tb][:, kc, :], in_=psb[b][:], func=AF.Copy, scale=gc_),
                         reads=[("ps", b), "gcol"], writes=[("xnT", tb)])
                else:
                    S.op("dve", lambda e, b=b, kc=kc, gc_=gc_: e.tensor_scalar(out=xnT[tb][:, kc, :], in0=psb[b][:], scalar1=gc_,
                                                                              scalar2=None, op0=ALU.mult),
                         reads=[("ps", b), "gcol"], writes=[("xnT", tb)])
        return norm, transp

    pend = []

    def defer(fn, depth=1):
        pend.append([depth, fn])

    trq = []

    def tick():
        if trq:
            trq.pop(0)()
        ready = [p for p in pend if p[0] <= 1]
        for p in pend:
            p[0] -= 1
        pend[:] = [p for p in pend if p[0] > 0]
        for p in ready:
            p[1]()

    def flush():
        while pend:
            tick()

    def mm_feat(wb, j, xT, tb, wkey):
        b = ps_next()

        def f(e):
            for kc in range(KC):
                ins = e.matmul(psb[b][:], lhsT=wb[:, kc, j * P:(j + 1) * P], rhs=xT[tb][:, kc, :],
                               start=(kc == 0), stop=(kc == KC - 1))
            return ins
        S.op("pe", f, reads=[wkey, ("xnT", tb)], writes=[("ps", b)])
        tick()
        return b

    def mm_tok(wb, blk, xT, tb, wkey, n):
        b = ps_next()

        def f(e):
            for kc in range(KC):
                ins = e.matmul(psb[b][:, 0:n], lhsT=xT[tb][:, kc, blk * P:(blk + 1) * P], rhs=wb[:, kc, 0:n],
                               start=(kc == 0), stop=(kc == KC - 1))
            return ins
        S.op("pe", f, reads=[wkey, ("xnT", tb)], writes=[("ps", b)])
        tick()
        return b

    def silu_gate(b, ebuf, ei, out_ap, out_key, mul_ap=None, mul_key=None):
        eb = ebuf[ei]
        ek = ("ebuf", ei)
        S.op("act", lambda e: e.activation(out=eb[:], in_=psb[b][:], func=AF.Exp, scale=-1.0),
             reads=[("ps", b)], writes=[ek])
        S.op("act", lambda e: e.activation(out=eb[:], in_=eb[:], func=AF.Ln, bias=onec[:]), reads=[ek, "onec"], writes=[ek])
        S.op("act", lambda e: e.activation(out=eb[:], in_=eb[:], func=AF.Exp, scale=-1.0), reads=[ek], writes=[ek])
        if mul_ap is None:
            S.op("dve", lambda e: e.tensor_tensor(out=out_ap, in0=psb[b][:], in1=eb[:], op=ALU.mult),
                 reads=[("ps", b), ek], writes=[out_key])
        else:
            S.op("dve", lambda e: e.tensor_tensor(out=eb[:], in0=psb[b][:], in1=eb[:], op=ALU.mult),
                 reads=[("ps", b), ek], writes=[ek])
            S.op("dve", lambda e: e.tensor_tensor(out=out_ap, in0=eb[:], in1=mul_ap, op=ALU.mult),
                 reads=[ek, mul_key], writes=[out_key])

    def out_proj(t, aT, akey, wo, x_src, x_dst, wobuf, xrbuf, ctr, resident=None, hook=None):
        for og in range(4):
            k = ctr[0]
            ctr[0] += 1
            if resident is None:
                wob = wobuf[k % 2]
                wk = ("wob", k % 2)
                ld(wob[:], wo[og].rearrange("p (kc o) -> p kc o", o=512), wk)
            else:
                wob = resident[og]
                wk = ("wres", og)
            xr = xrbuf[k % len(xrbuf)]
            xk = ("xr", k % len(xrbuf))
            src = x_src[t * TT:(t + 1) * TT, og * 512:(og + 1) * 512].rearrange("(b p) c -> p b c", p=P)
            dst = x_dst[t * TT:(t + 1) * TT, og * 512:(og + 1) * 512].rearrange("(b p) c -> p b c", p=P)
            ld(xr[:], src, xk)
            for blk in range(4):
                b = ps_next()

                def f(e, b=b, blk=blk, wob=wob):
                    for kc in range(KC):
                        ins = e.matmul(psb[b][:], lhsT=aT[:, kc, blk * P:(blk + 1) * P], rhs=wob[:, kc, :],
                                       start=(kc == 0), stop=(kc == KC - 1))
                    return ins
                S.op("pe", f, reads=[akey, wk], writes=[("ps", b)])
                S.op("dve", lambda e, b=b, blk=blk, xr=xr: e.tensor_tensor(out=xr[:, blk, :], in0=psb[b][:], in1=xr[:, blk, :], op=ALU.add),
                     reads=[("ps", b), xk], writes=[xk])
                if hook is not None:
                    hook(og * 4 + blk)
            S.dma("pool", lambda e, dst=dst, xr=xr: e.dma_start(out=dst, in_=xr[:]), reads=[xk], writes=[("xdst", t, og)])

    def even_layer(l, x_src, x_dst):
        i = l // 2
        A.reset()
        xin = [A.alloc([P, D], F32) for _ in range(2)]
        xn = [A.alloc([P, D], BF16) for _ in range(4)]
        xnT = [A.alloc([P, KC, TT], BF16) for _ in range(2)]
        wbuf = [A.alloc([P, KC, WG], BF16) for _ in range(3)]
        st_qk = A.alloc([P, 16, TT], BF16)
        st_sg = A.alloc([P, 16, TT], BF16)
        st_v = A.alloc([P, 4, 512], BF16)
        st_P = A.alloc([P, 4, 4, 256], BF16)
        qraw = [A.alloc([P, TT], F32) for _ in range(3)]
        sq = [A.alloc([P, TT], BF16) for _ in range(3)]
        rr = [A.alloc([P, TT], F32) for _ in range(2)]
        fT = [A.alloc([P, TT], BF16) for _ in range(2)]
        ebuf = [A.alloc([P, TT], F32) for _ in range(2)]
        W12 = A.alloc([P, 4, 256], BF16)
        wf_f = A.alloc([P, 4, P], F32)
        wf_b = A.alloc([P, 4, P], BF16)
        cs_b = A.alloc([P, 2 * P], BF16)
        ld(wf_f[:], p_wf[:, i * 4 * P:(i + 1) * 4 * P].rearrange("p (g d) -> p g d", d=P), "wf_f")
        ld(cs_b[:], c_cs, "cs_b")
        S.op("dve", lambda e: e.tensor_copy(out=wf_b[:], in_=wf_f[:]), reads=["wf_f"], writes=["wf_b"])
        for g in range(4):
            if STOP in ("e1n", "e1t"):
                break
            b = ps_next()

            def f(e, b=b, g=g):
                e.matmul(psb[b][:, 0:P], lhsT=cs_b[:, 0:P], rhs=wf_b[:, g, :], start=True, stop=True)
                return e.matmul(psb[b][:, P:2 * P], lhsT=cs_b[:, P:2 * P], rhs=wf_b[:, g, :], start=True, stop=True)
            S.op("pe", f, reads=["cs_b", "wf_b"], writes=[("ps", b)])
            S.op("dve", lambda e, b=b, g=g: e.tensor_copy(out=W12[:, g, :], in_=psb[b][:, 0:256]), reads=[("ps", b)], writes=["W12"])

        inline = (l == 0)
        NSTG = 2
        if inline:
            stg_f = [bg_f, A.alloc([P, KC, 256], F32)]
            LEAD = 2
            n_in = need[("in", 0)]

            def inline_load(j):
                if j < n_in:
                    fb, fk = stg_f[j % 2], (("stg_f", j % 2) if j % 2 else "bg_f")
                    S.dma("sp", lambda e, fb=fb, s_=tasks[j][0]: e.dma_start(out=fb[:], in_=s_), writes=[fk])

            def inline_cast(j):
                if j < n_in:
                    fb, fk = stg_f[j % 2], (("stg_f", j % 2) if j % 2 else "bg_f")
                    bb, bk = bg_b[j % NSTG], ("bg_b", j % NSTG)
                    if j % 2:
                        S.op("act", lambda e, fb=fb, bb=bb: e.activation(out=bb[:], in_=fb[:], func=AF.Copy), reads=[fk], writes=[bk])
                    else:
                        S.op("dve", lambda e, fb=fb, bb=bb: e.tensor_copy(out=bb[:], in_=fb[:]), reads=[fk], writes=[bk])
                    S.dma("pool", lambda e, bb=bb, d=tasks[j][1]: e.dma_start(out=d, in_=bb[:]), reads=[bk], writes=[("wtask", j)])
            tdone[0] = n_in
        norm, transp = make_norm(l, x_src, xin, xn, xnT)
        norm(0)
        if inline:
            for j in range(LEAD):
                inline_load(j)
                inline_cast(j)
        if STOP == "e1n":
            S.barrier()
            return
        transp(0)
        if STOP == "e1t":
            S.barrier()
            return
        wctr = 0
        cnt = [0]
        wlimit = int(STOP[3:]) if (STOP or "").startswith("e1w") else None
        for t in range(NT):
            tb = t % 2
            if wlimit is not None and t > 0:
                break
            for wg in range(NGE):
                if wlimit is not None and wg >= wlimit:
                    break
                if wg == 0:
                    bg_tile()
                if wg >= 4 and wg % 2 == 0 and (wg - 4) // 2 < 4 and t + 1 < NT:
                    norm(t + 1, ((wg - 4) // 2,))
                if wg == 12 and t + 1 < NT and wlimit is None:
                    for kc_ in range(KC):
                        trq.append(lambda kc_=kc_, t=t: transp(t + 1, (kc_,)))
                wb = wbuf[wctr % 3]
                wk = ("wbuf", wctr % 3)
                wctr += 1
                if inline and t == 0:
                    inline_load(wg + LEAD)
                    wb = bg_b[wg % NSTG]
                    wk = ("bg_b", wg % NSTG)
                else:
                    ld(wb[:], wie[i][wg].rearrange("p (kc o) -> p kc o", o=WG), wk, reads=[("wtask", wg)] if inline else ())
                    if wg % 2 == 1:
                        bg1()
                c0 = wg * WG // P
                for j in range(WG // P):
                    oc = c0 + j
                    if oc < 16:
                        b = mm_feat(wb, j, xnT, tb, wk)
                        n = cnt[0] % 3
                        cnt[0] += 1
                        S.op("dve", lambda e, b=b, n=n: e.tensor_copy(out=qraw[n][:], in_=psb[b][:]), reads=[("ps", b)], writes=[("qraw", n)])
                        S.op("act", lambda e, b=b, n=n: e.activation(out=sq[n][:], in_=psb[b][:], func=AF.Square), reads=[("ps", b)], writes=[("sq", n)])
                        gcol_ = qkg[:, i:i + 1] if oc < 12 else qkg[:, 2 + i:3 + i]

                        def rest(n=n, oc=oc, gcol_=gcol_, r_=cnt[0] % 2):
                            b2 = ps_next()
                            S.op("pe", lambda e, b2=b2, n=n: e.matmul(psb[b2][:], lhsT=ones_s[:], rhs=sq[n][:], start=True, stop=True),
                                 reads=["ones_s", ("sq", n)], writes=[("ps", b2)])
                            S.op("act", lambda e, b2=b2, r_=r_: e.activation(out=rr[r_][:], in_=psb[b2][:], func=AF.Ln, bias=epsc[:]),
                                 reads=[("ps", b2), "epsc"], writes=[("rr", r_)])
                            S.op("act", lambda e, r_=r_: e.activation(out=rr[r_][:], in_=rr[r_][:], func=AF.Exp, scale=-0.5),
                                 reads=[("rr", r_)], writes=[("rr", r_)])
                            S.op("dve", lambda e, n=n, oc=oc, gcol_=gcol_, r_=r_: e.scalar_tensor_tensor(out=st_qk[:, oc, :], in0=qraw[n][:], scalar=gcol_,
                                                                                                 in1=rr[r_][:], op0=ALU.mult, op1=ALU.mult),
                                 reads=[("qraw", n), ("rr", r_), "qkg"], writes=["st_qk"])
                        defer(rest, 2)
                    elif oc < 20:
                        pass
                    elif oc < 24:
                        g = oc - 20
                        b = mm_feat(wb, j, xnT, tb, wk)
                        n = cnt[0] % 2
                        cnt[0] += 1
                        S.op("act", lambda e, b=b, n=n: e.activation(out=fT[n][:], in_=psb[b][:], func=AF.Copy), reads=[("ps", b)], writes=[("fT", n)])
                        def restf(n=n, g=g):
                          for hp in range(2):
                            b2 = ps_next()

                            def f(e, b2=b2, n=n, g=g, hp=hp):
                                for q in range(2):
                                    blk = hp * 2 + q
                                    ins = e.matmul(psb[b2][:, q * 256:(q + 1) * 256], lhsT=fT[n][:, blk * P:(blk + 1) * P], rhs=W12[:, g, :],
                                                   start=True, stop=True)
                                return ins
                            S.op("pe", f, reads=[("fT", n), "W12"], writes=[("ps", b2)])
                            S.op("dve", lambda e, b2=b2, g=g, hp=hp: e.tensor_copy(out=st_P[:, hp * 2:hp * 2 + 2, g, :],
                                                                                 in_=psb[b2][:].rearrange("p (a c) -> p a c", c=256)),
                                 reads=[("ps", b2)], writes=["st_P"])
                        defer(restf, 1)
                    else:
                        b = mm_feat(wb, j, xnT, tb, wk)
                        n = cnt[0] % 2
                        cnt[0] += 1
                        silu_gate(b, ebuf, n, st_sg[:, oc - 24, :], "st_sg")
                if 16 <= c0 < 20:
                    vo = (c0 - 16) * P
                    for blk in range(4):
                        b = mm_tok(wb, blk, xnT, tb, wk, WG)
                        if blk % 2:
                            S.op("act", lambda e, b=b, blk=blk, vo=vo: e.activation(out=st_v[:, blk, vo:vo + WG], in_=psb[b][:, 0:WG], func=AF.Copy),
                                 reads=[("ps", b)], writes=["st_v"])
                        else:
                            S.op("dve", lambda e, b=b, blk=blk, vo=vo: e.tensor_copy(out=st_v[:, blk, vo:vo + WG], in_=psb[b][:, 0:WG]),
                                 reads=[("ps", b)], writes=["st_v"])
                if inline and t == 0:
                    inline_cast(wg + LEAD)
                last = c0 + WG // P - 1
                if last in (15, 19, 23, 39):
                    flush()
                if last == 15:
                    S.dma("pool", lambda e, t=t: e.dma_start(out=qk_s[t].rearrange("p (c n) -> p c n", n=TT), in_=st_qk[:]),
                          reads=["st_qk"], writes=[("qk_s", t)])
                if last == 19:
                    S.dma("pool", lambda e, t=t: e.dma_start(out=v_s[t * TT:(t + 1) * TT, :].rearrange("(b p) c -> p b c", p=P), in_=st_v[:]),
                          reads=["st_v"], writes=[("v_s", t)])
                if last == 23:
                    S.dma("pool", lambda e, t=t: e.dma_start(out=P_s[t * TT:(t + 1) * TT, :].rearrange("(b p) (g c) -> p b g c", p=P, c=256), in_=st_P[:]),
                          reads=["st_P"], writes=[("P_s", t)])
                if last == 39:
                    S.dma("pool", lambda e, t=t: e.dma_start(out=sg_s[t].rearrange("p (c n) -> p c n", n=TT), in_=st_sg[:]),
                          reads=["st_sg"], writes=[("sg_s", t)])
            bg_rest()
            if t + 1 < NT:
                if wlimit is not None:
                    transp(t + 1)
                while trq:
                    trq.pop(0)()
        S.barrier()
        if STOP == "e1" or wlimit is not None:
            return

        A.reset()
        Pres = A.alloc([P, NB, 1024], BF16)
        mbuf = [A.alloc([P, 4, 2, TT], BF16) for _ in range(4)]
        sgf = [A.alloc([P, 4, TT], BF16) for _ in range(2)]
        st_fo = [A.alloc([P, 4, TT], BF16) for _ in range(2)]
        if l == 0:
            build_bias_table()
        mctr = 0
        for t in range(NT):
            n = t % 2
            bg_tile()
            ld(sgf[n][:], sg_s[t].rearrange("p (c n) -> p c n", n=TT)[:, 12:16, :], ("sgf", n))
            banks = [ps_next() for _ in range(4)]
            for grp in range(NG4):
                if t == 0:
                    ld(Pres[:, grp * 4:(grp + 1) * 4, :], P_s[grp * 512:(grp + 1) * 512, :].rearrange("(b p) f -> p b f", p=P), ("Pres", grp))
                mb = mbuf[mctr % 4]
                mk = ("mbuf", mctr % 4)
                mctr += 1
                ld(mb[:], c_mdft[t, grp].rearrange("p (c h n) -> p c h n", h=2, n=TT), mk)
                if grp % 2 == 1:
                    bg1()

                def f(e, grp=grp, mb=mb, banks=banks):
                    for cc in range(4):
                        sc = grp * 4 + cc
                        for hf in range(2):
                            for g in range(4):
                                ins = e.matmul(psb[banks[g]][:], lhsT=Pres[:, sc, g * 256 + hf * P:g * 256 + (hf + 1) * P],
                                               rhs=mb[:, cc, hf, :], start=(sc == 0 and hf == 0), stop=(sc == NB - 1 and hf == 1))
                    return ins
                S.op("pe", f, reads=[mk, ("Pres", grp)], writes=[("ps", bb) for bb in banks])
            for g in range(4):
                S.op("dve", lambda e, g=g, n=n, bb=banks[g]: e.scalar_tensor_tensor(out=st_fo[n][:, g, :], in0=psb[bb][:], scalar=bfc[:, i * 4 + g:i * 4 + g + 1],
                                                                                   in1=sgf[n][:, g, :], op0=ALU.add, op1=ALU.mult),
                     reads=[("ps", banks[g]), ("sgf", n), "bfc"], writes=[("st_fo", n)])
            S.dma("pool", lambda e, t=t, n=n: e.dma_start(out=fo_s[t].rearrange("p (c n) -> p c n", n=TT), in_=st_fo[n][:]),
                  reads=[("st_fo", n)], writes=[("fo_s", t)])
            bg_rest()
        bg_need(("out", l))
        S.barrier()
        if STOP == "e15":
            return

        A.reset()
        qk = [A.alloc([P, 16, TT], BF16) for _ in range(2)]
        kh = [A.alloc([P, 4, 2, P], BF16) for _ in range(2)]
        v6 = [A.alloc([P, 6, 512], BF16) for _ in range(2)]
        sga = [A.alloc([P, 12, TT], BF16) for _ in range(2)]
        aTb = [A.alloc([P, 16, TT], BF16), A.alias([P, 16, TT], BF16, bg_f_off)]
        xrbuf = [A.alloc([P, 4, 512], F32) for _ in range(2)]
        wobuf = [A.alloc([P, KC, 512], BF16) for _ in range(2)]
        esrow = A.alloc([1, 4, 384], BF16)
        PT = [A.alloc([P, 3, 384], BF16) for _ in range(2)]
        t1 = [A.alloc([P, 384], F32) for _ in range(2)]
        t2 = [A.alloc([P, 384], F32) for _ in range(2)]
        biasT = A.alloc([P, 3, 12, P], BF16)
        biasB = A.alloc([P, 2, 12, P], BF16)
        esf = A.alloc([1, 4, 384], F32)
        ld(biasT[:], bias_s.rearrange("k (kc h q) -> k kc h q", kc=3, h=12), "biasT")
        S.op("dve", lambda e: e.tensor_scalar(out=biasB[:, 0, :, :], in0=biasT[:, 0, :, :], scalar1=bmc[:, 0:1], scalar2=None, op0=ALU.add),
             reads=["biasT", "bmc"], writes=["biasB"])
        S.op("dve", lambda e: e.tensor_scalar(out=biasB[:, 1, :, :], in0=biasT[:, 2, :, :], scalar1=bmc[:, 0:1], scalar2=None, op0=ALU.add),
             reads=["biasT", "bmc", "biasB"], writes=["biasB"])
        for h in range(12):
            S.op("dve", lambda e, h=h: e.tensor_scalar(out=esf[0:1, h // 3, (h % 3) * P:(h % 3 + 1) * P], in0=ones_f[0:1, :],
                                                       scalar1=sinkb[0:1, i * 12 + h:i * 12 + h + 1], scalar2=None, op0=ALU.mult),
                 reads=["ones_f", "sinkb", "esf"], writes=["esf"])

        S.op("dve", lambda e: e.tensor_copy(out=esrow[:], in_=esf[0:1, :, :]), reads=["esf"], writes=["esrow"])

        def e2_loads(t):
            n = t % 2
            ld(qk[n][:], qk_s[t].rearrange("p (c n) -> p c n", n=TT), ("qk", n))
            if t > 0:
                ld(kh[n][:, :, 0, :], qk_s[t - 1].rearrange("p (c n) -> p c n", n=TT)[:, 12:16, TT - P:TT], ("kh", n))
            if t + 1 < NT:
                ld(kh[n][:, :, 1, :], qk_s[t + 1].rearrange("p (c n) -> p c n", n=TT)[:, 12:16, 0:P], ("kh", n))
            b0 = max(4 * t - 1, 0)
            b1 = min(4 * t + 5, NB)
            j0 = b0 - (4 * t - 1)
            ld(v6[n][:, j0:j0 + (b1 - b0), :], v_s[b0 * P:b1 * P, :].rearrange("(b p) c -> p b c", p=P), ("v6", n))
            ld(sga[n][:], sg_s[t].rearrange("p (c n) -> p c n", n=TT)[:, 0:12, :], ("sga", n))

        e2_loads(0)
        if NT > 1:
            e2_loads(1)
        octr = [0]
        units = [(blk, h) for blk in range(4) for h in range(4)]
        NU = len(units)
        ustate = {}

        def emit_st(t, u):
            n = t % 2
            blk, h = units[u]
            gb = 4 * t + blk
            m = (t * NU + u) % 2
            kcs = [kc for kc in range(3) if 0 <= gb + kc - 1 < NB]
            for kc in kcs:
                nbk = gb + kc - 1
                lb = nbk - 4 * t
                if 0 <= lb < 4:
                    k_ap = qk[n][:, 12 + h, lb * P:(lb + 1) * P]
                elif lb < 0:
                    k_ap = kh[n][:, h, 0, :]
                else:
                    k_ap = kh[n][:, h, 1, :]
                if kc == 0 and gb == HB:
                    bias_ap = biasB[:, 0, 3 * h:3 * h + 3, :]
                elif kc == 2 and gb == HB - 1:
                    bias_ap = biasB[:, 1, 3 * h:3 * h + 3, :]
                else:
                    bias_ap = biasT[:, kc, 3 * h:3 * h + 3, :]
                b = ps_next()

                def f(e, b=b, k_ap=k_ap, bias_ap=bias_ap, blk=blk, h=h, n=n):
                    o = psb[b][:, 0:384].rearrange("p (g q) -> p g q", q=P)
                    e.matmul(o, lhsT=k_ap, rhs=qk[n][:, 3 * h:3 * h + 3, blk * P:(blk + 1) * P], start=True, stop=False)
                    return e.matmul(o, lhsT=ident[:], rhs=bias_ap, start=False, stop=True)
                S.op("pe", f, reads=[("qk", n), ("kh", n), "biasT", "biasB", "ident"], writes=[("ps", b)])
                S.op("act", lambda e, b=b, kc=kc, m=m: e.activation(out=PT[m][:, kc, :], in_=psb[b][:, 0:384], func=AF.Exp),
                     reads=[("ps", b)], writes=[("PT", m)])
            ustate[(t, u)] = (kcs, m)

        def emit_pv(t, u):
            n = t % 2
            aT = aTb[n]
            ak = ("aT", n)
            blk, h = units[u]
            gb = 4 * t + blk
            kcs, m = ustate.pop((t, u))
            bo = ps_next()
            bd = ps_next()

            def f(e, bo=bo, bd=bd, kcs=kcs, m=m, h=h, gb=gb, t=t, n=n):
                for kc in kcs:
                    j = gb + kc - 1 - (4 * t - 1)
                    e.matmul(psb[bo][:, 0:384], lhsT=v6[n][:, j, h * P:(h + 1) * P], rhs=PT[m][:, kc, :],
                             start=(kc == kcs[0]), stop=(kc == kcs[-1]))
                for kc in kcs:
                    e.matmul(psb[bd][:, 0:384], lhsT=ones1[:], rhs=PT[m][:, kc, :], start=(kc == kcs[0]), stop=False)
                return e.matmul(psb[bd][:, 0:384], lhsT=ones1[0:1, :], rhs=esrow[0:1, h, :], start=False, stop=True)
            S.op("pe", f, reads=[("v6", n), ("PT", m), "ones1", "esrow"], writes=[("ps", bo), ("ps", bd)])
            S.op("dve", lambda e, bd=bd, m=m: e.reciprocal(out=t1[m][:], in_=psb[bd][:, 0:384]), reads=[("ps", bd)], writes=[("t1", m)])
            S.op("dve", lambda e, bo=bo, m=m: e.tensor_tensor(out=t2[m][:], in0=psb[bo][:, 0:384], in1=t1[m][:], op=ALU.mult),
                 reads=[("ps", bo), ("t1", m)], writes=[("t2", m)])
            S.op("pool", lambda e, m=m, h=h, blk=blk, n=n, aT=aT: e.tensor_tensor(out=aT[:, 3 * h:3 * h + 3, blk * P:(blk + 1) * P],
                                                                                 in0=t2[m][:].rearrange("p (g q) -> p g q", q=P),
                                                                                 in1=sga[n][:, 3 * h:3 * h + 3, blk * P:(blk + 1) * P], op=ALU.mult),
                 reads=[("t2", m), ("sga", n), ak], writes=[ak])

        SKEW = 2

        def attn_begin(t):
            n = t % 2
            ld(aTb[n][:, 12:16, :], fo_s[t].rearrange("p (c n) -> p c n", n=TT), ("aT", n))
            emit_st(t, 0)
            for u in range(SKEW):
                attn_step(t, u)

        def attn_step(t, u):
            if u + 1 < NU:
                emit_st(t, u + 1)
            emit_pv(t, u)

        attn_begin(0)
        for u in range(SKEW, NU):
            attn_step(0, u)
        for t in range(NT):
            n = t % 2
            if t + 1 < NT:
                attn_begin(t + 1)

            def hook(u, t=t):
                if t + 1 < NT and u + SKEW < NU:
                    attn_step(t + 1, u + SKEW)
                if u == 5 and t + 2 < NT:
                    e2_loads(t + 2)
            out_proj(t, aTb[n], ("aT", n), woe[i], x_src, x_dst, wobuf, xrbuf, octr, hook=hook)
        S.barrier()
        if l + 1 < NL and tdone[0] < need[("in", l + 1)]:
            bg_need(("in", l + 1))
            S.barrier()

    def odd_layer(l, x_src, x_dst):
        i = l // 2
        A.reset()
        xin = [A.alloc([P, D], F32) for _ in range(2)]
        xn = [A.alloc([P, D], BF16) for _ in range(4)]
        xnT = [A.alloc([P, KC, TT], BF16) for _ in range(2)]
        wbuf = [A.alloc([P, KC, WG], BF16) for _ in range(3)]
        st_h = A.alloc([P, 8, TT], BF16)
        st_bg = A.alloc([P, 8, TT], BF16)
        st_o = A.alloc([P, 24, TT], BF16)
        vn = A.alloc([P, 4, 8, P], BF16)
        vsq = [A.alloc([P, WG], F32) for _ in range(2)]
        vss = [A.alloc([P, 4], F32) for _ in range(2)]
        tmp = [A.alloc([P, P], F32) for _ in range(2)]
        ebuf = [A.alloc([P, TT], F32) for _ in range(2)]
        ws_f = A.alloc([P, 8, P], F32)
        ws_b = A.alloc([P, 8, P], BF16)
        bs_bc = A.alloc([P, 8, P], F32)
        ld(ws_f[:], p_wsT[:, i * 8 * P:(i + 1) * 8 * P].rearrange("p (g q) -> p g q", q=P), "ws_f")
        ld(bs_bc[:], p_bs[:, i * 8 * P:(i + 1) * 8 * P].rearrange("p (g q) -> p g q", q=P), "bs_bc")
        S.op("dve", lambda e: e.tensor_copy(out=ws_b[:], in_=ws_f[:]), reads=["ws_f"], writes=["ws_b"])

        norm, transp = make_norm(l, x_src, xin, xn, xnT)
        norm(0)
        transp(0)
        seg_order = [0, 2, 1, 3, 5, 4, 6]
        groups = [s * 4 + k for s in seg_order for k in range(4)]
        wctr = 0
        cnt = [0]
        for t in range(NT):
            tb = t % 2
            for gi, wg in enumerate(groups):
                if gi == 0:
                    bg_tile()
                if gi >= 6 and gi % 2 == 0 and (gi - 6) // 2 < 4 and t + 1 < NT:
                    norm(t + 1, ((gi - 6) // 2,))
                if gi == 16 and t + 1 < NT:
                    for kc_ in range(KC):
                        trq.append(lambda kc_=kc_, t=t: transp(t + 1, (kc_,)))
                wb = wbuf[wctr % 3]
                wk = ("wbuf", wctr % 3)
                wctr += 1
                ld(wb[:], wio[i][wg].rearrange("p (kc o) -> p kc o", o=WG), wk)
                if gi % 2 == 1:
                    bg1()
                seg = wg // 4
                for j in range(2):
                    c = (wg % 4) * 2 + j
                    if seg == 5:
                        continue
                    b = mm_feat(wb, j, xnT, tb, wk)
                    n = cnt[0] % 2
                    cnt[0] += 1
                    if seg == 0:
                        S.op("act", lambda e, b=b, c=c: e.activation(out=st_h[:, c, :], in_=psb[b][:], func=AF.Copy), reads=[("ps", b)], writes=["st_h"])
                    elif seg == 2:
                        S.op("dve", lambda e, b=b, c=c: e.tensor_tensor(out=st_o[:, c, :], in0=psb[b][:], in1=st_h[:, c, :], op=ALU.mult),
                             reads=[("ps", b), "st_h"], writes=["st_o"])
                    elif seg == 1:
                        S.op("act", lambda e, b=b, c=c: e.activation(out=st_bg[:, c, :], in_=psb[b][:], func=AF.Copy), reads=[("ps", b)], writes=["st_bg"])
                    elif seg == 3:
                        silu_gate(b, ebuf, n, st_o[:, 8 + c, :], "st_o", st_bg[:, c, :], "st_bg")
                    elif seg == 4:
                        S.op("dve", lambda e, b=b, c=c: e.tensor_tensor(out=st_o[:, 16 + c, :], in0=psb[b][:], in1=st_o[:, 16 + c, :], op=ALU.mult),
                             reads=[("ps", b), "st_o"], writes=["st_o"])
                    elif seg == 6:
                        silu_gate(b, ebuf, n, st_o[:, 16 + c, :], "st_o", st_o[:, 16 + c, :], "st_o")
                if seg == 5:
                    g0 = (wg % 4) * 2
                    for blk in range(4):
                        b = mm_tok(wb, blk, xnT, tb, wk, WG)
                        n = cnt[0] % 2
                        cnt[0] += 1
                        S.op("act", lambda e, b=b, n=n: e.activation(out=vsq[n][:], in_=psb[b][:, 0:WG], func=AF.Square), reads=[("ps", b)], writes=[("vsq", n)])
                        S.op("dve", lambda e, n=n: e.tensor_reduce(out=vss[n][:, 0:2], in_=vsq[n][:].rearrange("p (g c) -> p g c", c=P), axis=AX.X, op=ALU.add),
                             reads=[("vsq", n)], writes=[("vss", n)])
                        S.op("act", lambda e, n=n: e.activation(out=vss[n][:, 2:4], in_=vss[n][:, 0:2], func=AF.Ln, scale=1.0 / P, bias=epsc[:]),
                             reads=[("vss", n), "epsc"], writes=[("vss", n)])
                        S.op("act", lambda e, n=n: e.activation(out=vss[n][:, 2:4], in_=vss[n][:, 2:4], func=AF.Exp, scale=-0.5),
                             reads=[("vss", n)], writes=[("vss", n)])
                        for q in range(2):
                            S.op("act", lambda e, b=b, n=n, q=q, blk=blk, g0=g0: e.activation(out=vn[:, blk, g0 + q, :], in_=psb[b][:, q * P:(q + 1) * P], func=AF.Copy,
                                                                                      scale=vss[n][:, 2 + q:3 + q]),
                                 reads=[("ps", b), ("vss", n)], writes=[("vn", blk)])
                        def restv(blk=blk, g0=g0):
                            b2 = ps_next()

                            def f(e, b2=b2, blk=blk, g0=g0):
                                for q in range(2):
                                    ins = e.matmul(psb[b2][:, q * P:(q + 1) * P], lhsT=vn[:, blk, g0 + q, :], rhs=ws_b[:, g0 + q, :], start=True, stop=True)
                                return ins
                            S.op("pe", f, reads=[("vn", blk), "ws_b"], writes=[("ps", b2)])
                            for q in range(2):
                                g = g0 + q
                                S.op("dve", lambda e, b2=b2, q=q, g=g, blk=blk: e.scalar_tensor_tensor(out=st_o[:, 16 + g, blk * P:(blk + 1) * P], in0=psb[b2][:, q * P:(q + 1) * P],
                                                                                            scalar=vgain[:, i * 8 + g:i * 8 + g + 1], in1=bs_bc[:, g, :],
                                                                                            op0=ALU.mult, op1=ALU.add),
                                     reads=[("ps", b2), "vgain", "bs_bc", "st_o"], writes=["st_o"])
                        defer(restv, 4)
            flush()
            S.op("dve", lambda e, t=t: e.tensor_copy(out=mhalo[:, (t * 2) * 8:(t * 2 + 1) * 8], in_=st_o[:, 0:8, 0]), reads=["st_o"], writes=["mhalo"])
            S.op("dve", lambda e, t=t: e.tensor_copy(out=mhalo[:, (t * 2 + 1) * 8:(t * 2 + 2) * 8], in_=st_o[:, 0:8, TT - 1]), reads=["st_o", "mhalo"], writes=["mhalo"])
            S.dma("pool", lambda e, t=t: e.dma_start(out=so_s[t].rearrange("p (c n) -> p c n", n=TT), in_=st_o[:]), reads=["st_o"], writes=[("so_s", t)])
            bg_rest()
            while trq:
                trq.pop(0)()
        bg_need(("out", l))
        S.barrier()

        A.reset()
        mpad = [A.alloc([P, 8, TT + 2], BF16) for _ in range(2)]
        bgs = [A.alloc([P, 8, TT], BF16) for _ in range(2)]
        aT = [A.alloc([P, 16, TT], BF16) for _ in range(2)]
        cacc = [A.alloc([P, TT], F32) for _ in range(4)]
        xrbuf = [A.alloc([P, 4, 512], F32) for _ in range(4)]
        wobuf = None

        def o2_loads(t):
            n = t % 2
            so3 = so_s[t].rearrange("p (c n) -> p c n", n=TT)
            ld(mpad[n][:, :, 1:TT + 1], so3[:, 0:8, :], ("mpad", n))
            ld(bgs[n][:], so3[:, 8:16, :], ("bgs", n))
            ld(aT[n][:, 8:16, :], so3[:, 16:24, :], ("aT", n))

        wres = [A.alloc([P, KC, 512], BF16) for _ in range(4)]
        o2_loads(0)
        for og in range(4):
            ld(wres[og][:], woo[i][og].rearrange("p (kc o) -> p kc o", o=512), ("wres", og))
        octr = [0]
        ccn = [0]

        def halos(t):
            n = t % 2
            mk = ("mpad", n)
            if t == 0:
                S.op("dve", lambda e, n=n: e.memset(mpad[n][:, :, 0], 0.0), reads=[mk], writes=[mk])
            else:
                S.op("dve", lambda e, n=n, t=t: e.tensor_copy(out=mpad[n][:, :, 0], in_=mhalo[:, ((t - 1) * 2 + 1) * 8:((t - 1) * 2 + 2) * 8]),
                     reads=[mk, "mhalo"], writes=[mk])
                if 4 * t == HB:
                    S.op("dve", lambda e, n=n: e.tensor_scalar(out=mpad[n][:, :, 0], in0=mpad[n][:, :, 0], scalar1=bmc[:, 1:2], scalar2=None, op0=ALU.mult),
                         reads=[mk, "bmc"], writes=[mk])
            if t == NT - 1:
                S.op("dve", lambda e, n=n: e.memset(mpad[n][:, :, TT + 1], 0.0), reads=[mk], writes=[mk])
            else:
                S.op("dve", lambda e, n=n, t=t: e.tensor_copy(out=mpad[n][:, :, TT + 1], in_=mhalo[:, ((t + 1) * 2) * 8:((t + 1) * 2 + 1) * 8]),
                     reads=[mk, "mhalo"], writes=[mk])
                if 4 * (t + 1) == HB:
                    S.op("dve", lambda e, n=n: e.tensor_scalar(out=mpad[n][:, :, TT + 1], in0=mpad[n][:, :, TT + 1], scalar1=bmc[:, 1:2], scalar2=None, op0=ALU.mult),
                         reads=[mk, "bmc"], writes=[mk])

        def conv_chunk(t, c):
            n = t % 2
            mk = ("mpad", n)
            ca = cacc[ccn[0] % 4]
            ck = ("cacc", ccn[0] % 4)
            ccn[0] += 1
            w0 = convw[:, i * 24 + 0 * 8 + c:i * 24 + 0 * 8 + c + 1]
            w1 = convw[:, i * 24 + 1 * 8 + c:i * 24 + 1 * 8 + c + 1]
            w2 = convw[:, i * 24 + 2 * 8 + c:i * 24 + 2 * 8 + c + 1]
            S.op("act", lambda e, ca=ca, n=n, c=c, w0=w0: e.activation(out=ca[:], in_=mpad[n][:, c, 0:TT], func=AF.Copy, scale=w0),
                 reads=[mk, "convw"], writes=[ck])
            S.op("dve", lambda e, ca=ca, n=n, c=c, w1=w1: e.scalar_tensor_tensor(out=ca[:], in0=mpad[n][:, c, 1:TT + 1], scalar=w1, in1=ca[:], op0=ALU.mult, op1=ALU.add),
                 reads=[mk, "convw", ck], writes=[ck])
            S.op("dve", lambda e, ca=ca, n=n, c=c, w2=w2: e.scalar_tensor_tensor(out=ca[:], in0=mpad[n][:, c, 2:TT + 2], scalar=w2, in1=ca[:], op0=ALU.mult, op1=ALU.add),
                 reads=[mk, "convw", ck], writes=[ck])
            S.op("pool", lambda e, ca=ca, n=n, c=c: e.tensor_tensor(out=aT[n][:, c, :], in0=ca[:], in1=bgs[n][:, c, :], op=ALU.mult),
                 reads=[ck, ("bgs", n), ("aT", n)], writes=[("aT", n)])

        halos(0)
        for c in range(8):
            conv_chunk(0, c)
        for t in range(NT):
            n = t % 2
            bg_tile()
            bg1()
            if t + 1 < NT:
                o2_loads(t + 1)
                halos(t + 1)

            def hook(u, t=t):
                if t + 1 < NT and u % 2 == 0:
                    conv_chunk(t + 1, u // 2)
                if u == 8:
                    bg1()
            out_proj(t, aT[n], ("aT", n), woo[i], x_src, x_dst, wobuf, xrbuf, octr, resident=wres, hook=hook)
            bg_rest()
        if l + 1 < NL:
            bg_need(("in", l + 1))
        S.barrier()

    chain = [x_in, xa, xb, xa, xb]
    for l in range(NL):
        src = chain[l]
        dst = y_out if l == NL - 1 else chain[l + 1]
        if l % 2 == 0:
            even_layer(l, src, dst)
        else:
            odd_layer(l, src, dst)
    S.barrier()

    with nc.Block() as block:
        @block.tensor
        def _(e):
            S.replay("pe", e)

        @block.scalar
        def _(e):
            S.replay("act", e)

        @block.vector
        def _(e):
            S.replay("dve", e)

        @block.gpsimd
        def _(e):
            S.replay("pool", e)

        @block.sync
        def _(e):
            S.replay("sp", e)
    stack.close()
    return nc


def _t5_bucket(rel):
    nb = 16
    ret = (rel > 0).astype(np.int32) * nb
    n = np.abs(rel)
    max_exact = nb // 2
    large = max_exact + (np.log(np.maximum(n, 1) / max_exact) / np.log(128 / max_exact) * (nb - max_exact)).astype(np.int32)
    large = np.minimum(large, nb - 1)
    return (ret + np.where(n < max_exact, n, large)).astype(np.int32)


def make_consts(T):
    bf = ml_dtypes.bfloat16
    c = {}
    c["c_ident"] = np.eye(P, dtype=np.float32).astype(bf)
    a = np.arange(P)
    ang = 2 * np.pi * ((a[:, None] * a[None, :]) % P) / P
    c["c_cs"] = np.concatenate([np.cos(ang), np.sin(ang)], axis=1).astype(np.float64) / np.sqrt(P)
    c["c_cs"] = c["c_cs"].astype(np.float32).astype(bf)
    kc = np.arange(3)[:, None, None]
    k = np.arange(P)[None, :, None]
    q = np.arange(P)[None, None, :]
    rel = (kc - 1) * P + k - q
    bucket = _t5_bucket(rel)
    oh = np.zeros((34, 3, P, P), np.float32)
    for b in range(32):
        oh[b] = (bucket == b)
    oh[32] = (np.abs(rel) > 128)
    oh[33] = 1.0
    c["c_oh"] = oh.reshape(34, 3 * P * P).astype(bf)
    return c


def make_mdft(T, two_seq):
    bf = ml_dtypes.bfloat16
    NT, NG4 = T // TT, T // 512
    S_ = T // 2 if two_seq else T
    s = np.arange(T, dtype=np.int64)
    M = np.zeros((T, 2, T), np.float32)
    for seq in range(T // S_):
        idx = s[seq * S_:(seq + 1) * S_] - seq * S_
        ang = 2 * np.pi * ((idx[:, None] * idx[None, :]) % S_).astype(np.float64) / S_
        M[seq * S_:(seq + 1) * S_, 0, seq * S_:(seq + 1) * S_] = np.cos(ang) / np.sqrt(S_)
        M[seq * S_:(seq + 1) * S_, 1, seq * S_:(seq + 1) * S_] = -np.sin(ang) / np.sqrt(S_)
    M = M.reshape(NG4, 4, P, 2, NT, TT).transpose(4, 0, 2, 1, 3, 5)
    return np.ascontiguousarray(M).reshape(NT, NG4, P, 4 * 2 * TT).astype(bf)


def layout_params(norm_gain, rel_bias, q_gain, k_gain, sink, w_f, b_f, conv_w, v_gain, w_s, b_s):
    f = np.float32
    p = {}
    p["p_gcol"] = np.ascontiguousarray(np.asarray(norm_gain, f).reshape(4, KC, P).transpose(2, 0, 1).reshape(P, 4 * KC))
    p["p_qkg"] = np.ascontiguousarray(np.concatenate([np.asarray(q_gain, f).T, np.asarray(k_gain, f).T], axis=1))
    p["p_sink"] = np.ascontiguousarray(np.broadcast_to(np.asarray(sink, f).reshape(1, 24), (P, 24)))
    p["p_bf"] = np.ascontiguousarray(np.asarray(b_f, f).reshape(8, P).T)
    p["p_wf"] = np.ascontiguousarray(np.asarray(w_f, f).reshape(8, P, P).transpose(1, 0, 2).reshape(P, 8 * P))
    p["p_convw"] = np.ascontiguousarray(np.asarray(conv_w, f).reshape(2, 3, 8, P).transpose(3, 0, 1, 2).reshape(P, 48))
    p["p_vgain"] = np.ascontiguousarray(np.asarray(v_gain, f).reshape(2, 8, P).transpose(2, 0, 1).reshape(P, 16))
    p["p_wsT"] = np.ascontiguousarray(np.asarray(w_s, f).reshape(16, P, P).transpose(2, 0, 1).reshape(P, 16 * P))
    p["p_bs"] = np.ascontiguousarray(np.broadcast_to(np.asarray(b_s, f).reshape(1, 16 * P), (P, 16 * P)))
    rbx = np.zeros((34, 12), f)
    rbx[:32] = np.asarray(rel_bias, f)
    rbx[32] = NEG
    rbx[33] = -SHIFT
    p["c_rbx"] = rbx
    return p


_NC_CACHE = {}


def run_cores(xs, two_seq_flags, weights, params, T, NL=4):
    key = (T, NL)
    if key not in _NC_CACHE:
        _NC_CACHE[key] = build(T, NL)
    nc = _NC_CACHE[key]
    consts = make_consts(T)
    md = {True: make_mdft(T, True), False: make_mdft(T, False)}
    in_maps = []
    for x, ts in zip(xs, two_seq_flags):
        m = {"x": np.ascontiguousarray(x, dtype=np.float32)}
        m.update(weights)
        m.update(params)
        m.update(consts)
        m["c_mdft"] = md[bool(ts)]
        bm = np.zeros((P, 2), np.float32)
        bm[:, 0] = NEG if ts else 0.0
        bm[:, 1] = 0.0 if ts else 1.0
        m["c_bm"] = bm
        in_maps.append(m)
    res = run_bass_kernel_spmd(nc, in_maps, core_ids=list(range(len(xs))))
    return [r["y"] for r in res.results]


def kernel(x_prompt, x_sample, norm_gain, rel_bias, w_in_e, w_out_e, q_gain, k_gain, sink, w_f, b_f,
           w_in_o, conv_w, v_gain, w_s, b_s, w_out_o):
    T = 4096
    f = np.float32
    xp = np.asarray(x_prompt, f)
    xs_ = np.asarray(x_sample, f)
    xs = [xp[2 * c:2 * c + 2].reshape(T, D) for c in range(4)] + [xs_[c].reshape(T, D) for c in range(4)]
    flags = [True] * 4 + [False] * 4
    weights = {"w_in_e": np.ascontiguousarray(w_in_e, dtype=f), "w_out_e": np.ascontiguousarray(w_out_e, dtype=f),
               "w_in_o": np.ascontiguousarray(w_in_o, dtype=f), "w_out_o": np.ascontiguousarray(w_out_o, dtype=f)}
    params = layout_params(norm_gain, rel_bias, q_gain, k_gain, sink, w_f, b_f, conv_w, v_gain, w_s, b_s)
    ys = run_cores(xs, flags, weights, params, T, 4)
    y_prompt = np.stack([ys[c].reshape(2, 2048, D) for c in range(4)]).reshape(8, 2048, D)
    y_sample = np.stack([ys[4 + c].reshape(4096, D) for c in range(4)])
    return (y_prompt.astype(f), y_sample.astype(f))
```

```python
import contextlib
import numpy as np
import ml_dtypes
import concourse.bass as bass
import concourse.mybir as mybir
from concourse.bass_utils import run_bass_kernel_spmd

F32 = mybir.dt.float32
BF16 = mybir.dt.bfloat16
AF = mybir.ActivationFunctionType
ALU = mybir.AluOpType
AX = mybir.AxisListType

D = 2048
KC = 16
P = 128
TT = 512
WG = 256
E_IN = 5120
O_IN = 7168
NEG = -30000.0
SHIFT = 4.0
EPS = 1e-6
SAME_SYNC = True
NDMA_SEM = 8
SBUF_LO = 16384 + 512
SBUF_BYTES = 224 * 1024 - 512


class Sched:
    ENGS = ("pe", "act", "dve", "pool", "sp")

    def __init__(self, nc, stack):
        self.nc = nc
        self.sems = {}
        self.prog = {e: [] for e in self.ENGS}
        self.count = {}
        self.seen = {e: {} for e in self.ENGS}
        self.res = {}
        for e in ("pe", "act", "dve", "pool"):
            self.sems["c_" + e] = stack.enter_context(nc.semaphore("c_" + e))
            self.count["c_" + e] = 0
        self.dq = {}
        for q in ("sp", "pool"):
            keys = []
            for i in range(NDMA_SEM):
                k = "d_%s%d" % (q, i)
                self.sems[k] = stack.enter_context(nc.semaphore(k))
                self.count[k] = 0
                keys.append(k)
            self.dq[q] = [keys, 0]

    def _deps(self, reads, writes):
        deps = {}

        def add(tok):
            if tok is None:
                return
            k, v = tok
            if deps.get(k, 0) < v:
                deps[k] = v
        for r in reads:
            st = self.res.get(r)
            if st:
                add(st[0])
        for w in writes:
            st = self.res.get(w)
            if st:
                add(st[0])
                for k, v in st[1].items():
                    add((k, v))
        return deps

    def _commit(self, eng, deps, fn, tok, reads, writes, amount):
        waits = []
        own = "c_" + eng
        for k, v in deps.items():
            if k == own and (eng == "pe" or not SAME_SYNC):
                continue
            if self.seen[eng].get(k, 0) >= v:
                continue
            self.seen[eng][k] = v
            waits.append((k, v))
        self.prog[eng].append((waits, fn, (tok[0], amount)))
        for r in reads:
            st = self.res.setdefault(r, [None, {}])
            if st[1].get(tok[0], 0) < tok[1]:
                st[1][tok[0]] = tok[1]
        for w in writes:
            self.res[w] = [tok, {}]

    def op(self, eng, fn, reads=(), writes=()):
        ps_r = [r for r in reads if isinstance(r, tuple) and r[0] == "ps"]
        if ps_r:
            reads = [r for r in reads if not (isinstance(r, tuple) and r[0] == "ps")]
            writes = list(writes) + ps_r
        deps = self._deps(reads, writes)
        k = "c_" + eng
        self.count[k] += 1
        tok = (k, self.count[k])
        self._commit(eng, deps, fn, tok, reads, writes, 1)
        return tok

    def dma(self, q, fn, reads=(), writes=()):
        keys, rr = self.dq[q]
        k = keys[rr]
        self.dq[q][1] = (rr + 1) % len(keys)
        deps = self._deps(reads, writes)
        if self.count[k] > 0:
            v = 16 * self.count[k]
            if deps.get(k, 0) < v:
                deps[k] = v
        self.count[k] += 1
        tok = (k, 16 * self.count[k])
        self._commit(q, deps, fn, tok, reads, writes, 16)
        return tok

    def barrier(self):
        finals = {}
        for k, c in self.count.items():
            v = c * (16 if k.startswith("d_") else 1)
            if v > 0:
                finals[k] = v
        for e in self.ENGS:
            waits = []
            for k, v in finals.items():
                if k == "c_" + e and e == "pe":
                    continue
                if self.seen[e].get(k, 0) >= v:
                    continue
                self.seen[e][k] = v
                waits.append((k, v))
            if waits:
                self.prog[e].append((waits, None, None))
        self.res = {}

    def replay(self, name, e):
        sems = self.sems
        for waits, fn, inc in self.prog[name]:
            for k, v in waits:
                e.wait_ge(sems[k], v)
            if fn is not None:
                ins = fn(e)
                ins.then_inc(sems[inc[0]], inc[1])


class Arena:
    def __init__(self, nc):
        self.nc = nc
        self.base = SBUF_LO
        self.off = SBUF_LO
        self.n = 0

    def alloc(self, shape, dtype):
        sz = 1
        for s in shape[1:]:
            sz *= s
        sz *= 2 if dtype == BF16 else 4
        sz = (sz + 63) // 64 * 64
        off = self.off
        self.off += sz
        assert self.off <= SBUF_BYTES, "SBUF arena overflow %d" % self.off
        self.n += 1
        self.last_off = off
        return self.nc.alloc_sbuf_tensor_at("sb%d" % self.n, list(shape), dtype, offset=off)

    def alias(self, shape, dtype, off):
        self.n += 1
        return self.nc.alloc_sbuf_tensor_at("sb%d" % self.n, list(shape), dtype, offset=off)

    def mark(self):
        self.base = self.off

    def reset(self):
        self.off = self.base


def build(T, NL=4, STOP=None):
    NB = T // P
    NT = T // TT
    NG4 = NB // 4
    HB = NB // 2
    nc = bass.Bass("TRN2", target_bir_lowering=False)

    def din(name, shape, dt=F32):
        return nc.dram_tensor(name, list(shape), dt, kind="ExternalInput").ap()

    def dscr(name, shape, dt=BF16):
        return nc.dram_tensor(name, list(shape), dt, kind="Internal").ap()

    x_in = din("x", [T, D])
    y_out = nc.dram_tensor("y", [T, D], F32, kind="ExternalOutput").ap()
    w_in_e = din("w_in_e", [2, D, E_IN])
    w_out_e = din("w_out_e", [2, D, D])
    w_in_o = din("w_in_o", [2, D, O_IN])
    w_out_o = din("w_out_o", [2, D, D])
    c_ident = din("c_ident", [P, P], BF16)
    c_cs = din("c_cs", [P, 2 * P], BF16)
    c_oh = din("c_oh", [34, 3 * P * P], BF16)
    c_rbx = din("c_rbx", [34, 12])
    c_mdft = din("c_mdft", [NT, NG4, P, 4 * 2 * TT], BF16)
    c_bm = din("c_bm", [P, 2])
    p_gcol = din("p_gcol", [P, 4 * KC])
    p_qkg = din("p_qkg", [P, 4])
    p_sink = din("p_sink", [P, 24])
    p_bf = din("p_bf", [P, 8])
    p_wf = din("p_wf", [P, 8 * P])
    p_convw = din("p_convw", [P, 48])
    p_vgain = din("p_vgain", [P, 16])
    p_wsT = din("p_wsT", [P, 16 * P])
    p_bs = din("p_bs", [P, 16 * P])

    NGE, NGO, NGW = E_IN // WG, O_IN // WG, D // 512
    wie = [dscr("wie%d" % i, [NGE, P, KC * WG]) for i in range(2)]
    wio = [dscr("wio%d" % i, [NGO, P, KC * WG]) for i in range(2)]
    woe = [dscr("woe%d" % i, [NGW, P, KC * 512]) for i in range(2)]
    woo = [dscr("woo%d" % i, [NGW, P, KC * 512]) for i in range(2)]
    xa = dscr("xa", [T, D], F32)
    xb = dscr("xb", [T, D], F32)
    qk_s = dscr("qk_s", [NT, P, 16 * TT])
    sg_s = dscr("sg_s", [NT, P, 16 * TT])
    v_s = dscr("v_s", [T, 512])
    P_s = dscr("P_s", [T, 1024])
    fo_s = dscr("fo_s", [NT, P, 4 * TT])
    so_s = dscr("so_s", [NT, P, 24 * TT])
    bias_s = dscr("bias_s", [P, 3 * 12 * P])

    stack = contextlib.ExitStack()
    S = Sched(nc, stack)
    A = Arena(nc)
    psb = [nc.alloc_psum_tensor("psb%d" % i, [P, 512], F32) for i in range(8)]
    ps_rr = [0]

    def ps_next():
        i = ps_rr[0]
        ps_rr[0] = (i + 1) % 8
        return i

    ident = A.alloc([P, P], BF16)
    ones_s = A.alloc([P, P], BF16)
    ones1 = A.alloc([P, P], BF16)
    ones_f = A.alloc([P, P], F32)
    epsc = A.alloc([P, 1], F32)
    onec = A.alloc([P, 1], F32)
    gcol = A.alloc([P, 4 * KC], F32)
    qkg = A.alloc([P, 4], F32)
    sinkb = A.alloc([P, 24], F32)
    bfc = A.alloc([P, 8], F32)
    convw = A.alloc([P, 48], F32)
    vgain = A.alloc([P, 16], F32)
    bmc = A.alloc([P, 2], F32)
    mhalo = A.alloc([P, NT * 2 * 8], BF16)
    ss = A.alloc([P, 8], F32)
    bg_f = A.alloc([P, KC, 256], F32)
    bg_f_off = A.last_off
    bg_b = [A.alloc([P, KC, 256], BF16) for _ in range(2)]
    A.mark()

    def ld(dst, src, key, reads=()):
        S.dma("sp", lambda e, d=dst, s=src: e.dma_start(out=d, in_=s), reads=list(reads), writes=[key])

    ld(ident[:], c_ident, "ident")
    ld(gcol[:], p_gcol, "gcol")
    ld(qkg[:], p_qkg, "qkg")
    ld(sinkb[:], p_sink, "sinkb")
    ld(bfc[:], p_bf, "bfc")
    ld(convw[:], p_convw, "convw")
    ld(vgain[:], p_vgain, "vgain")
    ld(bmc[:], c_bm, "bmc")
    S.op("dve", lambda e: e.memset(ones_s[:], 1.0 / 128), writes=["ones_s"])
    S.op("dve", lambda e: e.memset(ones1[:], 1.0), writes=["ones1"])
    S.op("dve", lambda e: e.memset(ones_f[:], 1.0), writes=["ones_f"])
    S.op("dve", lambda e: e.memset(epsc[:], EPS), writes=["epsc"])
    S.op("dve", lambda e: e.memset(onec[:], 1.0), writes=["onec"])
    S.op("dve", lambda e: e.tensor_scalar(out=qkg[:, 0:2], in0=qkg[:, 0:2], scalar1=float(P) ** -0.5,
                                          scalar2=None, op0=ALU.mult), reads=["qkg"], writes=["qkg"])
    S.op("dve", lambda e: e.tensor_scalar(out=sinkb[:], in0=sinkb[:], scalar1=-SHIFT, scalar2=None,
                                          op0=ALU.add), reads=["sinkb"], writes=["sinkb"])
    S.op("act", lambda e: e.activation(out=sinkb[:], in_=sinkb[:], func=AF.Exp), reads=["sinkb"],
         writes=["sinkb"])

    n_even = (NL + 1) // 2
    n_odd = NL // 2
    tasks = []

    def add_tasks(src, dst, ncols, gw):
        for g in range(ncols // 256):
            s_ap = src[:, g * 256:(g + 1) * 256].rearrange("(kc p) o -> p kc o", p=P)
            dg, dh = (g * 256) // gw, ((g * 256) % gw) // 256
            d_ap = dst[dg].rearrange("p (kc o) -> p kc o", o=gw)[:, :, dh * 256:(dh + 1) * 256]
            tasks.append((s_ap, d_ap))

    need = {}
    for l_ in range(NL):
        i_ = l_ // 2
        if l_ % 2 == 0:
            add_tasks(w_in_e[i_], wie[i_], E_IN, WG)
            need[("in", l_)] = len(tasks)
            add_tasks(w_out_e[i_], woe[i_], D, 512)
            need[("out", l_)] = len(tasks)
        else:
            add_tasks(w_in_o[i_], wio[i_], O_IN, WG)
            need[("in", l_)] = len(tasks)
            add_tasks(w_out_o[i_], woo[i_], D, 512)
            need[("out", l_)] = len(tasks)
    tdone = [0]

    def do_cast(task, fb, fk, bb, bk, eng):
        s_ap, d_ap = task
        S.dma("sp", lambda e, fb=fb, s=s_ap: e.dma_start(out=fb[:], in_=s), writes=[fk])
        if eng == "act":
            S.op("act", lambda e, fb=fb, bb=bb: e.activation(out=bb[:], in_=fb[:], func=AF.Copy), reads=[fk], writes=[bk])
        else:
            S.op("dve", lambda e, fb=fb, bb=bb: e.tensor_copy(out=bb[:], in_=fb[:]), reads=[fk], writes=[bk])
        S.dma("pool", lambda e, bb=bb, d=d_ap: e.dma_start(out=d, in_=bb[:]), reads=[bk], writes=[("wtask", tasks.index(task))])

    def bg(k):
        for _ in range(k):
            if tdone[0] >= len(tasks):
                return
            j = tdone[0]
            tdone[0] += 1
            do_cast(tasks[j], bg_f, "bg_f", bg_b[j % 2], ("bg_b", j % 2), "act")

    def bg_need(key):
        while tdone[0] < need[key]:
            bg(1)

    BG_PER_TILE = -(-24 // NT)
    bgl = [0]

    def bg_tile():
        bgl[0] = BG_PER_TILE

    def bg1():
        if bgl[0] > 0:
            bgl[0] -= 1
            bg(1)

    def bg_rest():
        while bgl[0] > 0:
            bg1()
    def build_bias_table():
        rbx_f = A.alloc([34, 12], F32)
        rbx_b = A.alloc([34, 12], BF16)
        ohb = [A.alloc([34, 4096], BF16) for _ in range(2)]
        btabs = [A.alloc([12, 4096], BF16) for _ in range(2)]
        ld(rbx_f[:], c_rbx, "rbx_f")
        S.op("dve", lambda e: e.tensor_copy(out=rbx_b[:], in_=rbx_f[:]), reads=["rbx_f"], writes=["rbx_b"])
        for c in range(12):
            ob = ohb[c % 2]
            ld(ob[:], c_oh[:, c * 4096:(c + 1) * 4096], ("ohb", c % 2))
            for j in range(8):
                b = ps_next()
                S.op("pe", lambda e, b=b, ob=ob, j=j: e.matmul(psb[b][0:12, :], lhsT=rbx_b[:], rhs=ob[:, j * 512:(j + 1) * 512],
                                                             start=True, stop=True),
                     reads=["rbx_b", ("ohb", c % 2)], writes=[("ps", b)])
                col = j * 512
                btab = btabs[c % 2]
                bkey = ("btab", c % 2)
                if j % 2:
                    S.op("act", lambda e, b=b, col=col, btab=btab: e.activation(out=btab[:, col:col + 512], in_=psb[b][0:12, :], func=AF.Copy),
                         reads=[("ps", b)], writes=[bkey])
                else:
                    S.op("dve", lambda e, b=b, col=col, btab=btab: e.tensor_copy(out=btab[:, col:col + 512], in_=psb[b][0:12, :]),
                         reads=[("ps", b)], writes=[bkey])
            kc_, k0 = c // 4, 32 * (c % 4)
            d_ap = bias_s.rearrange("k (kc h q) -> kc h k q", kc=3, h=12)[kc_, :, k0:k0 + 32, :]
            S.dma("pool", lambda e, d_ap=d_ap, btab=btabs[c % 2]: e.dma_start(out=d_ap, in_=btab[:].rearrange("h (k q) -> h k q", q=P)),
                  reads=[("btab", c % 2)], writes=[("bias_s", c)])

    S.barrier()
    A.reset()
    if STOP == "prologue":
        tmpx = A.alloc([P, D], F32)
        for gb in range(NB):
            S.dma("sp", lambda e, gb=gb: e.dma_start(out=tmpx[:], in_=x_in[gb * P:(gb + 1) * P, :]), writes=["tmpx"])
            S.dma("pool", lambda e, gb=gb: e.dma_start(out=y_out[gb * P:(gb + 1) * P, :], in_=tmpx[:]), reads=["tmpx"], writes=[("y", gb)])
        NL = 0

    def make_norm(l, x_src, xin, xn, xnT):
        def norm(t, blks=(0, 1, 2, 3)):
            for blk in blks:
                gb = t * 4 + blk
                xi = xin[gb % 2]
                kx = ("xin", gb % 2)
                ld(xi[:], x_src[gb * P:(gb + 1) * P, :], kx)
                c = gb % 4
                S.op("act", lambda e, xi=xi, blk=blk, c=c: e.activation(out=xn[blk][:], in_=xi[:], func=AF.Square,
                                                                        accum_out=ss[:, c:c + 1]),
                     reads=[kx], writes=[("xn", blk), ("ss", c)])
                S.op("act", lambda e, c=c: e.activation(out=ss[:, 4 + c:5 + c], in_=ss[:, c:c + 1], func=AF.Ln,
                                                        scale=1.0 / D, bias=epsc[:]),
                     reads=[("ss", c), "epsc"], writes=[("ss", 4 + c)])
                S.op("act", lambda e, c=c: e.activation(out=ss[:, 4 + c:5 + c], in_=ss[:, 4 + c:5 + c], func=AF.Exp, scale=-0.5),
                     reads=[("ss", 4 + c)], writes=[("ss", 4 + c)])
                S.op("dve", lambda e, xi=xi, blk=blk, c=c: e.tensor_scalar(out=xn[blk][:], in0=xi[:], scalar1=ss[:, 4 + c:5 + c],
                                                                           scalar2=None, op0=ALU.mult),
                     reads=[kx, ("ss", 4 + c)], writes=[("xn", blk)])

        def transp(t, kcs=None):
            tb = t % 2
            for kc in (range(KC) if kcs is None else kcs):
                b = ps_next()

                def f(e, b=b, kc=kc):
                    for blk in range(4):
                        ins = e.matmul(psb[b][:, blk * P:(blk + 1) * P], lhsT=xn[blk][:, kc * P:(kc + 1) * P], rhs=ident[:],
                                       start=True, stop=True)
                    return ins
                S.op("pe", f, reads=[("xn", i) for i in range(4)] + ["ident"], writes=[("ps", b)])
                gc_ = gcol[:, l * KC + kc:l * KC + kc + 1]
                if kc % 2:
                    S.op("act", lambda e, b=b, kc=kc, gc_=gc_: e.activation(out=xnT[tb][:, kc, :], in_=psb[b][:], func=AF.Copy, scale=gc_),
                         reads=[("ps", b), "gcol"], writes=[("xnT", tb)])
                else:
                    S.op("dve", lambda e, b=b, kc=kc, gc_=gc_: e.tensor_scalar(out=xnT[tb][:, kc, :], in0=psb[b][:], scalar1=gc_,
                                                                              scalar2=None, op0=ALU.mult),
                         reads=[("ps", b), "gcol"], writes=[("xnT", tb)])
        return norm, transp

    pend = []

    def defer(fn, depth=1):
        pend.append([depth, fn])

    trq = []

    def tick():
        if trq:
            trq.pop(0)()
        ready = [p for p in pend if p[0] <= 1]
        for p in pend:
            p[0] -= 1
        pend[:] = [p for p in pend if p[0] > 0]
        for p in ready:
            p[1]()

    def flush():
        while pend:
            tick()

    def mm_feat(wb, j, xT, tb, wkey):
        b = ps_next()

        def f(e):
            for kc in range(KC):
                ins = e.matmul(psb[b][:], lhsT=wb[:, kc, j * P:(j + 1) * P], rhs=xT[tb][:, kc, :],
                               start=(kc == 0), stop=(kc == KC - 1))
            return ins
        S.op("pe", f, reads=[wkey, ("xnT", tb)], writes=[("ps", b)])
        tick()
        return b

    def mm_tok(wb, blk, xT, tb, wkey, n):
        b = ps_next()

        def f(e):
            for kc in range(KC):
                ins = e.matmul(psb[b][:, 0:n], lhsT=xT[tb][:, kc, blk * P:(blk + 1) * P], rhs=wb[:, kc, 0:n],
                               start=(kc == 0), stop=(kc == KC - 1))
            return ins
        S.op("pe", f, reads=[wkey, ("xnT", tb)], writes=[("ps", b)])
        tick()
        return b

    def silu_gate(b, ebuf, ei, out_ap, out_key, mul_ap=None, mul_key=None):
        eb = ebuf[ei]
        ek = ("ebuf", ei)
        S.op("act", lambda e: e.activation(out=eb[:], in_=psb[b][:], func=AF.Exp, scale=-1.0),
             reads=[("ps", b)], writes=[ek])
        S.op("act", lambda e: e.activation(out=eb[:], in_=eb[:], func=AF.Ln, bias=onec[:]), reads=[ek, "onec"], writes=[ek])
        S.op("act", lambda e: e.activation(out=eb[:], in_=eb[:], func=AF.Exp, scale=-1.0), reads=[ek], writes=[ek])
        if mul_ap is None:
            S.op("dve", lambda e: e.tensor_tensor(out=out_ap, in0=psb[b][:], in1=eb[:], op=ALU.mult),
                 reads=[("ps", b), ek], writes=[out_key])
        else:
            S.op("dve", lambda e: e.tensor_tensor(out=eb[:], in0=psb[b][:], in1=eb[:], op=ALU.mult),
                 reads=[("ps", b), ek], writes=[ek])
            S.op("dve", lambda e: e.tensor_tensor(out=out_ap, in0=eb[:], in1=mul_ap, op=ALU.mult),
                 reads=[ek, mul_key], writes=[out_key])

    def out_proj(t, aT, akey, wo, x_src, x_dst, wobuf, xrbuf, ctr, resident=None, hook=None):
        for og in range(4):
            k = ctr[0]
            ctr[0] += 1
            if resident is None:
                wob = wobuf[k % 2]
                wk = ("wob", k % 2)
                ld(wob[:], wo[og].rearrange("p (kc o) -> p kc o", o=512), wk)
            else:
                wob = resident[og]
                wk = ("wres", og)
            xr = xrbuf[k % len(xrbuf)]
            xk = ("xr", k % len(xrbuf))
            src = x_src[t * TT:(t + 1) * TT, og * 512:(og + 1) * 512].rearrange("(b p) c -> p b c", p=P)
            dst = x_dst[t * TT:(t + 1) * TT, og * 512:(og + 1) * 512].rearrange("(b p) c -> p b c", p=P)
            ld(xr[:], src, xk)
            for blk in range(4):
                b = ps_next()

                def f(e, b=b, blk=blk, wob=wob):
                    for kc in range(KC):
                        ins = e.matmul(psb[b][:], lhsT=aT[:, kc, blk * P:(blk + 1) * P], rhs=wob[:, kc, :],
                                       start=(kc == 0), stop=(kc == KC - 1))
                    return ins
                S.op("pe", f, reads=[akey, wk], writes=[("ps", b)])
                S.op("dve", lambda e, b=b, blk=blk, xr=xr: e.tensor_tensor(out=xr[:, blk, :], in0=psb[b][:], in1=xr[:, blk, :], op=ALU.add),
                     reads=[("ps", b), xk], writes=[xk])
                if hook is not None:
                    hook(og * 4 + blk)
            S.dma("pool", lambda e, dst=dst, xr=xr: e.dma_start(out=dst, in_=xr[:]), reads=[xk], writes=[("xdst", t, og)])

    def even_layer(l, x_src, x_dst):
        i = l // 2
        A.reset()
        xin = [A.alloc([P, D], F32) for _ in range(2)]
        xn = [A.alloc([P, D], BF16) for _ in range(4)]
        xnT = [A.alloc([P, KC, TT], BF16) for _ in range(2)]
        wbuf = [A.alloc([P, KC, WG], BF16) for _ in range(3)]
        st_qk = A.alloc([P, 16, TT], BF16)
        st_sg = A.alloc([P, 16, TT], BF16)
        st_v = A.alloc([P, 4, 512], BF16)
        st_P = A.alloc([P, 4, 4, 256], BF16)
        qraw = [A.alloc([P, TT], F32) for _ in range(3)]
        sq = [A.alloc([P, TT], BF16) for _ in range(3)]
        rr = [A.alloc([P, TT], F32) for _ in range(2)]
        fT = [A.alloc([P, TT], BF16) for _ in range(2)]
        ebuf = [A.alloc([P, TT], F32) for _ in range(2)]
        W12 = A.alloc([P, 4, 256], BF16)
        wf_f = A.alloc([P, 4, P], F32)
        wf_b = A.alloc([P, 4, P], BF16)
        cs_b = A.alloc([P, 2 * P], BF16)
        ld(wf_f[:], p_wf[:, i * 4 * P:(i + 1) * 4 * P].rearrange("p (g d) -> p g d", d=P), "wf_f")
        ld(cs_b[:], c_cs, "cs_b")
        S.op("dve", lambda e: e.tensor_copy(out=wf_b[:], in_=wf_f[:]), reads=["wf_f"], writes=["wf_b"])
        for g in range(4):
            if STOP in ("e1n", "e1t"):
                break
            b = ps_next()

            def f(e, b=b, g=g):
                e.matmul(psb[b][:, 0:P], lhsT=cs_b[:, 0:P], rhs=wf_b[:, g, :], start=True, stop=True)
                return e.matmul(psb[b][:, P:2 * P], lhsT=cs_b[:, P:2 * P], rhs=wf_b[:, g, :], start=True, stop=True)
            S.op("pe", f, reads=["cs_b", "wf_b"], writes=[("ps", b)])
            S.op("dve", lambda e, b=b, g=g: e.tensor_copy(out=W12[:, g, :], in_=psb[b][:, 0:256]), reads=[("ps", b)], writes=["W12"])

        inline = (l == 0)
        NSTG = 2
        if inline:
            stg_f = [bg_f, A.alloc([P, KC, 256], F32)]
            LEAD = 2
            n_in = need[("in", 0)]

            def inline_load(j):
                if j < n_in:
                    fb, fk = stg_f[j % 2], (("stg_f", j % 2) if j % 2 else "bg_f")
                    S.dma("sp", lambda e, fb=fb, s_=tasks[j][0]: e.dma_start(out=fb[:], in_=s_), writes=[fk])

            def inline_cast(j):
                if j < n_in:
                    fb, fk = stg_f[j % 2], (("stg_f", j % 2) if j % 2 else "bg_f")
                    bb, bk = bg_b[j % NSTG], ("bg_b", j % NSTG)
                    if j % 2:
                        S.op("act", lambda e, fb=fb, bb=bb: e.activation(out=bb[:], in_=fb[:], func=AF.Copy), reads=[fk], writes=[bk])
                    else:
                        S.op("dve", lambda e, fb=fb, bb=bb: e.tensor_copy(out=bb[:], in_=fb[:]), reads=[fk], writes=[bk])
                    S.dma("pool", lambda e, bb=bb, d=tasks[j][1]: e.dma_start(out=d, in_=bb[:]), reads=[bk], writes=[("wtask", j)])
            tdone[0] = n_in
        norm, transp = make_norm(l, x_src, xin, xn, xnT)
        norm(0)
        if inline:
            for j in range(LEAD):
                inline_load(j)
                inline_cast(j)
        if STOP == "e1n":
            S.barrier()
            return
        transp(0)
        if STOP == "e1t":
            S.barrier()
            return
        wctr = 0
        cnt = [0]
        wlimit = int(STOP[3:]) if (STOP or "").startswith("e1w") else None
        for t in range(NT):
            tb = t % 2
            if wlimit is not None and t > 0:
                break
            for wg in range(NGE):
                if wlimit is not None and wg >= wlimit:
                    break
                if wg == 0 and not (inline and t == 0):
                    bg_tile()
                if wg >= 4 and wg % 2 == 0 and (wg - 4) // 2 < 4 and t + 1 < NT:
                    norm(t + 1, ((wg - 4) // 2,))
                if wg == 12 and t + 1 < NT and wlimit is None:
                    for kc_ in range(KC):
                        trq.append(lambda kc_=kc_, t=t: transp(t + 1, (kc_,)))
                wb = wbuf[wctr % 3]
                wk = ("wbuf", wctr % 3)
                wctr += 1
                if inline and t == 0:
                    inline_load(wg + LEAD)
                    wb = bg_b[wg % NSTG]
                    wk = ("bg_b", wg % NSTG)
                else:
                    ld(wb[:], wie[i][wg].rearrange("p (kc o) -> p kc o", o=WG), wk, reads=[("wtask", wg)] if inline else ())
                    if wg % 2 == 1:
                        bg1()
                c0 = wg * WG // P
                for j in range(WG // P):
                    oc = c0 + j
                    if oc < 16:
                        b = mm_feat(wb, j, xnT, tb, wk)
                        n = cnt[0] % 3
                        cnt[0] += 1
                        S.op("dve", lambda e, b=b, n=n: e.tensor_copy(out=qraw[n][:], in_=psb[b][:]), reads=[("ps", b)], writes=[("qraw", n)])
                        S.op("act", lambda e, b=b, n=n: e.activation(out=sq[n][:], in_=psb[b][:], func=AF.Square), reads=[("ps", b)], writes=[("sq", n)])
                        gcol_ = qkg[:, i:i + 1] if oc < 12 else qkg[:, 2 + i:3 + i]

                        def rest(n=n, oc=oc, gcol_=gcol_, r_=cnt[0] % 2):
                            b2 = ps_next()
                            S.op("pe", lambda e, b2=b2, n=n: e.matmul(psb[b2][:], lhsT=ones_s[:], rhs=sq[n][:], start=True, stop=True),
                                 reads=["ones_s", ("sq", n)], writes=[("ps", b2)])
                            S.op("act", lambda e, b2=b2, r_=r_: e.activation(out=rr[r_][:], in_=psb[b2][:], func=AF.Ln, bias=epsc[:]),
                                 reads=[("ps", b2), "epsc"], writes=[("rr", r_)])
                            S.op("act", lambda e, r_=r_: e.activation(out=rr[r_][:], in_=rr[r_][:], func=AF.Exp, scale=-0.5),
                                 reads=[("rr", r_)], writes=[("rr", r_)])
                            S.op("dve", lambda e, n=n, oc=oc, gcol_=gcol_, r_=r_: e.scalar_tensor_tensor(out=st_qk[:, oc, :], in0=qraw[n][:], scalar=gcol_,
                                                                                                 in1=rr[r_][:], op0=ALU.mult, op1=ALU.mult),
                                 reads=[("qraw", n), ("rr", r_), "qkg"], writes=["st_qk"])
                        defer(rest, 2)
                    elif oc < 20:
                        pass
                    elif oc < 24:
                        g = oc - 20
                        b = mm_feat(wb, j, xnT, tb, wk)
                        n = cnt[0] % 2
                        cnt[0] += 1
                        S.op("act", lambda e, b=b, n=n: e.activation(out=fT[n][:], in_=psb[b][:], func=AF.Copy), reads=[("ps", b)], writes=[("fT", n)])
                        def restf(n=n, g=g):
                          for hp in range(2):
                            b2 = ps_next()

                            def f(e, b2=b2, n=n, g=g, hp=hp):
                                for q in range(2):
                                    blk = hp * 2 + q
                                    ins = e.matmul(psb[b2][:, q * 256:(q + 1) * 256], lhsT=fT[n][:, blk * P:(blk + 1) * P], rhs=W12[:, g, :],
                                                   start=True, stop=True)
                                return ins
                            S.op("pe", f, reads=[("fT", n), "W12"], writes=[("ps", b2)])
                            S.op("dve", lambda e, b2=b2, g=g, hp=hp: e.tensor_copy(out=st_P[:, hp * 2:hp * 2 + 2, g, :],
                                                                                 in_=psb[b2][:].rearrange("p (a c) -> p a c", c=256)),
                                 reads=[("ps", b2)], writes=["st_P"])
                        defer(restf, 1)
                    else:
                        b = mm_feat(wb, j, xnT, tb, wk)
                        n = cnt[0] % 2
                        cnt[0] += 1
                        silu_gate(b, ebuf, n, st_sg[:, oc - 24, :], "st_sg")
                if 16 <= c0 < 20:
                    vo = (c0 - 16) * P
                    for blk in range(4):
                        b = mm_tok(wb, blk, xnT, tb, wk, WG)
                        if blk % 2:
                            S.op("act", lambda e, b=b, blk=blk, vo=vo: e.activation(out=st_v[:, blk, vo:vo + WG], in_=psb[b][:, 0:WG], func=AF.Copy),
                                 reads=[("ps", b)], writes=["st_v"])
                        else:
                            S.op("dve", lambda e, b=b, blk=blk, vo=vo: e.tensor_copy(out=st_v[:, blk, vo:vo + WG], in_=psb[b][:, 0:WG]),
                                 reads=[("ps", b)], writes=["st_v"])
                if inline and t == 0:
                    inline_cast(wg + LEAD)
                last = c0 + WG // P - 1
                if last in (15, 19, 23, 39):
                    flush()
                if last == 15:
                    S.dma("pool", lambda e, t=t: e.dma_start(out=qk_s[t].rearrange("p (c n) -> p c n", n=TT), in_=st_qk[:]),
                          reads=["st_qk"], writes=[("qk_s", t)])
                if last == 19:
                    S.dma("pool", lambda e, t=t: e.dma_start(out=v_s[t * TT:(t + 1) * TT, :].rearrange("(b p) c -> p b c", p=P), in_=st_v[:]),
                          reads=["st_v"], writes=[("v_s", t)])
                if last == 23:
                    S.dma("pool", lambda e, t=t: e.dma_start(out=P_s[t * TT:(t + 1) * TT, :].rearrange("(b p) (g c) -> p b g c", p=P, c=256), in_=st_P[:]),
                          reads=["st_P"], writes=[("P_s", t)])
                if last == 39:
                    S.dma("pool", lambda e, t=t: e.dma_start(out=sg_s[t].rearrange("p (c n) -> p c n", n=TT), in_=st_sg[:]),
                          reads=["st_sg"], writes=[("sg_s", t)])
            bg_rest()
            if t + 1 < NT:
                if wlimit is not None:
                    transp(t + 1)
                while trq:
                    trq.pop(0)()
        S.barrier()
        if STOP == "e1" or wlimit is not None:
            return

        A.reset()
        Pres = A.alloc([P, NB, 1024], BF16)
        mbuf = [A.alloc([P, 4, 2, TT], BF16) for _ in range(4)]
        sgf = [A.alloc([P, 4, TT], BF16) for _ in range(2)]
        st_fo = [A.alloc([P, 4, TT], BF16) for _ in range(2)]
        if l == 0:
            build_bias_table()
        mctr = 0
        for t in range(NT):
            n = t % 2
            bg_tile()
            ld(sgf[n][:], sg_s[t].rearrange("p (c n) -> p c n", n=TT)[:, 12:16, :], ("sgf", n))
            banks = [ps_next() for _ in range(4)]
            for grp in range(NG4):
                if t == 0:
                    ld(Pres[:, grp * 4:(grp + 1) * 4, :], P_s[grp * 512:(grp + 1) * 512, :].rearrange("(b p) f -> p b f", p=P), ("Pres", grp))
                mb = mbuf[mctr % 4]
                mk = ("mbuf", mctr % 4)
                mctr += 1
                ld(mb[:], c_mdft[t, grp].rearrange("p (c h n) -> p c h n", h=2, n=TT), mk)
                if grp % 2 == 1:
                    bg1()

                def f(e, grp=grp, mb=mb, banks=banks):
                    for cc in range(4):
                        sc = grp * 4 + cc
                        for hf in range(2):
                            for g in range(4):
                                ins = e.matmul(psb[banks[g]][:], lhsT=Pres[:, sc, g * 256 + hf * P:g * 256 + (hf + 1) * P],
                                               rhs=mb[:, cc, hf, :], start=(sc == 0 and hf == 0), stop=(sc == NB - 1 and hf == 1))
                    return ins
                S.op("pe", f, reads=[mk, ("Pres", grp)], writes=[("ps", bb) for bb in banks])
            for g in range(4):
                S.op("dve", lambda e, g=g, n=n, bb=banks[g]: e.scalar_tensor_tensor(out=st_fo[n][:, g, :], in0=psb[bb][:], scalar=bfc[:, i * 4 + g:i * 4 + g + 1],
                                                                                   in1=sgf[n][:, g, :], op0=ALU.add, op1=ALU.mult),
                     reads=[("ps", banks[g]), ("sgf", n), "bfc"], writes=[("st_fo", n)])
            S.dma("pool", lambda e, t=t, n=n: e.dma_start(out=fo_s[t].rearrange("p (c n) -> p c n", n=TT), in_=st_fo[n][:]),
                  reads=[("st_fo", n)], writes=[("fo_s", t)])
            bg_rest()
        bg_need(("out", l))
        S.barrier()
        if STOP == "e15":
            return

        A.reset()
        qk = [A.alloc([P, 16, TT], BF16) for _ in range(2)]
        kh = [A.alloc([P, 4, 2, P], BF16) for _ in range(2)]
        v6 = [A.alloc([P, 6, 512], BF16) for _ in range(2)]
        sga = [A.alloc([P, 12, TT], BF16) for _ in range(2)]
        aTb = [A.alloc([P, 16, TT], BF16), A.alias([P, 16, TT], BF16, bg_f_off)]
        xrbuf = [A.alloc([P, 4, 512], F32) for _ in range(2)]
        wobuf = [A.alloc([P, KC, 512], BF16) for _ in range(2)]
        esrow = A.alloc([1, 4, 384], BF16)
        PT = [A.alloc([P, 3, 384], BF16) for _ in range(2)]
        t1 = [A.alloc([P, 384], F32) for _ in range(2)]
        t2 = [A.alloc([P, 384], F32) for _ in range(2)]
        biasT = A.alloc([P, 3, 12, P], BF16)
        biasB = A.alloc([P, 2, 12, P], BF16)
        esf = A.alloc([1, 4, 384], F32)
        ld(biasT[:], bias_s.rearrange("k (kc h q) -> k kc h q", kc=3, h=12), "biasT")
        S.op("dve", lambda e: e.tensor_scalar(out=biasB[:, 0, :, :], in0=biasT[:, 0, :, :], scalar1=bmc[:, 0:1], scalar2=None, op0=ALU.add),
             reads=["biasT", "bmc"], writes=["biasB"])
        S.op("dve", lambda e: e.tensor_scalar(out=biasB[:, 1, :, :], in0=biasT[:, 2, :, :], scalar1=bmc[:, 0:1], scalar2=None, op0=ALU.add),
             reads=["biasT", "bmc", "biasB"], writes=["biasB"])
        for h in range(12):
            S.op("dve", lambda e, h=h: e.tensor_scalar(out=esf[0:1, h // 3, (h % 3) * P:(h % 3 + 1) * P], in0=ones_f[0:1, :],
                                                       scalar1=sinkb[0:1, i * 12 + h:i * 12 + h + 1], scalar2=None, op0=ALU.mult),
                 reads=["ones_f", "sinkb", "esf"], writes=["esf"])

        S.op("dve", lambda e: e.tensor_copy(out=esrow[:], in_=esf[0:1, :, :]), reads=["esf"], writes=["esrow"])

        def e2_loads(t):
            n = t % 2
            ld(qk[n][:], qk_s[t].rearrange("p (c n) -> p c n", n=TT), ("qk", n))
            if t > 0:
                ld(kh[n][:, :, 0, :], qk_s[t - 1].rearrange("p (c n) -> p c n", n=TT)[:, 12:16, TT - P:TT], ("kh", n))
            if t + 1 < NT:
                ld(kh[n][:, :, 1, :], qk_s[t + 1].rearrange("p (c n) -> p c n", n=TT)[:, 12:16, 0:P], ("kh", n))
            b0 = max(4 * t - 1, 0)
            b1 = min(4 * t + 5, NB)
            j0 = b0 - (4 * t - 1)
            ld(v6[n][:, j0:j0 + (b1 - b0), :], v_s[b0 * P:b1 * P, :].rearrange("(b p) c -> p b c", p=P), ("v6", n))
            ld(sga[n][:], sg_s[t].rearrange("p (c n) -> p c n", n=TT)[:, 0:12, :], ("sga", n))

        e2_loads(0)
        if NT > 1:
            e2_loads(1)
        octr = [0]
        units = [(blk, h) for blk in range(4) for h in range(4)]
        NU = len(units)
        ustate = {}

        def emit_st(t, u):
            n = t % 2
            blk, h = units[u]
            gb = 4 * t + blk
            m = (t * NU + u) % 2
            kcs = [kc for kc in range(3) if 0 <= gb + kc - 1 < NB]
            for kc in kcs:
                nbk = gb + kc - 1
                lb = nbk - 4 * t
                if 0 <= lb < 4:
                    k_ap = qk[n][:, 12 + h, lb * P:(lb + 1) * P]
                elif lb < 0:
                    k_ap = kh[n][:, h, 0, :]
                else:
                    k_ap = kh[n][:, h, 1, :]
                if kc == 0 and gb == HB:
                    bias_ap = biasB[:, 0, 3 * h:3 * h + 3, :]
                elif kc == 2 and gb == HB - 1:
                    bias_ap = biasB[:, 1, 3 * h:3 * h + 3, :]
                else:
                    bias_ap = biasT[:, kc, 3 * h:3 * h + 3, :]
                b = ps_next()

                def f(e, b=b, k_ap=k_ap, bias_ap=bias_ap, blk=blk, h=h, n=n):
                    o = psb[b][:, 0:384].rearrange("p (g q) -> p g q", q=P)
                    e.matmul(o, lhsT=k_ap, rhs=qk[n][:, 3 * h:3 * h + 3, blk * P:(blk + 1) * P], start=True, stop=False)
                    return e.matmul(o, lhsT=ident[:], rhs=bias_ap, start=False, stop=True)
                S.op("pe", f, reads=[("qk", n), ("kh", n), "biasT", "biasB", "ident"], writes=[("ps", b)])
                S.op("act", lambda e, b=b, kc=kc, m=m: e.activation(out=PT[m][:, kc, :], in_=psb[b][:, 0:384], func=AF.Exp),
                     reads=[("ps", b)], writes=[("PT", m)])
            ustate[(t, u)] = (kcs, m)

        def emit_pv(t, u):
            n = t % 2
            aT = aTb[n]
            ak = ("aT", n)
            blk, h = units[u]
            gb = 4 * t + blk
            kcs, m = ustate.pop((t, u))
            bo = ps_next()
            bd = ps_next()

            def f(e, bo=bo, bd=bd, kcs=kcs, m=m, h=h, gb=gb, t=t, n=n):
                for kc in kcs:
                    j = gb + kc - 1 - (4 * t - 1)
                    e.matmul(psb[bo][:, 0:384], lhsT=v6[n][:, j, h * P:(h + 1) * P], rhs=PT[m][:, kc, :],
                             start=(kc == kcs[0]), stop=(kc == kcs[-1]))
                for kc in kcs:
                    e.matmul(psb[bd][:, 0:384], lhsT=ones1[:], rhs=PT[m][:, kc, :], start=(kc == kcs[0]), stop=False)
                return e.matmul(psb[bd][:, 0:384], lhsT=ones1[0:1, :], rhs=esrow[0:1, h, :], start=False, stop=True)
            S.op("pe", f, reads=[("v6", n), ("PT", m), "ones1", "esrow"], writes=[("ps", bo), ("ps", bd)])
            S.op("dve", lambda e, bd=bd, m=m: e.reciprocal(out=t1[m][:], in_=psb[bd][:, 0:384]), reads=[("ps", bd)], writes=[("t1", m)])
            S.op("dve", lambda e, bo=bo, m=m: e.tensor_tensor(out=t2[m][:], in0=psb[bo][:, 0:384], in1=t1[m][:], op=ALU.mult),
                 reads=[("ps", bo), ("t1", m)], writes=[("t2", m)])
            S.op("pool", lambda e, m=m, h=h, blk=blk, n=n, aT=aT: e.tensor_tensor(out=aT[:, 3 * h:3 * h + 3, blk * P:(blk + 1) * P],
                                                                                 in0=t2[m][:].rearrange("p (g q) -> p g q", q=P),
                                                                                 in1=sga[n][:, 3 * h:3 * h + 3, blk * P:(blk + 1) * P], op=ALU.mult),
                 reads=[("t2", m), ("sga", n), ak], writes=[ak])

        SKEW = 2

        def attn_begin(t):
            n = t % 2
            ld(aTb[n][:, 12:16, :], fo_s[t].rearrange("p (c n) -> p c n", n=TT), ("aT", n))
            emit_st(t, 0)
            for u in range(SKEW):
                attn_step(t, u)

        def attn_step(t, u):
            if u + 1 < NU:
                emit_st(t, u + 1)
            emit_pv(t, u)

        attn_begin(0)
        for u in range(SKEW, NU):
            attn_step(0, u)
        for t in range(NT):
            n = t % 2
            if t + 1 < NT:
                attn_begin(t + 1)

            def hook(u, t=t):
                if t + 1 < NT and u + SKEW < NU:
                    attn_step(t + 1, u + SKEW)
                if u == 5 and t + 2 < NT:
                    e2_loads(t + 2)
            out_proj(t, aTb[n], ("aT", n), woe[i], x_src, x_dst, wobuf, xrbuf, octr, hook=hook)
        S.barrier()
        if l + 1 < NL and tdone[0] < need[("in", l + 1)]:
            bg_need(("in", l + 1))
            S.barrier()

    def odd_layer(l, x_src, x_dst):
        i = l // 2
        A.reset()
        xin = [A.alloc([P, D], F32) for _ in range(2)]
        xn = [A.alloc([P, D], BF16) for _ in range(4)]
        xnT = [A.alloc([P, KC, TT], BF16) for _ in range(2)]
        wbuf = [A.alloc([P, KC, WG], BF16) for _ in range(3)]
        st_h = A.alloc([P, 8, TT], BF16)
        st_bg = A.alloc([P, 8, TT], BF16)
        st_o = A.alloc([P, 24, TT], BF16)
        vn = A.alloc([P, 4, 8, P], BF16)
        vsq = [A.alloc([P, WG], F32) for _ in range(2)]
        vss = [A.alloc([P, 4], F32) for _ in range(2)]
        tmp = [A.alloc([P, P], F32) for _ in range(2)]
        ebuf = [A.alloc([P, TT], F32) for _ in range(2)]
        ws_f = A.alloc([P, 8, P], F32)
        ws_b = A.alloc([P, 8, P], BF16)
        bs_bc = A.alloc([P, 8, P], F32)
        ld(ws_f[:], p_wsT[:, i * 8 * P:(i + 1) * 8 * P].rearrange("p (g q) -> p g q", q=P), "ws_f")
        ld(bs_bc[:], p_bs[:, i * 8 * P:(i + 1) * 8 * P].rearrange("p (g q) -> p g q", q=P), "bs_bc")
        S.op("dve", lambda e: e.tensor_copy(out=ws_b[:], in_=ws_f[:]), reads=["ws_f"], writes=["ws_b"])

        norm, transp = make_norm(l, x_src, xin, xn, xnT)
        norm(0)
        transp(0)
        seg_order = [0, 2, 1, 3, 5, 4, 6]
        groups = [s * 4 + k for s in seg_order for k in range(4)]
        wctr = 0
        cnt = [0]
        for t in range(NT):
            tb = t % 2
            for gi, wg in enumerate(groups):
                if gi == 0:
                    bg_tile()
                if gi >= 6 and gi % 2 == 0 and (gi - 6) // 2 < 4 and t + 1 < NT:
                    norm(t + 1, ((gi - 6) // 2,))
                if gi == 16 and t + 1 < NT:
                    for kc_ in range(KC):
                        trq.append(lambda kc_=kc_, t=t: transp(t + 1, (kc_,)))
                wb = wbuf[wctr % 3]
                wk = ("wbuf", wctr % 3)
                wctr += 1
                ld(wb[:], wio[i][wg].rearrange("p (kc o) -> p kc o", o=WG), wk)
                if gi % 2 == 1:
                    bg1()
                seg = wg // 4
                for j in range(2):
                    c = (wg % 4) * 2 + j
                    if seg == 5:
                        continue
                    b = mm_feat(wb, j, xnT, tb, wk)
                    n = cnt[0] % 2
                    cnt[0] += 1
                    if seg == 0:
                        S.op("act", lambda e, b=b, c=c: e.activation(out=st_h[:, c, :], in_=psb[b][:], func=AF.Copy), reads=[("ps", b)], writes=["st_h"])
                    elif seg == 2:
                        S.op("dve", lambda e, b=b, c=c: e.tensor_tensor(out=st_o[:, c, :], in0=psb[b][:], in1=st_h[:, c, :], op=ALU.mult),
                             reads=[("ps", b), "st_h"], writes=["st_o"])
                    elif seg == 1:
                        S.op("act", lambda e, b=b, c=c: e.activation(out=st_bg[:, c, :], in_=psb[b][:], func=AF.Copy), reads=[("ps", b)], writes=["st_bg"])
                    elif seg == 3:
                        silu_gate(b, ebuf, n, st_o[:, 8 + c, :], "st_o", st_bg[:, c, :], "st_bg")
                    elif seg == 4:
                        S.op("dve", lambda e, b=b, c=c: e.tensor_tensor(out=st_o[:, 16 + c, :], in0=psb[b][:], in1=st_o[:, 16 + c, :], op=ALU.mult),
                             reads=[("ps", b), "st_o"], writes=["st_o"])
                    elif seg == 6:
                        silu_gate(b, ebuf, n, st_o[:, 16 + c, :], "st_o", st_o[:, 16 + c, :], "st_o")
                if seg == 5:
                    g0 = (wg % 4) * 2
                    for blk in range(4):
                        b = mm_tok(wb, blk, xnT, tb, wk, WG)
                        n = cnt[0] % 2
                        cnt[0] += 1
                        S.op("act", lambda e, b=b, n=n: e.activation(out=vsq[n][:], in_=psb[b][:, 0:WG], func=AF.Square), reads=[("ps", b)], writes=[("vsq", n)])
                        S.op("dve", lambda e, n=n: e.tensor_reduce(out=vss[n][:, 0:2], in_=vsq[n][:].rearrange("p (g c) -> p g c", c=P), axis=AX.X, op=ALU.add),
                             reads=[("vsq", n)], writes=[("vss", n)])
                        S.op("act", lambda e, n=n: e.activation(out=vss[n][:, 2:4], in_=vss[n][:, 0:2], func=AF.Ln, scale=1.0 / P, bias=epsc[:]),
                             reads=[("vss", n), "epsc"], writes=[("vss", n)])
                        S.op("act", lambda e, n=n: e.activation(out=vss[n][:, 2:4], in_=vss[n][:, 2:4], func=AF.Exp, scale=-0.5),
                             reads=[("vss", n)], writes=[("vss", n)])
                        for q in range(2):
                            S.op("act", lambda e, b=b, n=n, q=q, blk=blk, g0=g0: e.activation(out=vn[:, blk, g0 + q, :], in_=psb[b][:, q * P:(q + 1) * P], func=AF.Copy,
                                                                                      scale=vss[n][:, 2 + q:3 + q]),
                                 reads=[("ps", b), ("vss", n)], writes=[("vn", blk)])
                        def restv(blk=blk, g0=g0):
                            b2 = ps_next()

                            def f(e, b2=b2, blk=blk, g0=g0):
                                for q in range(2):
                                    ins = e.matmul(psb[b2][:, q * P:(q + 1) * P], lhsT=vn[:, blk, g0 + q, :], rhs=ws_b[:, g0 + q, :], start=True, stop=True)
                                return ins
                            S.op("pe", f, reads=[("vn", blk), "ws_b"], writes=[("ps", b2)])
                            for q in range(2):
                                g = g0 + q
                                S.op("dve", lambda e, b2=b2, q=q, g=g, blk=blk: e.scalar_tensor_tensor(out=st_o[:, 16 + g, blk * P:(blk + 1) * P], in0=psb[b2][:, q * P:(q + 1) * P],
                                                                                            scalar=vgain[:, i * 8 + g:i * 8 + g + 1], in1=bs_bc[:, g, :],
                                                                                            op0=ALU.mult, op1=ALU.add),
                                     reads=[("ps", b2), "vgain", "bs_bc", "st_o"], writes=["st_o"])
                        defer(restv, 4)
            flush()
            S.op("dve", lambda e, t=t: e.tensor_copy(out=mhalo[:, (t * 2) * 8:(t * 2 + 1) * 8], in_=st_o[:, 0:8, 0]), reads=["st_o"], writes=["mhalo"])
            S.op("dve", lambda e, t=t: e.tensor_copy(out=mhalo[:, (t * 2 + 1) * 8:(t * 2 + 2) * 8], in_=st_o[:, 0:8, TT - 1]), reads=["st_o", "mhalo"], writes=["mhalo"])
            S.dma("pool", lambda e, t=t: e.dma_start(out=so_s[t].rearrange("p (c n) -> p c n", n=TT), in_=st_o[:]), reads=["st_o"], writes=[("so_s", t)])
            bg_rest()
            while trq:
                trq.pop(0)()
        bg_need(("out", l))
        S.barrier()

        A.reset()
        mpad = [A.alloc([P, 8, TT + 2], BF16) for _ in range(2)]
        bgs = [A.alloc([P, 8, TT], BF16) for _ in range(2)]
        aT = [A.alloc([P, 16, TT], BF16) for _ in range(2)]
        cacc = [A.alloc([P, TT], F32) for _ in range(4)]
        xrbuf = [A.alloc([P, 4, 512], F32) for _ in range(4)]
        wobuf = None

        def o2_loads(t):
            n = t % 2
            so3 = so_s[t].rearrange("p (c n) -> p c n", n=TT)
            ld(mpad[n][:, :, 1:TT + 1], so3[:, 0:8, :], ("mpad", n))
            ld(bgs[n][:], so3[:, 8:16, :], ("bgs", n))
            ld(aT[n][:, 8:16, :], so3[:, 16:24, :], ("aT", n))

        wres = [A.alloc([P, KC, 512], BF16) for _ in range(4)]
        o2_loads(0)
        for og in range(4):
            ld(wres[og][:], woo[i][og].rearrange("p (kc o) -> p kc o", o=512), ("wres", og))
        octr = [0]
        ccn = [0]

        def halos(t):
            n = t % 2
            mk = ("mpad", n)
            if t == 0:
                S.op("dve", lambda e, n=n: e.memset(mpad[n][:, :, 0], 0.0), reads=[mk], writes=[mk])
            else:
                S.op("dve", lambda e, n=n, t=t: e.tensor_copy(out=mpad[n][:, :, 0], in_=mhalo[:, ((t - 1) * 2 + 1) * 8:((t - 1) * 2 + 2) * 8]),
                     reads=[mk, "mhalo"], writes=[mk])
                if 4 * t == HB:
                    S.op("dve", lambda e, n=n: e.tensor_scalar(out=mpad[n][:, :, 0], in0=mpad[n][:, :, 0], scalar1=bmc[:, 1:2], scalar2=None, op0=ALU.mult),
                         reads=[mk, "bmc"], writes=[mk])
            if t == NT - 1:
                S.op("dve", lambda e, n=n: e.memset(mpad[n][:, :, TT + 1], 0.0), reads=[mk], writes=[mk])
            else:
                S.op("dve", lambda e, n=n, t=t: e.tensor_copy(out=mpad[n][:, :, TT + 1], in_=mhalo[:, ((t + 1) * 2) * 8:((t + 1) * 2 + 1) * 8]),
                     reads=[mk, "mhalo"], writes=[mk])
                if 4 * (t + 1) == HB:
                    S.op("dve", lambda e, n=n: e.tensor_scalar(out=mpad[n][:, :, TT + 1], in0=mpad[n][:, :, TT + 1], scalar1=bmc[:, 1:2], scalar2=None, op0=ALU.mult),
                         reads=[mk, "bmc"], writes=[mk])

        def conv_chunk(t, c):
            n = t % 2
            mk = ("mpad", n)
            ca = cacc[ccn[0] % 4]
            ck = ("cacc", ccn[0] % 4)
            ccn[0] += 1
            w0 = convw[:, i * 24 + 0 * 8 + c:i * 24 + 0 * 8 + c + 1]
            w1 = convw[:, i * 24 + 1 * 8 + c:i * 24 + 1 * 8 + c + 1]
            w2 = convw[:, i * 24 + 2 * 8 + c:i * 24 + 2 * 8 + c + 1]
            S.op("act", lambda e, ca=ca, n=n, c=c, w0=w0: e.activation(out=ca[:], in_=mpad[n][:, c, 0:TT], func=AF.Copy, scale=w0),
                 reads=[mk, "convw"], writes=[ck])
            S.op("dve", lambda e, ca=ca, n=n, c=c, w1=w1: e.scalar_tensor_tensor(out=ca[:], in0=mpad[n][:, c, 1:TT + 1], scalar=w1, in1=ca[:], op0=ALU.mult, op1=ALU.add),
                 reads=[mk, "convw", ck], writes=[ck])
            S.op("dve", lambda e, ca=ca, n=n, c=c, w2=w2: e.scalar_tensor_tensor(out=ca[:], in0=mpad[n][:, c, 2:TT + 2], scalar=w2, in1=ca[:], op0=ALU.mult, op1=ALU.add),
                 reads=[mk, "convw", ck], writes=[ck])
            S.op("pool", lambda e, ca=ca, n=n, c=c: e.tensor_tensor(out=aT[n][:, c, :], in0=ca[:], in1=bgs[n][:, c, :], op=ALU.mult),
                 reads=[ck, ("bgs", n), ("aT", n)], writes=[("aT", n)])

        halos(0)
        for c in range(8):
            conv_chunk(0, c)
        for t in range(NT):
            n = t % 2
            bg_tile()
            bg1()
            if t + 1 < NT:
                o2_loads(t + 1)
                halos(t + 1)

            def hook(u, t=t):
                if t + 1 < NT and u % 2 == 0:
                    conv_chunk(t + 1, u // 2)
                if u == 8:
                    bg1()
            out_proj(t, aT[n], ("aT", n), woo[i], x_src, x_dst, wobuf, xrbuf, octr, resident=wres, hook=hook)
            bg_rest()
        if l + 1 < NL:
            bg_need(("in", l + 1))
        S.barrier()

    chain = [x_in, xa, xb, xa, xb]
    for l in range(NL):
        src = chain[l]
        dst = y_out if l == NL - 1 else chain[l + 1]
        if l % 2 == 0:
            even_layer(l, src, dst)
        else:
            odd_layer(l, src, dst)
    S.barrier()

    with nc.Block() as block:
        @block.tensor
        def _(e):
            S.replay("pe", e)

        @block.scalar
        def _(e):
            S.replay("act", e)

        @block.vector
        def _(e):
            S.replay("dve", e)

        @block.gpsimd
        def _(e):
            S.replay("pool", e)

        @block.sync
        def _(e):
            S.replay("sp", e)
    stack.close()
    return nc


def _t5_bucket(rel):
    nb = 16
    ret = (rel > 0).astype(np.int32) * nb
    n = np.abs(rel)
    max_exact = nb // 2
    large = max_exact + (np.log(np.maximum(n, 1) / max_exact) / np.log(128 / max_exact) * (nb - max_exact)).astype(np.int32)
    large = np.minimum(large, nb - 1)
    return (ret + np.where(n < max_exact, n, large)).astype(np.int32)


def make_consts(T):
    bf = ml_dtypes.bfloat16
    c = {}
    c["c_ident"] = np.eye(P, dtype=np.float32).astype(bf)
    a = np.arange(P)
    ang = 2 * np.pi * ((a[:, None] * a[None, :]) % P) / P
    c["c_cs"] = np.concatenate([np.cos(ang), np.sin(ang)], axis=1).astype(np.float64) / np.sqrt(P)
    c["c_cs"] = c["c_cs"].astype(np.float32).astype(bf)
    kc = np.arange(3)[:, None, None]
    k = np.arange(P)[None, :, None]
    q = np.arange(P)[None, None, :]
    rel = (kc - 1) * P + k - q
    bucket = _t5_bucket(rel)
    oh = np.zeros((34, 3, P, P), np.float32)
    for b in range(32):
        oh[b] = (bucket == b)
    oh[32] = (np.abs(rel) > 128)
    oh[33] = 1.0
    c["c_oh"] = oh.reshape(34, 3 * P * P).astype(bf)
    return c


def make_mdft(T, two_seq):
    bf = ml_dtypes.bfloat16
    NT, NG4 = T // TT, T // 512
    S_ = T // 2 if two_seq else T
    s = np.arange(T, dtype=np.int64)
    M = np.zeros((T, 2, T), np.float32)
    for seq in range(T // S_):
        idx = s[seq * S_:(seq + 1) * S_] - seq * S_
        ang = 2 * np.pi * ((idx[:, None] * idx[None, :]) % S_).astype(np.float64) / S_
        M[seq * S_:(seq + 1) * S_, 0, seq * S_:(seq + 1) * S_] = np.cos(ang) / np.sqrt(S_)
        M[seq * S_:(seq + 1) * S_, 1, seq * S_:(seq + 1) * S_] = -np.sin(ang) / np.sqrt(S_)
    M = M.reshape(NG4, 4, P, 2, NT, TT).transpose(4, 0, 2, 1, 3, 5)
    return np.ascontiguousarray(M).reshape(NT, NG4, P, 4 * 2 * TT).astype(bf)


def layout_params(norm_gain, rel_bias, q_gain, k_gain, sink, w_f, b_f, conv_w, v_gain, w_s, b_s):
    f = np.float32
    p = {}
    p["p_gcol"] = np.ascontiguousarray(np.asarray(norm_gain, f).reshape(4, KC, P).transpose(2, 0, 1).reshape(P, 4 * KC))
    p["p_qkg"] = np.ascontiguousarray(np.concatenate([np.asarray(q_gain, f).T, np.asarray(k_gain, f).T], axis=1))
    p["p_sink"] = np.ascontiguousarray(np.broadcast_to(np.asarray(sink, f).reshape(1, 24), (P, 24)))
    p["p_bf"] = np.ascontiguousarray(np.asarray(b_f, f).reshape(8, P).T)
    p["p_wf"] = np.ascontiguousarray(np.asarray(w_f, f).reshape(8, P, P).transpose(1, 0, 2).reshape(P, 8 * P))
    p["p_convw"] = np.ascontiguousarray(np.asarray(conv_w, f).reshape(2, 3, 8, P).transpose(3, 0, 1, 2).reshape(P, 48))
    p["p_vgain"] = np.ascontiguousarray(np.asarray(v_gain, f).reshape(2, 8, P).transpose(2, 0, 1).reshape(P, 16))
    p["p_wsT"] = np.ascontiguousarray(np.asarray(w_s, f).reshape(16, P, P).transpose(2, 0, 1).reshape(P, 16 * P))
    p["p_bs"] = np.ascontiguousarray(np.broadcast_to(np.asarray(b_s, f).reshape(1, 16 * P), (P, 16 * P)))
    rbx = np.zeros((34, 12), f)
    rbx[:32] = np.asarray(rel_bias, f)
    rbx[32] = NEG
    rbx[33] = -SHIFT
    p["c_rbx"] = rbx
    return p


_NC_CACHE = {}


def run_cores(xs, two_seq_flags, weights, params, T, NL=4):
    key = (T, NL)
    if key not in _NC_CACHE:
        _NC_CACHE[key] = build(T, NL)
    nc = _NC_CACHE[key]
    consts = make_consts(T)
    md = {True: make_mdft(T, True), False: make_mdft(T, False)}
    in_maps = []
    for x, ts in zip(xs, two_seq_flags):
        m = {"x": np.ascontiguousarray(x, dtype=np.float32)}
        m.update(weights)
        m.update(params)
        m.update(consts)
        m["c_mdft"] = md[bool(ts)]
        bm = np.zeros((P, 2), np.float32)
        bm[:, 0] = NEG if ts else 0.0
        bm[:, 1] = 0.0 if ts else 1.0
        m["c_bm"] = bm
        in_maps.append(m)
    res = run_bass_kernel_spmd(nc, in_maps, core_ids=list(range(len(xs))))
    return [r["y"] for r in res.results]


def kernel(x_prompt, x_sample, norm_gain, rel_bias, w_in_e, w_out_e, q_gain, k_gain, sink, w_f, b_f,
           w_in_o, conv_w, v_gain, w_s, b_s, w_out_o):
    T = 4096
    f = np.float32
    xp = np.asarray(x_prompt, f)
    xs_ = np.asarray(x_sample, f)
    xs = [xp[2 * c:2 * c + 2].reshape(T, D) for c in range(4)] + [xs_[c].reshape(T, D) for c in range(4)]
    flags = [True] * 4 + [False] * 4
    weights = {"w_in_e": np.ascontiguousarray(w_in_e, dtype=f), "w_out_e": np.ascontiguousarray(w_out_e, dtype=f),
               "w_in_o": np.ascontiguousarray(w_in_o, dtype=f), "w_out_o": np.ascontiguousarray(w_out_o, dtype=f)}
    params = layout_params(norm_gain, rel_bias, q_gain, k_gain, sink, w_f, b_f, conv_w, v_gain, w_s, b_s)
    ys = run_cores(xs, flags, weights, params, T, 4)
    y_prompt = np.stack([ys[c].reshape(2, 2048, D) for c in range(4)]).reshape(8, 2048, D)
    y_sample = np.stack([ys[4 + c].reshape(4096, D) for c in range(4)])
    return (y_prompt.astype(f), y_sample.astype(f))
```
